# Optimizing a Trainium2 kernel written in Bass

```python
import math
import jax
import jax.numpy as jnp
from jax import lax
import numpy as np

D_MODEL = 1024
BATCH = 2
SEQ = 8192
DEPTH = 2
DEC_BATCH = 4
DEC_SEQ = 4096
PAST_LEN = 128

GRID_W = 64
N_MEM = 256
D_FF = 2816
HEAD_DIM = 64
BRANCH_W = 256
N_BRANCH = 5
SSM_P = 16
SSM_G = BRANCH_W // SSM_P
SSM_N = 64
SWA_HQ = 4
SWA_HKV = 2
SWA_WIN = 128
SWA_BLK = 128
T5_BUCKETS = 32
T5_MAX_DIST = 128
NA_H = 4
NA_KH = 8
NA_KW = 16
MLA_H = 4
MLA_Q_RANK = 192
MLA_KV_RANK = 128
MLA_NOPE = 64
MLA_ROPE = 32
MLA_V = 64
MLA_BLK = 128
ROPE_THETA = 10000.0
MEM_H = 4
EPS = 1e-6
NEG = -1e30
IN_WIDTHS = (BRANCH_W,
             SWA_HQ * HEAD_DIM, SWA_HKV * HEAD_DIM, SWA_HKV * HEAD_DIM,
             NA_H * HEAD_DIM, NA_H * HEAD_DIM, NA_H * HEAD_DIM,
             MLA_Q_RANK, MLA_KV_RANK, MLA_ROPE,
             MEM_H * HEAD_DIM,
             N_BRANCH * D_MODEL)
D_IN = sum(IN_WIDTHS)

kernel_name = 'hybrid_bidir_encoder_two_groups'

F32 = jnp.float32


def _rmsnorm(x, g):
    xf = x.astype(F32)
    y = xf * lax.rsqrt(jnp.mean(xf * xf, axis=-1, keepdims=True) + EPS)
    return (y * g.astype(F32)).astype(x.dtype)


def _swiglu(x, w_gate, w_up, w_down):
    return (jax.nn.silu(x @ w_gate) * (x @ w_up)) @ w_down


def _cplx_combine(e1, e2):
    a1r, a1i, b1r, b1i = e1
    a2r, a2i, b2r, b2i = e2
    return (a2r * a1r - a2i * a1i,
            a2r * a1i + a2i * a1r,
            a2r * b1r - a2i * b1i + b2r,
            a2r * b1i + a2i * b1r + b2i)


def _s5(u, lam_re, lam_im, log_step, b_re, b_im, c_re, c_im, d_skip, w_glu):
    Bsz, S, _ = u.shape
    uf = u.astype(F32).reshape(Bsz, S, SSM_G, SSM_P)
    lr = lam_re.astype(F32)
    li = lam_im.astype(F32)
    dt = jnp.exp(log_step.astype(F32))[..., None]
    mag = jnp.exp(lr * dt)
    a_re = mag * jnp.cos(li * dt)
    a_im = mag * jnp.sin(li * dt)
    den = lr * lr + li * li
    xr = a_re - 1.0
    k_re = (xr * lr + a_im * li) / den
    k_im = (a_im * lr - xr * li) / den
    br = b_re.astype(F32)
    bi = b_im.astype(F32)
    bb_re = k_re[..., None] * br - k_im[..., None] * bi
    bb_im = k_re[..., None] * bi + k_im[..., None] * br
    cr = c_re.astype(F32)
    ci = c_im.astype(F32)
    y = d_skip.astype(F32).reshape(SSM_G, SSM_P) * uf
    for dirn, rev in ((0, False), (1, True)):
        bu_re = jnp.einsum('bsgp,gnp->bsgn', uf, bb_re[dirn])
        bu_im = jnp.einsum('bsgp,gnp->bsgn', uf, bb_im[dirn])
        ar = jnp.broadcast_to(a_re[dirn], bu_re.shape)
        ai = jnp.broadcast_to(a_im[dirn], bu_re.shape)
        _, _, st_re, st_im = lax.associative_scan(_cplx_combine, (ar, ai, bu_re, bu_im),
                                                  reverse=rev, axis=1)
        y = y + jnp.einsum('bsgn,gpn->bsgp', st_re, cr[dirn]) \
              - jnp.einsum('bsgn,gpn->bsgp', st_im, ci[dirn])
    g = jax.nn.gelu(y.reshape(Bsz, S, BRANCH_W)).astype(u.dtype)
    return g * jax.nn.sigmoid(g @ w_glu)


def _t5_bucket(rel):
    nb = T5_BUCKETS // 2
    max_exact = nb // 2
    ret = (rel > 0).astype(jnp.int32) * nb
    n = jnp.abs(rel)
    nf = jnp.maximum(n, 1).astype(F32)
    large = max_exact + (jnp.log(nf / max_exact) / math.log(T5_MAX_DIST / max_exact)
                         * (nb - max_exact)).astype(jnp.int32)
    large = jnp.minimum(large, nb - 1)
    return ret + jnp.where(n < max_exact, n, large)


def _swa(q, k, v, sink, t5_bias):
    Bsz, S, _ = q.shape
    nb = S // SWA_BLK
    rep = SWA_HQ // SWA_HKV
    qb = q.reshape(Bsz, nb, SWA_BLK, SWA_HKV, rep, HEAD_DIM)
    pad = ((0, 0), (SWA_BLK, SWA_BLK), (0, 0), (0, 0))
    kp = jnp.pad(k.reshape(Bsz, S, SWA_HKV, HEAD_DIM), pad).reshape(Bsz, nb + 2, SWA_BLK, SWA_HKV, HEAD_DIM)
    vp = jnp.pad(v.reshape(Bsz, S, SWA_HKV, HEAD_DIM), pad).reshape(Bsz, nb + 2, SWA_BLK, SWA_HKV, HEAD_DIM)
    kb = jnp.concatenate([kp[:, :-2], kp[:, 1:-1], kp[:, 2:]], axis=2)
    vb = jnp.concatenate([vp[:, :-2], vp[:, 1:-1], vp[:, 2:]], axis=2)
    logits = jnp.einsum('bnqgrd,bnkgd->bngrqk', qb, kb,
                        preferred_element_type=F32) * (HEAD_DIM ** -0.5)
    qi = jnp.arange(SWA_BLK)[:, None]
    kj = jnp.arange(3 * SWA_BLK)[None, :]
    rel = kj - SWA_BLK - qi
    bias = t5_bias.astype(F32)[_t5_bucket(rel)]
    bias = bias.transpose(2, 0, 1).reshape(SWA_HKV, rep, SWA_BLK, 3 * SWA_BLK)
    kpos = jnp.arange(nb)[:, None] * SWA_BLK + kj - SWA_BLK
    valid = (jnp.abs(rel) <= SWA_WIN)[None] & ((kpos >= 0) & (kpos < S))[:, None, :]
    logits = jnp.where(valid[None, :, None, None], logits + bias, NEG)
    sink_col = jnp.broadcast_to(sink.astype(F32).reshape(SWA_HKV, rep)[None, None, :, :, None, None],
                                logits.shape[:-1] + (1,))
    p = jax.nn.softmax(jnp.concatenate([logits, sink_col], axis=-1), axis=-1)[..., :-1]
    out = jnp.einsum('bngrqk,bnkgd->bnqgrd', p.astype(v.dtype), vb)
    return out.reshape(Bsz, S, SWA_HQ * HEAD_DIM)


def _na(q, k, v, rpb):
    Bsz, S, _ = q.shape
    rows = S // GRID_W
    kh = min(NA_KH, rows)
    kw = NA_KW
    qg = q.reshape(Bsz, rows, GRID_W, NA_H, HEAD_DIM)
    kg = k.reshape(Bsz, rows, GRID_W, NA_H, HEAD_DIM)
    vg = v.reshape(Bsz, rows, GRID_W, NA_H, HEAD_DIM)
    cols = jnp.arange(GRID_W)
    cs = jnp.clip(cols - kw // 2, 0, GRID_W - kw)
    col_idx = cs[:, None] + jnp.arange(kw)[None, :]
    dc_idx = col_idx - cols[:, None] + (NA_KW - 1)
    rpb_f = rpb.astype(F32)

    def row_fn(args):
        r, q_row = args
        rs = jnp.clip(r - kh // 2, 0, rows - kh)
        k_rows = lax.dynamic_slice_in_dim(kg, rs, kh, axis=1)
        v_rows = lax.dynamic_slice_in_dim(vg, rs, kh, axis=1)
        k_win = k_rows[:, :, col_idx]
        v_win = v_rows[:, :, col_idx]
        logits = jnp.einsum('bchd,brckhd->bhcrk', q_row, k_win,
                            preferred_element_type=F32) * (HEAD_DIM ** -0.5)
        dr_idx = rs + jnp.arange(kh) - r + (NA_KH - 1)
        bias = rpb_f[:, dr_idx[None, :, None], dc_idx[:, None, :]]
        logits = logits + bias[None]
        p = jax.nn.softmax(logits.reshape(Bsz, NA_H, GRID_W, kh * kw), axis=-1)
        p = p.reshape(Bsz, NA_H, GRID_W, kh, kw).astype(v.dtype)
        return jnp.einsum('bhcrk,brckhd->bchd', p, v_win)

    out = lax.map(row_fn, (jnp.arange(rows), qg.transpose(1, 0, 2, 3, 4)))
    return out.transpose(1, 0, 2, 3, 4).reshape(Bsz, S, NA_H * HEAD_DIM)


def _rope(x, pos):
    half = x.shape[-1] // 2
    inv = ROPE_THETA ** (-jnp.arange(half, dtype=F32) / half)
    ang = pos.astype(F32)[:, None] * inv[None, :]
    cos = jnp.cos(ang)[:, None, :]
    sin = jnp.sin(ang)[:, None, :]
    x1 = x[..., :half].astype(F32)
    x2 = x[..., half:].astype(F32)
    return jnp.concatenate([x1 * cos - x2 * sin, x1 * sin + x2 * cos], axis=-1).astype(x.dtype)


def _mla(c_q, c_kv, k_rope, q_norm, w_q_up, kv_norm, w_kv_up):
    Bsz, S, _ = c_q.shape
    dq = MLA_NOPE + MLA_ROPE
    pos = jnp.arange(S)
    q = (_rmsnorm(c_q, q_norm) @ w_q_up).reshape(Bsz, S, MLA_H, dq)
    kv = (_rmsnorm(c_kv, kv_norm) @ w_kv_up).reshape(Bsz, S, MLA_H, MLA_NOPE + MLA_V)
    q = jnp.concatenate([q[..., :MLA_NOPE], _rope(q[..., MLA_NOPE:], pos)], axis=-1)
    kr = jnp.broadcast_to(_rope(k_rope[:, :, None, :], pos), (Bsz, S, MLA_H, MLA_ROPE))
    k = jnp.concatenate([kv[..., :MLA_NOPE], kr], axis=-1)
    v = kv[..., MLA_NOPE:]
    nb = S // MLA_BLK
    qb = q.reshape(Bsz, nb, MLA_BLK, MLA_H, dq).transpose(1, 0, 2, 3, 4)

    def blk(q_blk):
        logits = jnp.einsum('bqhd,bkhd->bhqk', q_blk, k, preferred_element_type=F32) * (dq ** -0.5)
        p = jax.nn.softmax(logits, axis=-1).astype(v.dtype)
        return jnp.einsum('bhqk,bkhd->bqhd', p, v)

    out = lax.map(blk, qb)
    return out.transpose(1, 0, 2, 3, 4).reshape(Bsz, S, MLA_H * MLA_V)


def _mem_attn(q, mem, mem_norm, w_mem_kv):
    Bsz, S, _ = q.shape
    kv = _rmsnorm(mem, mem_norm) @ w_mem_kv
    k, v = jnp.split(kv, 2, axis=-1)
    k = k.reshape(Bsz, -1, MEM_H, HEAD_DIM)
    v = v.reshape(Bsz, -1, MEM_H, HEAD_DIM)
    qh = q.reshape(Bsz, S, MEM_H, HEAD_DIM)
    logits = jnp.einsum('bshd,bmhd->bhsm', qh, k, preferred_element_type=F32) * (HEAD_DIM ** -0.5)
    p = jax.nn.softmax(logits, axis=-1).astype(v.dtype)
    return jnp.einsum('bhsm,bmhd->bshd', p, v).reshape(Bsz, S, MEM_H * HEAD_DIM)


def _trunk(x, mem, w):
    Bsz, S, _ = x.shape
    offs = np.cumsum(IN_WIDTHS)[:-1].tolist()
    h = x
    for l in range(DEPTH):
        h = h + 0.5 * _swiglu(_rmsnorm(h, w['ffn1_norm'][l]), w['ffn1_w_gate'][l],
                              w['ffn1_w_up'][l], w['ffn1_w_down'][l])
        u = _rmsnorm(h, w['mix_norm'][l])
        (a_in, swa_q, swa_k, swa_v, na_q, na_k, na_v, c_q, c_kv, k_rope, mem_q,
         gate_logits) = jnp.split(u @ w['w_in'][l], offs, axis=-1)
        branches = (
            _s5(a_in, w['ssm_lam_re'][l], w['ssm_lam_im'][l], w['ssm_log_step'][l],
                w['ssm_b_re'][l], w['ssm_b_im'][l], w['ssm_c_re'][l], w['ssm_c_im'][l],
                w['ssm_d'][l], w['ssm_w_glu'][l]),
            _swa(swa_q, swa_k, swa_v, w['swa_sink'][l], w['t5_bias']),
            _na(na_q, na_k, na_v, w['na_rpb'][l]),
            _mla(c_q, c_kv, k_rope, w['mla_q_norm'][l], w['mla_w_q_up'][l],
                 w['mla_kv_norm'][l], w['mla_w_kv_up'][l]),
            _mem_attn(mem_q, mem, w['mem_norm'][l], w['mem_w_kv'][l]),
        )
        gates = jax.nn.sigmoid(gate_logits.astype(F32)).astype(h.dtype).reshape(Bsz, S, N_BRANCH, D_MODEL)
        merged = gates[:, :, 0] * (branches[0] @ w['w_branch'][l, 0])
        for n in range(1, N_BRANCH):
            merged = merged + gates[:, :, n] * (branches[n] @ w['w_branch'][l, n])
        h = h + merged @ w['w_out'][l]
        h = h + 0.5 * _swiglu(_rmsnorm(h, w['ffn2_norm'][l]), w['ffn2_w_gate'][l],
                              w['ffn2_w_up'][l], w['ffn2_w_down'][l])
    return _rmsnorm(h, w['final_norm'])


def setup_inputs(seed: int = 0) -> dict:
    key = jax.random.key(seed)
    ks = iter(jax.random.split(key, 48))

    def nrm(shape, scale):
        return jax.random.normal(next(ks), shape, F32) * scale

    def gain(shape):
        return 1.0 + 0.01 * jax.random.normal(next(ks), shape, F32)

    L, G, N, P = DEPTH, SSM_G, SSM_N, SSM_P
    return {
        'x_prompt': nrm((BATCH, SEQ, D_MODEL), 1.0),
        'x_sample': nrm((DEC_BATCH, DEC_SEQ, D_MODEL), 1.0),
        'mem_prompt': nrm((BATCH, N_MEM, D_MODEL), 1.0),
        'mem_sample': nrm((DEC_BATCH, N_MEM, D_MODEL), 1.0),
        'ffn1_norm': gain((L, D_MODEL)),
        'ffn1_w_gate': nrm((L, D_MODEL, D_FF), D_MODEL ** -0.5),
        'ffn1_w_up': nrm((L, D_MODEL, D_FF), D_MODEL ** -0.5),
        'ffn1_w_down': nrm((L, D_FF, D_MODEL), D_FF ** -0.5),
        'mix_norm': gain((L, D_MODEL)),
        'w_in': nrm((L, D_MODEL, D_IN), D_MODEL ** -0.5),
        'ssm_lam_re': -0.5 + nrm((L, 2, G, N), 0.01),
        'ssm_lam_im': jnp.pi * jnp.arange(N, dtype=F32) + nrm((L, 2, G, N), 0.01),
        'ssm_log_step': jax.random.uniform(next(ks), (L, 2, G), F32, math.log(1e-3), math.log(1e-1)),
        'ssm_b_re': nrm((L, 2, G, N, P), (2 * P) ** -0.5),
        'ssm_b_im': nrm((L, 2, G, N, P), (2 * P) ** -0.5),
        'ssm_c_re': nrm((L, 2, G, P, N), (2 * N) ** -0.5),
        'ssm_c_im': nrm((L, 2, G, P, N), (2 * N) ** -0.5),
        'ssm_d': nrm((L, BRANCH_W), 1.0),
        'ssm_w_glu': nrm((L, BRANCH_W, BRANCH_W), BRANCH_W ** -0.5),
        'swa_sink': nrm((L, SWA_HQ), 0.5),
        't5_bias': nrm((T5_BUCKETS, SWA_HQ), 0.1),
        'na_rpb': nrm((L, NA_H, 2 * NA_KH - 1, 2 * NA_KW - 1), 0.02),
        'mla_q_norm': gain((L, MLA_Q_RANK)),
        'mla_w_q_up': nrm((L, MLA_Q_RANK, MLA_H * (MLA_NOPE + MLA_ROPE)), MLA_Q_RANK ** -0.5),
        'mla_kv_norm': gain((L, MLA_KV_RANK)),
        'mla_w_kv_up': nrm((L, MLA_KV_RANK, MLA_H * (MLA_NOPE + MLA_V)), MLA_KV_RANK ** -0.5),
        'mem_norm': gain((L, D_MODEL)),
        'mem_w_kv': nrm((L, D_MODEL, 2 * MEM_H * HEAD_DIM), D_MODEL ** -0.5),
        'w_branch': nrm((L, N_BRANCH, BRANCH_W, D_MODEL), BRANCH_W ** -0.5),
        'w_out': nrm((L, D_MODEL, D_MODEL), D_MODEL ** -0.5),
        'ffn2_norm': gain((L, D_MODEL)),
        'ffn2_w_gate': nrm((L, D_MODEL, D_FF), D_MODEL ** -0.5),
        'ffn2_w_up': nrm((L, D_MODEL, D_FF), D_MODEL ** -0.5),
        'ffn2_w_down': nrm((L, D_FF, D_MODEL), D_FF ** -0.5),
        'final_norm': gain((D_MODEL,)),
    }


def reference(x_prompt, x_sample, mem_prompt, mem_sample,
              ffn1_norm, ffn1_w_gate, ffn1_w_up, ffn1_w_down,
              mix_norm, w_in,
              ssm_lam_re, ssm_lam_im, ssm_log_step, ssm_b_re, ssm_b_im, ssm_c_re, ssm_c_im,
              ssm_d, ssm_w_glu,
              swa_sink, t5_bias, na_rpb,
              mla_q_norm, mla_w_q_up, mla_kv_norm, mla_w_kv_up,
              mem_norm, mem_w_kv, w_branch, w_out,
              ffn2_norm, ffn2_w_gate, ffn2_w_up, ffn2_w_down,
              final_norm):
    w = dict(ffn1_norm=ffn1_norm, ffn1_w_gate=ffn1_w_gate, ffn1_w_up=ffn1_w_up, ffn1_w_down=ffn1_w_down,
             mix_norm=mix_norm, w_in=w_in,
             ssm_lam_re=ssm_lam_re, ssm_lam_im=ssm_lam_im, ssm_log_step=ssm_log_step,
             ssm_b_re=ssm_b_re, ssm_b_im=ssm_b_im, ssm_c_re=ssm_c_re, ssm_c_im=ssm_c_im,
             ssm_d=ssm_d, ssm_w_glu=ssm_w_glu,
             swa_sink=swa_sink, t5_bias=t5_bias, na_rpb=na_rpb,
             mla_q_norm=mla_q_norm, mla_w_q_up=mla_w_q_up, mla_kv_norm=mla_kv_norm, mla_w_kv_up=mla_w_kv_up,
             mem_norm=mem_norm, mem_w_kv=mem_w_kv, w_branch=w_branch, w_out=w_out,
             ffn2_norm=ffn2_norm, ffn2_w_gate=ffn2_w_gate, ffn2_w_up=ffn2_w_up, ffn2_w_down=ffn2_w_down,
             final_norm=final_norm)
    y_prompt = _trunk(x_prompt, mem_prompt, w)
    y_sample = _trunk(x_sample, mem_sample, w)
    return (y_prompt, y_sample)
```

```python
import numpy as np
from contextlib import ExitStack
import concourse.bass as bass
import concourse.mybir as mybir
from concourse.bass_utils import run_bass_kernel_spmd

F32 = mybir.dt.float32
BF16 = mybir.dt.bfloat16
AF = mybir.ActivationFunctionType
ALU = mybir.AluOpType
AX = mybir.AxisListType

D = 1024
DFF = 2816
NFC = DFF // 128
DEPTH = 2
EPS = 1e-6
NCORES = 8


class Res:
    __slots__ = ("name", "w", "r")

    def __init__(self, name=""):
        self.name = name
        self.w = None
        self.r = {}


class Sched:
    ENG = ("pe", "act", "dve", "pool", "sp")

    def __init__(self, nc, es, ndma=12):
        self.nc = nc
        self.es = es
        self.epoch = 0
        self.q = {e: [] for e in self.ENG}
        self.semh = {}
        self.cnt = {}
        self.known = {e: {} for e in self.ENG}
        for e in self.ENG:
            self.semh[e] = es.enter_context(nc.semaphore("s_" + e))
            self.cnt[e] = 0
        self.pending = {e: False for e in self.ENG}
        self.dslots = {}
        self.duse = {}
        self.dnext = {}
        for qn in ("sp", "pool"):
            ks = []
            for i in range(ndma if qn == "sp" else 8):
                k = "d_%s_%d" % (qn, i)
                self.semh[k] = es.enter_context(nc.semaphore(k))
                self.duse[k] = 0
                ks.append(k)
            self.dslots[qn] = ks
            self.dnext[qn] = 0

    def _deps(self, eng, reads, writes):
        deps = {}

        def add(k, v):
            if deps.get(k, 0) < v:
                deps[k] = v
        for r in reads:
            if r.w is not None:
                add(*r.w)
        for w in writes:
            if w.w is not None:
                add(*w.w)
            for k, v in w.r.items():
                add(k, v)
        out = []
        for k, v in deps.items():
            if eng == "pe" and k.split("#")[0] == "pe":
                continue
            if self.known[eng].get(k, 0) >= v:
                continue
            self.known[eng][k] = v
            out.append((k, v))
        return out

    def _reg(self, ev, reads, writes):
        k, v = ev
        for r in reads:
            if r.r.get(k, 0) < v:
                r.r[k] = v
        for w in writes:
            w.w = ev
            w.r = {}

    def op(self, eng, fn, reads=(), writes=(), sig=True):
        waits = self._deps(eng, reads, writes)
        ev = (self.ek(eng), self.cnt[eng] + 1)
        if sig:
            self.cnt[eng] += 1
            self.pending[eng] = False
            inc = (self.ek(eng), 1)
        else:
            self.pending[eng] = True
            inc = None
        self._reg(ev, reads, writes)
        self.q[eng].append((waits, fn, inc))

    def dma(self, out, in_, reads=(), writes=(), q="sp", **kw):
        ks = self.dslots[q]
        k = ks[self.dnext[q]]
        self.dnext[q] = (self.dnext[q] + 1) % len(ks)
        prev = 16 * self.duse[k]
        self.duse[k] += 1
        val = 16 * self.duse[k]
        waits = self._deps(q, reads, writes)
        if prev > 0 and self.known[q].get(k, 0) < prev:
            waits.append((k, prev))
            self.known[q][k] = prev
        self._reg((k, val), reads, writes)
        self.q[q].append((waits, (lambda e: e.dma_start(out=out, in_=in_, **kw)), (k, 16)))

    def coll(self, kind, op, groups, ins, outs, reads=(), writes=()):
        q = "pool"
        ks = self.dslots[q]
        k = ks[self.dnext[q]]
        self.dnext[q] = (self.dnext[q] + 1) % len(ks)
        prev = 16 * self.duse[k]
        self.duse[k] += 1
        val = 16 * self.duse[k]
        waits = self._deps(q, reads, writes)
        if prev > 0 and self.known[q].get(k, 0) < prev:
            waits.append((k, prev))
            self.known[q][k] = prev
        self._reg((k, val), reads, writes)
        self.q[q].append((waits, (lambda e: e.collective_compute(
            kind, op, replica_groups=groups, ins=ins, outs=outs)), (k, 16)))

    def barrier(self):
        for e in self.ENG:
            if self.pending[e]:
                self.op(e, lambda g: g.nop(), sig=True)
        evs = [(self.ek(e), self.cnt[e]) for e in self.ENG if self.cnt[e] > 0]
        evs += [(k, 16 * u) for k, u in self.duse.items() if u > 0]
        for e in self.ENG:
            waits = []
            for k, v in evs:
                if k == self.ek(e):
                    continue
                if self.known[e].get(k, 0) >= v:
                    continue
                self.known[e][k] = v
                waits.append((k, v))
            if waits:
                self.q[e].append((waits, None, None))
        if max(self.cnt.values()) > 12000:
            self.epoch += 1
            for e in self.ENG:
                self.semh[e + "#%d" % self.epoch] = self.es.enter_context(
                    self.nc.semaphore("s_%s_%d" % (e, self.epoch)))
                self.cnt[e] = 0

    def ek(self, e):
        return e if self.epoch == 0 else e + "#%d" % self.epoch

    def emit(self):
        self.barrier()
        nc = self.nc
        me = self

        def replay(name, eng):
            for waits, fn, inc in me.q[name]:
                for k, v in waits:
                    eng.wait_ge(me.semh[k], v)
                if fn is None:
                    continue
                ins = fn(eng)
                if inc is not None:
                    ins.then_inc(me.semh[inc[0]], inc[1])

        with nc.Block() as block:
            @block.tensor
            def _(e):
                replay("pe", e)

            @block.scalar
            def _(e):
                replay("act", e)

            @block.vector
            def _(e):
                replay("dve", e)

            @block.gpsimd
            def _(e):
                replay("pool", e)

            @block.sync
            def _(e):
                replay("sp", e)


class Ring:
    def __init__(self, tiles):
        self.t = tiles
        self.r = [Res() for _ in tiles]
        self.i = 0

    def next(self):
        i = self.i
        self.i = (i + 1) % len(self.t)
        return self.t[i], self.r[i]


N_MEM = 256
HD = 64
GRID_W = 64
SSM_G, SSM_N, SSM_P = 16, 64, 16
NA_KH, NA_KW = 8, 16
SWA_WIN = 128
NEGV = -30000.0
MLA_DQ = 96
MLA_SCALE = MLA_DQ ** -0.5
ATT_SCALE = HD ** -0.5
O_AIN, O_SWQ, O_SWK, O_SWV, O_NAQ, O_NAK, O_NAV, O_CQ, O_CKV, O_KR, O_MQ, O_GATE = (
    0, 256, 512, 640, 768, 1024, 1280, 1536, 1728, 1856, 1888, 2144)
NPROJ = 2144
FM_CHUNKS = [(0, 128), (128, 128),
             (256, 128), (384, 128),
             (512, 128),
             (768, 128), (896, 128),
             (1024, 128), (1152, 128),
             (1888, 128), (2016, 128)]
NFM = len(FM_CHUNKS)
LAT_CHUNKS = [(1536, 128), (1664, 64), (1728, 128)]
TOKW = 36


class ColPack:
    def __init__(self):
        self.off = {}
        self.n = 0

    def add(self, name, w):
        self.off[name] = (self.n, w)
        self.n += w

    def sl(self, name):
        o, w = self.off[name]
        return slice(o, o + w)


def make_colpack():
    cp = ColPack()
    for l in range(DEPTH):
        for nm, w in (("g1", 8), ("g2", 8), ("gm", 8), ("gmem", 8), ("qn", 2), ("kvn", 1),
                      ("ssd", 2), ("sink", 4), ("lre", 32), ("lim", 32), ("lst", 32)):
            cp.add("%s%d" % (nm, l), w)
    cp.add("flag", 1)
    cp.add("sgn", 1)
    return cp


CP = make_colpack()


class Builder:
    def __init__(self, T, dbg=False):
        self.T = T
        self.NT = T // 128
        self.dbg = dbg
        self.nc = bass.Bass("TRN2", target_bir_lowering=False)
        self.es = ExitStack()
        self.s = Sched(self.nc, self.es)
        self.uid = 0

    def name(self, p):
        self.uid += 1
        return "%s_%d" % (p, self.uid)

    def din(self, name, shape, dt=F32):
        return self.nc.dram_tensor(name, list(shape), dt, kind="ExternalInput").ap()

    def dout(self, name, shape, dt=F32):
        return self.nc.dram_tensor(name, list(shape), dt, kind="ExternalOutput").ap()

    def dscr(self, name, shape, dt=F32):
        kind = "ExternalOutput" if self.dbg else "Internal"
        return self.nc.dram_tensor(name, list(shape), dt, kind=kind).ap()

    def sb(self, ctx, shape, dt, p="t"):
        return ctx.enter_context(self.nc.sbuf_tensor(self.name(p), list(shape), dt))

    def ps(self, ctx, shape, dt=F32, p="ps"):
        return ctx.enter_context(self.nc.psum_tensor(self.name(p), list(shape), dt))

    def sbring(self, ctx, n, shape, dt, p="r"):
        return Ring([self.sb(ctx, shape, dt, p) for _ in range(n)])

    def psring(self, ctx, n, shape, dt=F32, p="pr"):
        return Ring([self.ps(ctx, shape, dt, p) for _ in range(n)])

    def mm(self, out, lhsT, rhs, start, stop, reads, writes, sig=True):
        self.s.op("pe", lambda e: e.matmul(out, lhsT=lhsT, rhs=rhs, start=start, stop=stop),
                  reads, writes, sig)

    def tr(self, out, in_, ident, reads, writes, sig=True):
        self.s.op("pe", lambda e: e.transpose(out=out, in_=in_, identity=ident), reads, writes, sig)

    def act(self, out, in_, func, reads, writes, **kw):
        self.s.op("act", lambda e: e.activation(out=out, in_=in_, func=func, **kw), reads, writes)

    def tsc(self, eng, out, in0, s1, s2, op0, op1, reads, writes):
        if op1 is None:
            self.s.op(eng, lambda e: e.tensor_scalar(out=out, in0=in0, scalar1=s1, scalar2=None, op0=op0),
                      reads, writes)
        else:
            self.s.op(eng, lambda e: e.tensor_scalar(out=out, in0=in0, scalar1=s1, scalar2=s2,
                                                     op0=op0, op1=op1), reads, writes)

    def ttn(self, eng, out, in0, in1, op, reads, writes):
        self.s.op(eng, lambda e: e.tensor_tensor(out=out, in0=in0, in1=in1, op=op), reads, writes)

    def stt(self, out, in0, scalar, in1, op0, op1, reads, writes):
        self.s.op("dve", lambda e: e.scalar_tensor_tensor(out=out, in0=in0, scalar=scalar, in1=in1,
                                                          op0=op0, op1=op1), reads, writes)

    def cp(self, eng, out, in_, reads, writes):
        if eng == "act":
            self.s.op("act", lambda e: e.copy(out=out, in_=in_), reads, writes)
        else:
            self.s.op(eng, lambda e: e.tensor_copy(out=out, in_=in_), reads, writes)

    def rstd(self, src, n, rsrc, junk, sv, rsv):
        jk, rj = junk
        self.act(jk, src, AF.Square, [rsrc], [rj, rsv], accum_out=sv[:, 0:1])
        self.tsc("dve", sv[:, 1:2], sv[:, 0:1], 1.0 / n, EPS, ALU.mult, ALU.add, [rsv], [rsv])
        self.act(sv[:, 2:3], sv[:, 1:2], AF.Sqrt, [rsv], [rsv])
        self.s.op("dve", lambda e: e.reciprocal(out=sv[:, 3:4], in_=sv[:, 2:3]), [rsv], [rsv])

    def load_cast(self, ctx2, dst_fn, src_fn, n, shape, gain_fn=None, rW=None, st=None):
        s = self.s
        if st is None:
            st = self.sbring(ctx2, 2, shape, F32, "stg")
        for i in range(n):
            t, r = st.next()
            src = src_fn(i)
            tv = t[0:src.shape[0], 0:src.shape[1]]
            s.dma(tv, src, writes=[r])
            eng = ("dve", "act")[i % 2]
            dst = dst_fn(i)
            g = gain_fn(i) if gain_fn is not None else None
            if g is None:
                self.cp(eng, dst, tv, [r], [rW])
            elif eng == "act":
                self.act(dst, tv, AF.Copy, [r], [rW], scale=g)
            else:
                self.tsc(eng, dst, tv, g, None, ALU.mult, None, [r], [rW])

    def ffn_phase(self, h_in, h_out, wg, wu, wd, gain_col):
        nc, s, T = self.nc, self.s, self.T
        ident = self.ident
        with ExitStack() as ctx:
            WG = self.sb(ctx, [128, 8, DFF], BF16, "WG")
            WU = self.sb(ctx, [128, 8, DFF], BF16, "WU")
            WD = self.sb(ctx, [128, NFC, D], BF16, "WD")
            rW = Res()
            with ExitStack() as c2:
                st = self.sbring(c2, 4, [128, DFF], F32, "stg")
                self.load_cast(c2, lambda i: WG[:, i, :], lambda i: wg[i * 128:(i + 1) * 128, :], 8,
                               None, lambda i: gain_col[:, i:i + 1], rW, st)
                self.load_cast(c2, lambda i: WU[:, i, :], lambda i: wu[i * 128:(i + 1) * 128, :], 8,
                               None, lambda i: gain_col[:, i:i + 1], rW, st)
                self.load_cast(c2, lambda i: WD[:, i, :], lambda i: wd[i * 128:(i + 1) * 128, :], NFC,
                               None, None, rW, st)
                s.barrier()
            xt = self.sbring(ctx, 2, [128, D], F32, "xt")
            xr = self.sbring(ctx, 2, [128, D], F32, "xr")
            xn = self.sbring(ctx, 2, [128, D], BF16, "xn")
            st4 = self.sbring(ctx, 3, [128, 4], F32, "st4")
            xTr = self.sbring(ctx, 2, [128, 8, 512], BF16, "xT")
            hT = self.sb(ctx, [128, NFC, 512], BF16, "hT")
            rhT = Res()
            sg = self.sbring(ctx, 2, [128, 512], BF16, "sg")
            pT = self.psring(ctx, 1, [128, D], BF16, "pT")
            pG = self.psring(ctx, 2, [128, 512], F32, "pG")
            pU = self.psring(ctx, 2, [128, 512], F32, "pU")
            pD = self.psring(ctx, 2, [128, 512], F32, "pD")
            NS = T // 512
            xTs = {}
            pend = {}

            def norm_chain(si, tt):
                if si >= NS:
                    return
                if tt == 0:
                    xTs[si] = xTr.next()
                t0 = si * 512 + tt * 128
                x, rx = xt.next()
                s.dma(x[:], h_in[t0:t0 + 128, :], writes=[rx])
                xb, rxb = xn.next()
                sv, rs = st4.next()
                self.rstd(x[:], D, rx, (xb[:], rxb), sv, rs)
                self.tsc("dve", xb[:], x[:], sv[:, 3:4], None, ALU.mult, None, [rx, rs], [rxb])
                pend[(si, tt)] = (xb, rxb)

            def transposes(si, tt):
                if si >= NS:
                    return
                xb, rxb = pend.pop((si, tt))
                xT, rxT = xTs[si]
                p, rp = pT.next()
                for kc in range(8):
                    self.tr(p[:, kc * 128:(kc + 1) * 128], xb[:, kc * 128:(kc + 1) * 128], ident[:],
                            [rxb], [rp], sig=(kc == 7))
                self.cp("act", xT[:, :, tt * 128:(tt + 1) * 128],
                        p[:].rearrange("p (k t) -> p k t", k=8), [rp], [rxT])
            norm_chain(0, 0)
            for tt in range(4):
                norm_chain(0, tt + 1) if tt + 1 < 4 else None
                transposes(0, tt)
            for st_i in range(NS):
                xT, rxT = xTs[st_i]
                for fc in range(NFC):
                    if fc in (1, 5, 9, 13):
                        norm_chain(st_i + 1, (fc - 1) // 4)
                    if fc in (4, 8, 12, 16):
                        transposes(st_i + 1, (fc - 4) // 4)
                    g, rg = pG.next()
                    u, ru = pU.next()
                    for (W, pt, rpt) in ((WG, g, rg), (WU, u, ru)):
                        for kc in range(8):
                            self.mm(pt[:], W[:, kc, fc * 128:(fc + 1) * 128], xT[:, kc, :],
                                    kc == 0, kc == 7, [rW, rxT], [rpt], sig=(kc == 7))
                    sgt, rsg = sg.next()
                    self.act(sgt[:], g[:], AF.Silu, [rg], [rsg])
                    self.ttn("dve", hT[:, fc, :], sgt[:], u[:], ALU.mult, [rsg, ru], [rhT])
                xos = []
                for tt in range(4):
                    t0 = st_i * 512 + tt * 128
                    xo, rxo = xr.next()
                    s.dma(xo[:], h_in[t0:t0 + 128, :], writes=[rxo])
                    for half in range(2):
                        d, rd = pD.next()
                        for fc in range(NFC):
                            self.mm(d[:], hT[:, fc, tt * 128:(tt + 1) * 128],
                                    WD[:, fc, half * 512:(half + 1) * 512],
                                    fc == 0, fc == NFC - 1, [rhT, rW], [rd], sig=(fc == NFC - 1))
                        self.stt(xo[:, half * 512:(half + 1) * 512], d[:], 0.5,
                                 xo[:, half * 512:(half + 1) * 512], ALU.mult, ALU.add, [rd, rxo], [rxo])
                    s.dma(h_out[t0:t0 + 128, :], xo[:], reads=[rxo], q="pool")
            s.barrier()

    def final_norm(self, h_in, y_out, gain_bc):
        s, T = self.s, self.T
        with ExitStack() as ctx:
            xt = self.sbring(ctx, 3, [128, D], F32, "fx")
            junk = self.sbring(ctx, 1, [128, D], BF16, "fj")
            st4 = self.sbring(ctx, 3, [128, 4], F32, "fs")
            for ti in range(T // 128):
                t0 = ti * 128
                x, rx = xt.next()
                s.dma(x[:], h_in[t0:t0 + 128, :], writes=[rx])
                jk, rj = junk.next()
                sv, rs = st4.next()
                self.rstd(x[:], D, rx, (jk[:], rj), sv, rs)
                self.stt(x[:], x[:], sv[:, 3:4], gain_bc[:], ALU.mult, ALU.mult, [rx, rs], [rx])
                s.dma(y_out[t0:t0 + 128, :], x[:], reads=[rx], q="pool")
            s.barrier()

    def proj_phase(self, h_in, l):
        s, T = self.s, self.T
        ident, cols, W = self.ident, self.cols, self.W
        with ExitStack() as ctx:
            Win = self.sb(ctx, [128, 8, NPROJ], BF16, "Win")
            Wq = self.sb(ctx, [128, 2, 384], BF16, "Wq")
            Wkv = self.sb(ctx, [128, 512], BF16, "Wkv")
            rW = Res()
            with ExitStack() as c2:
                st = self.sbring(c2, 3, [128, NPROJ], F32, "stg")
                gm = cols[:, CP.sl("gm%d" % l)]
                win = W["w_in%d" % l]
                self.load_cast(c2, lambda i: Win[:, i, :], lambda i: win[i * 128:(i + 1) * 128, 0:NPROJ],
                               8, None, lambda i: gm[:, i:i + 1], rW, st)
                qn = cols[:, CP.sl("qn%d" % l)]
                wq = W["wq%d" % l]
                self.load_cast(c2, lambda i: Wq[0:(128 if i == 0 else 64), i, :],
                               lambda i: wq[i * 128:min(192, (i + 1) * 128), :], 2, None,
                               lambda i: qn[0:(128 if i == 0 else 64), i:i + 1], rW, st)
                for c0 in (O_SWQ, O_NAQ, O_MQ):
                    self.tsc("dve", Win[:, :, c0:c0 + 256], Win[:, :, c0:c0 + 256], ATT_SCALE, None, ALU.mult,
                             None, [rW], [rW])
                kvn = cols[:, CP.sl("kvn%d" % l)]
                wkv = W["wkv%d" % l]
                self.load_cast(c2, lambda i: Wkv[:, :], lambda i: wkv[:, :], 1, None,
                               lambda i: kvn[:, 0:1], rW, st)
                s.barrier()
            xt = self.sbring(ctx, 2, [128, D], F32, "xt")
            xn = self.sbring(ctx, 2, [128, D], BF16, "xn")
            junk = self.sbring(ctx, 1, [128, 512], BF16, "junk")
            st4 = self.sbring(ctx, 3, [128, 4], F32, "st4")
            xTr = self.sbring(ctx, 2, [128, 8, 512], BF16, "xT")
            fm = self.sbring(ctx, 3, [128, 512], BF16, "fm")
            latTr = [self.sbring(ctx, 2, [128, 512], BF16, "latT") for _ in range(3)]
            vt = self.sbring(ctx, 2, [128, 384], BF16, "vt")
            tok = self.sbring(ctx, 3, [128, TOKW], F32, "tok")
            lat = self.sbring(ctx, 2, [128, 352], F32, "lat")
            sq = self.sbring(ctx, 2, [128, 4], F32, "sq")
            skv = self.sbring(ctx, 2, [128, 4], F32, "skv")
            q_s = self.sbring(ctx, 2, [128, 4, 96], F32, "q_s")
            kv_s = self.sbring(ctx, 2, [128, 4, 128], F32, "kv_s")
            qf = self.sbring(ctx, 2, [128, 4, 99], BF16, "qf")
            kf = self.sbring(ctx, 2, [128, 4, 99], BF16, "kf")
            vm = self.sbring(ctx, 2, [128, 4, 65], BF16, "vm")
            kr = self.sbring(ctx, 2, [128, 32], F32, "kr")
            tmpq = self.sbring(ctx, 4, [128, 4, 16], F32, "rtq")
            tmpk = self.sbring(ctx, 4, [128, 16], F32, "rtk")
            nrmq = self.sbring(ctx, 2, [128, 12], F32, "nrmq")
            nrmk = self.sbring(ctx, 2, [128, 12], F32, "nrmk")
            qTr = self.sbring(ctx, 2, [128, 4, 512], BF16, "qT_sb")
            kTr = self.sbring(ctx, 2, [128, 4, 512], BF16, "kT_sb")
            pT = self.psring(ctx, 1, [128, D], BF16, "pT")
            pF = self.psring(ctx, 2, [128, 512], F32, "pF")
            psV = self.psring(ctx, 1, [128, 384], F32, "psV")
            psL = self.psring(ctx, 1, [128, 352], F32, "psL")
            psU = self.psring(ctx, 2, [128, 512], F32, "psU")
            pQT = self.psring(ctx, 1, [128, 4, 128], BF16, "pQT")
            kmax2, rkm = self.kmax2, self.rkm
            s.op("dve", lambda e: e.memset(kmax2[:], 0.0), [], [rkm])
            for t_, r_ in zip(kf.t, kf.r):
                s.op("pool", lambda e, t_=t_: e.memset(t_[:], 1.0), [], [r_])
            for t_, r_ in zip(vm.t, vm.r):
                s.op("pool", lambda e, t_=t_: e.memset(t_[:], 1.0), [], [r_])
            tokd = self.tokd
            NS = T // 512
            xTs = {}

            def run(gens):
                gens = [g for g in gens if g is not None]
                while gens:
                    for g in list(gens):
                        try:
                            next(g)
                        except StopIteration:
                            gens.remove(g)

            def prep(si):
                if si >= NS:
                    return
                xTs[si] = xTr.next()
                xT, rxT = xTs[si]
                for tt in range(4):
                    tk = si * 512 + tt * 128
                    x, rx = xt.next()
                    s.dma(x[:], h_in[tk:tk + 128, :], writes=[rx])
                    xb, rxb = xn.next()
                    sv, rs = st4.next()
                    self.rstd(x[:], D, rx, (xb[:], rxb), sv, rs)
                    yield
                    self.tsc("dve", xb[:], x[:], sv[:, 3:4], None, ALU.mult, None, [rx, rs], [rxb])
                    yield
                    p, rp = pT.next()
                    for kc in range(8):
                        self.tr(p[:, kc * 128:(kc + 1) * 128], xb[:, kc * 128:(kc + 1) * 128], ident[:],
                                [rxb], [rp], sig=(kc == 7))
                    self.cp("act", xT[:, :, tt * 128:(tt + 1) * 128],
                            p[:].rearrange("p (k t) -> p k t", k=8), [rp], [rxT])
                    yield

            LT = {}

            def part2(si):
                xT, rxT = xTs[si]
                t0 = si * 512
                s.dma(self.uT_d[:, :, t0:t0 + 512].rearrange("k p t -> p k t"), xT[:], reads=[rxT], q="pool")
                lts = [r.next() for r in latTr]
                LT[si] = lts
                for ci, (c0, w) in enumerate(FM_CHUNKS + LAT_CHUNKS):
                    pf, rpf = pF.next()
                    for kc in range(8):
                        self.mm(pf[0:w, :], Win[:, kc, c0:c0 + w], xT[:, kc, :], kc == 0, kc == 7,
                                [rW, rxT], [rpf], sig=(kc == 7))
                    if ci < NFM:
                        f, rf = fm.next()
                        self.cp("act" if ci % 2 == 0 else "dve", f[0:w, :], pf[0:w, :], [rpf], [rf])
                        s.dma(self.PT_d[ci, 0:w, t0:t0 + 512], f[0:w, :], reads=[rf], q="pool")
                    else:
                        lt, rlt = lts[ci - NFM]
                        self.cp("act" if ci % 2 == 0 else "dve", lt[0:w, :], pf[0:w, :], [rpf], [rlt])
                    yield

            TS = {}

            def stA(si, tt):
                xT, rxT = xTs[si]
                lts = LT[si]
                tk = si * 512 + tt * 128
                tsl = slice(tt * 128, (tt + 1) * 128)
                pv, rpv = psV.next()
                for (c0, w, o0) in ((O_SWV, 128, 0), (O_NAV, 256, 128)):
                    for kc in range(8):
                        self.mm(pv[:, o0:o0 + w], xT[:, kc, tsl], Win[:, kc, c0:c0 + w],
                                kc == 0, kc == 7, [rW, rxT], [rpv], sig=(kc == 7))
                v, rv = vt.next()
                self.cp("act", v[:], pv[:], [rpv], [rv])
                s.dma(self.VT_d[tk:tk + 128, :], v[:], reads=[rv], q="pool")
                yield
                pl, rpl = psL.next()
                for kc in range(8):
                    self.mm(pl[:], xT[:, kc, tsl], Win[:, kc, O_CQ:O_CQ + 352], kc == 0, kc == 7,
                            [rW, rxT], [rpl], sig=(kc == 7))
                la, rla = lat.next()
                self.cp("act", la[:], pl[:], [rpl], [rla])
                tb, rtb = tok.next()
                s.dma(tb[:], tokd[tk:tk + 128, :], writes=[rtb])
                yield
                jk, rj = junk.next()
                a, ra = sq.next()
                self.rstd(la[:, 0:192], 192, rla, (jk[:, 0:192], rj), a, ra)
                yield
                b_, rb = skv.next()
                self.rstd(la[:, 192:320], 128, rla, (jk[:, 0:128], rj), b_, rb)
                yield
                pq, rpq = psU.next()
                self.mm(pq[:, 0:384], lts[0][0][:, tsl], Wq[:, 0, :], True, False, [rW, lts[0][1]], [rpq], sig=False)
                self.mm(pq[:, 0:384], lts[1][0][0:64, tsl], Wq[0:64, 1, :], False, True, [rW, lts[1][1]], [rpq])
                pk, rpk = psU.next()
                self.mm(pk[:], lts[2][0][:, tsl], Wkv[:, :], True, True, [rW, lts[2][1]], [rpk])
                TS[(si, tt)] = dict(tb=tb, rtb=rtb, la=la, rla=rla, a=a, ra=ra, b=b_, rb=rb, pq=pq, rpq=rpq,
                                    pk=pk, rpk=rpk)
                yield

            def stB(si, tt):
                d = TS[(si, tt)]
                tb, rtb, a, ra, pq, rpq = d["tb"], d["rtb"], d["a"], d["ra"], d["pq"], d["rpq"]
                tsl = slice(tt * 128, (tt + 1) * 128)
                if tt == 0:
                    d["qT"] = qTr.next()
                else:
                    d["qT"] = TS[(si, 0)]["qT"]
                qT_sb, rqT = d["qT"]
                jk, rj = junk.next()
                qs, rqs = q_s.next()
                self.tsc("dve", qs[:].rearrange("p h d -> p (h d)"), pq[:, 0:384], a[:, 3:4], None,
                         ALU.mult, None, [rpq, ra], [rqs])
                yield
                nm, rnm = nrmq.next()
                for h in range(4):
                    self.act(jk[:, 0:96], qs[:, h, :], AF.Square, [rqs], [rj, rnm], accum_out=nm[:, h:h + 1])
                    yield
                self.act(nm[:, 4:8], nm[:, 0:4], AF.Sqrt, [rnm], [rnm])
                q, rq = qf.next()
                self.tsc("dve", q[:, :, 0], nm[:, 4:8], -1.0, None, ALU.mult, None, [rnm], [rq])
                yield
                self.cp("pool", q[:, :, 1:3], tb[:, 32:34].unsqueeze(1).to_broadcast([128, 4, 2]), [rtb], [rq])
                self.cp("act", q[:, :, 3:67], qs[:, :, 0:64], [rqs], [rq])
                yield
                cosb = tb[:, 0:16].unsqueeze(1).to_broadcast([128, 4, 16])
                sinb = tb[:, 16:32].unsqueeze(1).to_broadcast([128, 4, 16])
                ta, rta = tmpq.next()
                tbb, rtbb = tmpq.next()
                self.ttn("dve", ta[:], qs[:, :, 64:80], cosb, ALU.mult, [rqs, rtb], [rta])
                yield
                self.ttn("dve", tbb[:], qs[:, :, 80:96], sinb, ALU.mult, [rqs, rtb], [rtbb])
                yield
                self.ttn("dve", q[:, :, 67:83], ta[:], tbb[:], ALU.subtract, [rta, rtbb], [rq])
                yield
                tc, rtc = tmpq.next()
                td, rtd = tmpq.next()
                self.ttn("dve", tc[:], qs[:, :, 64:80], sinb, ALU.mult, [rqs, rtb], [rtc])
                yield
                self.ttn("dve", td[:], qs[:, :, 80:96], cosb, ALU.mult, [rqs, rtb], [rtd])
                yield
                self.ttn("dve", q[:, :, 83:99], tc[:], td[:], ALU.add, [rtc, rtd], [rq])
                yield
                pt2, rpt2 = pQT.next()
                for h in range(4):
                    self.tr(pt2[0:99, h, :], q[:, h, :], ident[:], [rq], [rpt2], sig=(h == 3))
                self.cp("act", qT_sb[0:99, :, tsl], pt2[0:99, :, :], [rpt2], [rqT])
                if tt == 3:
                    t0 = si * 512
                    s.dma(self.QM_d[:, :, t0:t0 + 512].rearrange("h p t -> p h t"), qT_sb[0:99, :, :],
                          reads=[rqT], q="pool")
                yield

            def stC(si, tt):
                d = TS[(si, tt)]
                tb, rtb, la, rla, b_, rb, pk, rpk = (d["tb"], d["rtb"], d["la"], d["rla"], d["b"], d["rb"],
                                                      d["pk"], d["rpk"])
                tk = si * 512 + tt * 128
                tsl = slice(tt * 128, (tt + 1) * 128)
                if tt == 0:
                    d["kT"] = kTr.next()
                else:
                    d["kT"] = TS[(si, 0)]["kT"]
                kT_sb, rkT = d["kT"]
                jk, rj = junk.next()
                ks, rks = kv_s.next()
                self.tsc("dve", ks[:].rearrange("p h d -> p (h d)"), pk[:], b_[:, 3:4], None,
                         ALU.mult, None, [rpk, rb], [rks])
                yield
                krt, rkr = kr.next()
                c1 = tb[:, 0:16]
                s1 = tb[:, 16:32]
                ta, rta = tmpk.next()
                tbb, rtbb = tmpk.next()
                self.ttn("dve", ta[:], la[:, 320:336], c1, ALU.mult, [rla, rtb], [rta])
                yield
                self.ttn("dve", tbb[:], la[:, 336:352], s1, ALU.mult, [rla, rtb], [rtbb])
                yield
                self.ttn("dve", krt[:, 0:16], ta[:], tbb[:], ALU.subtract, [rta, rtbb], [rkr])
                yield
                tc, rtc = tmpk.next()
                td, rtd = tmpk.next()
                self.ttn("dve", tc[:], la[:, 320:336], s1, ALU.mult, [rla, rtb], [rtc])
                yield
                self.ttn("dve", td[:], la[:, 336:352], c1, ALU.mult, [rla, rtb], [rtd])
                yield
                self.ttn("dve", krt[:, 16:32], tc[:], td[:], ALU.add, [rtc, rtd], [rkr])
                yield
                k, rk = kf.next()
                self.cp("pool", k[:, :, 1:3], tb[:, 34:36].unsqueeze(1).to_broadcast([128, 4, 2]), [rtb], [rk])
                self.cp("act", k[:, :, 3:67], ks[:, :, 0:64], [rks], [rk])
                yield
                self.cp("pool", k[:, :, 67:99], krt[:].unsqueeze(1).to_broadcast([128, 4, 32]), [rkr], [rk])
                nm2, rnm2 = nrmk.next()
                for h in range(4):
                    self.act(jk[:, 0:64], ks[:, h, 0:64], AF.Square, [rks], [rj, rnm2], accum_out=nm2[:, h:h + 1])
                    yield
                self.act(jk[:, 0:32], krt[:], AF.Square, [rkr], [rj, rnm2], accum_out=nm2[:, 4:5])
                self.tsc("dve", nm2[:, 0:4], nm2[:, 0:4], nm2[:, 4:5], None, ALU.add, None, [rnm2], [rnm2])
                self.ttn("dve", kmax2[:], kmax2[:], nm2[:, 0:4], ALU.max, [rnm2, rkm], [rkm])
                yield
                vmt, rvm = vm.next()
                self.cp("pool", vmt[:, :, 0:64], ks[:, :, 64:128], [rks], [rvm])
                s.dma(self.VM_d[tk:tk + 128, :].rearrange("t (h c) -> t h c", h=4), vmt[:], reads=[rvm], q="pool")
                yield
                pt3, rpt3 = pQT.next()
                for h in range(4):
                    self.tr(pt3[0:99, h, :], k[:, h, :], ident[:], [rk], [rpt3], sig=(h == 3))
                self.cp("dve", kT_sb[0:99, :, tsl], pt3[0:99, :, :], [rpt3], [rkT])
                if tt == 3:
                    t0 = si * 512
                    s.dma(self.KM_d[:, :, t0:t0 + 512].rearrange("h p t -> p h t"), kT_sb[0:99, :, :],
                          reads=[rkT], q="pool")
                yield

            run([prep(0)])
            tiles = [(si, tt) for si in range(NS) for tt in range(4)]
            for si in range(NS):
                run([part2(si), prep(si + 1)])
                for tt in range(4):
                    if tt == 0:
                        run([stA(si, 0)])
                    nxt = stA(si, tt + 1) if tt + 1 < 4 else None
                    run([stB(si, tt), stC(si, tt), nxt])
                for tt in range(4):
                    del TS[(si, tt)]
            s.barrier()

    def mla_phase(self, l):
        s, T, NT = self.s, self.T, self.NT
        ident, identf = self.ident, self.identf
        with ExitStack() as ctx:
            KMs = self.sb(ctx, [128, 4, T], BF16, "KMs")
            VMs = self.sb(ctx, [128, NT, 260], BF16, "VMs")
            rK, rV = Res(), Res()
            for h in range(4):
                for c in range(0, T, 2048):
                    w = min(2048, T - c)
                    s.dma(KMs[0:99, h, c:c + w], self.KM_d[h, :, c:c + w], writes=[rK])
            for c in range(0, NT, 16):
                w = min(16, NT - c)
                s.dma(VMs[:, c:c + w, :],
                      self.VM_d[c * 128:(c + w) * 128, :].rearrange("(k p) c -> p k c", p=128), writes=[rV])
            ctxk = ExitStack()
            pK = self.ps(ctxk, [128, 4, 128], F32, "pK")
            rpK = Res()
            km = self.sb(ctx, [128, 8], F32, "km")
            rkm2 = Res()
            for h in range(4):
                self.tr(pK[0:1, h, :], self.kmax2[:, h:h + 1], identf[:], [self.rkm], [rpK], sig=(h == 3))
            s.op("dve", lambda e: e.tensor_reduce(out=km[0:1, 0:4], in_=pK[0:1, :, :], axis=AX.X, op=ALU.max),
                 [rpK], [rkm2])
            self.act(km[0:1, 4:8], km[0:1, 0:4], AF.Sqrt, [rkm2], [rkm2])
            s.barrier()
            ctxk.close()
            QW = 1024
            QT = self.sbring(ctx, 2, [128, 4, QW], BF16, "QT")
            Pt = self.sbring(ctx, 3, [128, QW], BF16, "Pt")
            rrr = self.sbring(ctx, 2, [128, 512], F32, "mrr")
            osb = self.sbring(ctx, 2, [64, 512], F32, "mosb")
            onb = self.sbring(ctx, 3, [64, 512], BF16, "monb")
            pS = self.psring(ctx, 3, [128, QW], F32, "pS")
            pO = self.psring(ctx, 1, [128, QW], F32, "pO")
            bcs = self.sbring(ctx, 2, [64, 512], F32, "mbc")
            nslot = [0]
            onesf = self.onesf
            NQG = T // QW
            qts = {}

            def load_q(qg):
                if qg >= NQG or qg in qts:
                    return
                qt, rq = QT.next()
                q0 = qg * QW
                s.dma(qt[0:99, :, :], self.QM_d[:, :, q0:q0 + QW].rearrange("h p t -> p h t"), writes=[rq])
                for h in range(4):
                    self.tsc("dve", qt[0:1, h, :], qt[0:1, h, :], km[0:1, 4 + h:5 + h], None, ALU.mult, None,
                             [rq, rkm2], [rq])
                qts[qg] = (qt, rq)
            items = [(qg, h, kt) for qg in range(NQG) for h in range(4) for kt in range(NT)]
            n = len(items)
            LA = 2
            sbuf_ = {}
            pbuf = {}
            obuf = {}

            def emit_S(i):
                qg, h, kt = items[i]
                load_q(qg)
                qt, rq = qts[qg]
                sp_, rs_ = pS.next()
                for hf in range(2):
                    self.mm(sp_[:, hf * 512:(hf + 1) * 512], KMs[0:99, h, kt * 128:(kt + 1) * 128],
                            qt[0:99, h, hf * 512:(hf + 1) * 512], True, True, [rK, rq], [rs_], sig=(hf == 1))
                sbuf_[i] = (sp_, rs_)

            def emit_exp(i):
                sp_, rs_ = sbuf_.pop(i)
                p, rp = Pt.next()
                self.act(p[:], sp_[:], AF.Exp, [rs_], [rp], scale=MLA_SCALE)
                pbuf[i] = (p, rp)

            def emit_PV(i):
                qg, h, kt = items[i]
                p, rp = pbuf.pop(i)
                if kt == 0:
                    obuf[(qg, h)] = pO.next()
                o, ro = obuf[(qg, h)]
                for hf in range(2):
                    self.mm(o[0:65, hf * 512:(hf + 1) * 512], VMs[:, kt, h * 65:(h + 1) * 65],
                            p[:, hf * 512:(hf + 1) * 512], kt == 0, kt == NT - 1, [rp, rV], [ro], sig=(hf == 1))
                if kt == NT - 1:
                    hs = []
                    for hf in range(2):
                        oh = o[:, hf * 512:(hf + 1) * 512]
                        rr, rrr_ = rrr.next()
                        s.op("dve", lambda e, rr=rr, oh=oh: e.reciprocal(out=rr[64:65, :], in_=oh[64:65, :]),
                             [ro], [rrr_])
                        ob, rob = osb.next()
                        self.cp("dve", ob[:], oh[0:64, :], [ro], [rob])
                        hs.append((rr, rrr_, ob, rob))
                    bb = []
                    for hf in range(2):
                        rr, rrr_, ob, rob = hs[hf]
                        sl = nslot[0] % 8
                        nslot[0] += 1
                        rsl = self.rNRM[sl]
                        s.dma(self.NRM_d[sl:sl + 1, :], rr[64:65, :], reads=[rrr_], writes=[rsl], q="pool")
                        bc, rbc = bcs.next()
                        s.dma(bc[:], self.NRM_d[sl:sl + 1, :].partition_broadcast(64), reads=[rsl], writes=[rbc])
                        bb.append((bc, rbc))
                    for hf in range(2):
                        rr, rrr_, ob, rob = hs[hf]
                        bc, rbc = bb[hf]
                        on, ron = onb.next()
                        self.ttn("dve", on[:], ob[:], bc[:], ALU.mult, [rob, rbc], [ron])
                        q0 = qg * QW + hf * 512
                        s.dma(self.BR_d[3, h // 2, (h % 2) * 64:(h % 2) * 64 + 64, q0:q0 + 512], on[:], reads=[ron],
                              q="pool")
                    del obuf[(qg, h)]
                    if h == 1:
                        load_q(qg + 1)
            for i in range(min(LA, n)):
                emit_S(i)
            for i in range(n):
                emit_exp(i)
                if i + LA < n:
                    emit_S(i + LA)
                emit_PV(i)
            s.barrier()

    def attn_core(self, ctx, name, qchunk0, kv_of, KT, rKT, V, rV, keytiles, table, sink, br_idx, maxk):
        s, T, NT = self.s, self.T, self.NT
        ident = self.ident
        QT = self.sbring(ctx, 2, [64, 4, 512], BF16, "aQT")
        Sb = self.sbring(ctx, 2, [128, maxk * 128], F32, "aSb")
        Pb = self.sbring(ctx, 2, [128, maxk * 128], BF16, "aPb")
        PTs = self.sbring(ctx, 2, [128, maxk, 128], BF16, "aPT")
        st = self.sbring(ctx, 6, [128, 8], F32, "ast")
        Oall = self.sbring(ctx, 2, [128, 256], BF16, "aOall")
        brT = self.sbring(ctx, 2, [128, 2, 512], BF16, "abrT")
        nb = 2 if maxk > 4 else 1
        pS = self.psring(ctx, 2, [128, 512 * nb], F32, "apS")
        pP = self.psring(ctx, 2, [128, maxk, 128], BF16, "apP")
        pO_t = self.ps(ctx, [128, 64], F32, "apO")
        pO = Ring([pO_t[:, :]])
        pT2 = self.psring(ctx, 1, [128, 2, 128], BF16, "apT2")
        NQG = T // 512
        qts = {}

        def load_q(qg):
            if qg >= NQG or qg in qts:
                return
            qt, rq = QT.next()
            q0 = qg * 512
            for h in range(4):
                s.dma(qt[0:64, h, :], self.PT_d[qchunk0 + h // 2, (h % 2) * 64:(h % 2) * 64 + 64, q0:q0 + 512],
                      writes=[rq])
            qts[qg] = (qt, rq)
        items = [(qg, jj, h) for qg in range(NQG) for jj in range(4) for h in range(4)]
        n = len(items)
        it = {}
        tabs = {}
        oas = {}
        bts = {}

        def fS(i):
            qg, jj, h = items[i]
            j = qg * 4 + jj
            load_q(qg)
            if jj == 1 and h == 0:
                load_q(qg + 1)
            qt, rq = qts[qg]
            kts = keytiles(j)
            nk = len(kts)
            if h == 0:
                tabs[j] = table(j) if table is not None else None
            g = kv_of(h)
            sp_, rs_ = pS.next()
            for ii, kt in enumerate(kts):
                self.mm(sp_[:, ii * 128:(ii + 1) * 128], qt[0:64, h, jj * 128:(jj + 1) * 128],
                        KT[0:64, g, kt * 128:(kt + 1) * 128], True, True, [rq, rKT], [rs_], sig=(ii == nk - 1))
            it[i] = dict(sp=sp_, rs=rs_, kts=kts, g=g, tab=tabs[j])

        def fD1(i):
            qg, jj, h = items[i]
            d = it[i]
            sv, rsv = st.next()
            Wd = len(d["kts"]) * 128
            sp_, rs_ = d["sp"], d["rs"]
            if d["tab"] is not None:
                tb_ap, rtab = d["tab"]
                sb_, rsb = Sb.next()
                self.ttn("dve", sb_[:, 0:Wd], sp_[:, 0:Wd], tb_ap[:, h, 0:Wd], ALU.add, [rs_, rtab], [rsb])
                s.op("dve", lambda e, sv=sv, sb_=sb_, Wd=Wd: e.tensor_reduce(
                    out=sv[:, 0:1], in_=sb_[:, 0:Wd], axis=AX.X, op=ALU.max), [rsb], [rsv])
                if sink is not None:
                    self.ttn("dve", sv[:, 0:1], sv[:, 0:1], sink[:, h:h + 1], ALU.max, [rsv], [rsv])
                src, rsrc = sb_, rsb
            else:
                s.op("dve", lambda e, sv=sv, sp_=sp_, Wd=Wd: e.tensor_reduce(
                    out=sv[:, 0:1], in_=sp_[:, 0:Wd], axis=AX.X, op=ALU.max), [rs_], [rsv])
                src, rsrc = sp_, rs_
            self.act(sv[:, 1:2], sv[:, 0:1], AF.Copy, [rsv], [rsv], scale=-1.0)
            d.update(sv=sv, rsv=rsv, src=src, rsrc=rsrc, Wd=Wd)

        def fE(i):
            qg, jj, h = items[i]
            d = it[i]
            pb, rpb = Pb.next()
            sv, rsv, Wd = d["sv"], d["rsv"], d["Wd"]
            self.act(pb[:, 0:Wd], d["src"][:, 0:Wd], AF.Exp, [d["rsrc"], rsv], [rpb, rsv], bias=sv[:, 1:2], scale=1.0,
                     accum_out=sv[:, 2:3])
            if sink is not None:
                self.act(sv[:, 3:4], sink[:, h:h + 1], AF.Exp, [rsv], [rsv], bias=sv[:, 1:2], scale=1.0)
            d.update(pb=pb, rpb=rpb)

        def fT(i):
            d = it[i]
            nk = len(d["kts"])
            pp, rpp = pP.next()
            for ii in range(nk):
                self.tr(pp[:, ii, :], d["pb"][:, ii * 128:(ii + 1) * 128], ident[:], [d["rpb"]], [rpp],
                        sig=(ii == nk - 1))
            d.update(pp=pp, rpp=rpp)

        def fC(i):
            d = it[i]
            nk = len(d["kts"])
            pts, rpts = PTs.next()
            self.cp("act", pts[:, 0:nk, :], d["pp"][:, 0:nk, :], [d["rpp"]], [rpts])
            d.update(pts=pts, rpts=rpts)

        def fPV(i):
            d = it[i]
            kts, g = d["kts"], d["g"]
            nk = len(kts)
            o, ro = pO.next()
            for ii, kt in enumerate(kts):
                self.mm(o, d["pts"][:, ii, :], V[:, kt, g * 64:(g + 1) * 64], ii == 0, ii == nk - 1,
                        [d["rpts"], rV], [ro], sig=(ii == nk - 1))
            d.update(o=o, ro=ro)

        def fD2(i):
            qg, jj, h = items[i]
            d = it.pop(i)
            sv, rsv = d["sv"], d["rsv"]
            if h == 0:
                oas[(qg, jj)] = Oall.next()
            oa, roa = oas[(qg, jj)]
            if jj == 0 and h == 0:
                bts[qg] = brT.next()
            bt, rbt = bts[qg]
            if sink is not None:
                self.ttn("dve", sv[:, 2:3], sv[:, 2:3], sv[:, 3:4], ALU.add, [rsv], [rsv])
            s.op("dve", lambda e, sv=sv: e.reciprocal(out=sv[:, 4:5], in_=sv[:, 2:3]), [rsv], [rsv])
            self.act(oa[:, h * 64:(h + 1) * 64], d["o"], AF.Copy, [d["ro"], rsv], [roa], scale=sv[:, 4:5])
            if h == 3:
                pt, rpt = pT2.next()
                for c in range(2):
                    self.tr(pt[:, c, :], oa[:, c * 128:(c + 1) * 128], ident[:], [roa], [rpt], sig=(c == 1))
                self.cp("act", bt[:, :, jj * 128:(jj + 1) * 128], pt[:, :, :], [rpt], [rbt])
                if jj == 3:
                    q0 = qg * 512
                    s.dma(self.BR_d[br_idx, :, :, q0:q0 + 512].rearrange("c p t -> p c t"), bt[:], reads=[rbt],
                          q="pool")
        for k in range(-2, n + 1):
            if 0 <= k + 2 < n:
                fS(k + 2)
            if 0 <= k + 1 < n:
                fD1(k + 1)
                fE(k + 1)
            if 0 <= k < n:
                fT(k)
                fC(k)
            if 0 <= k - 1 < n:
                fPV(k - 1)
                fD2(k - 1)

    def swa_phase(self, l):
        s, T, NT = self.s, self.T, self.NT
        with ExitStack() as ctx:
            KT = self.sb(ctx, [64, 2, T], BF16, "swKT")
            V = self.sb(ctx, [128, NT, 128], BF16, "swV")
            tabs = self.sb(ctx, [128, 5, 4, 384], F32, "swtab")
            rKT, rV, rT = Res(), Res(), Res()
            for g in range(2):
                s.dma(KT[0:64, g, :], self.PT_d[4, g * 64:(g + 1) * 64, :], writes=[rKT])
            s.dma(V[:], self.VT_d[:, 0:128].rearrange("(k p) c -> p k c", p=128), writes=[rV])
            for i in range(5):
                s.dma(tabs[:, i, :, :], self.W["swa_tab%d" % l][i], writes=[rT])
            mid = NT // 2
            slot = {0: 1, mid - 1: 2, mid: 3, NT - 1: 4}

            def keytiles(j):
                return [max(j - 1, 0), j, min(j + 1, NT - 1)]

            def table(j):
                return (tabs[:, slot.get(j, 0), :, :], rT)
            sink = self.cols[:, CP.sl("sink%d" % l)]
            self.attn_core(ctx, "swa", 2, lambda h: h // 2, KT, rKT, V, rV, keytiles, table, sink, 1, 3)
            s.barrier()

    def na_keytiles(self, j):
        NT = self.NT
        mid = NT // 2
        base = min(max(j - 2, 0), NT - 6)
        if j == mid - 1:
            base = min(max(mid - 4, 0), NT - 6)
        return list(range(base, base + 6))

    def na_specials(self):
        NT = self.NT
        mid = NT // 2
        sp = sorted(set(x for x in (0, 1, mid - 2, mid - 1, mid, mid + 1, NT - 3, NT - 2, NT - 1)
                        if 0 <= x < NT))
        return sp

    def na_phase(self, l):
        s, T, NT = self.s, self.T, self.NT
        with ExitStack() as ctx:
            KT = self.sb(ctx, [64, 4, T], BF16, "naKT")
            V = self.sb(ctx, [128, NT, 256], BF16, "naV")
            tab0 = self.sb(ctx, [128, 4, 768], F32, "natab0")
            tabr = self.sbring(ctx, 2, [128, 4, 768], F32, "natabr")
            rKT, rV, rT0 = Res(), Res(), Res()
            for h in range(4):
                s.dma(KT[0:64, h, :], self.PT_d[7 + h // 2, (h % 2) * 64:(h % 2) * 64 + 64, :], writes=[rKT])
            s.dma(V[:], self.VT_d[:, 128:384].rearrange("(k p) c -> p k c", p=128), writes=[rV])
            natab = self.W["na_tab%d" % l]
            s.dma(tab0[:], natab[0], writes=[rT0])
            sp = self.na_specials()
            slot = {j: i + 1 for i, j in enumerate(sp)}

            def table(j):
                if j not in slot:
                    return (tab0, rT0)
                t, r = tabr.next()
                s.dma(t[:], natab[slot[j]], writes=[r])
                return (t, r)
            self.attn_core(ctx, "na", 5, lambda h: h, KT, rKT, V, rV, self.na_keytiles, table, None, 2, 6)
            s.barrier()

    def mem_phase(self, l):
        s, T, NT = self.s, self.T, self.NT
        ident, cols, W = self.ident, self.cols, self.W
        with ExitStack() as ctx:
            KT = self.sb(ctx, [64, 4, 512], BF16, "mKT")
            V = self.sb(ctx, [128, 4, 256], BF16, "mV")
            rKT, rV = Res(), Res()
            with ExitStack() as c1:
                Wm = self.sb(c1, [128, 8, 512], BF16, "Wm")
                rW = Res()
                gmem = cols[:, CP.sl("gmem%d" % l)]
                wm = W["wmem%d" % l]
                st = self.sbring(c1, 2, [128, 512], F32, "stg")
                self.load_cast(c1, lambda i: Wm[:, i, :], lambda i: wm[i * 128:(i + 1) * 128, :], 8, None,
                               lambda i: gmem[:, i:i + 1], rW, st)
                xt = self.sbring(c1, 2, [128, D], F32, "xt")
                xn = self.sbring(c1, 1, [128, D], BF16, "xn")
                junk = self.sbring(c1, 1, [128, D], BF16, "junk")
                st4 = self.sbring(c1, 2, [128, 4], F32, "st4")
                memT = self.sb(c1, [128, 8, 512], BF16, "memT")
                rmT = Res()
                pT = self.psring(c1, 1, [128, D], BF16, "pT")
                pF = self.psring(c1, 2, [128, 512], F32, "pF")
                for i in range(4):
                    x, rx = xt.next()
                    s.dma(x[:], self.memd[i // 2, (i % 2) * 128:(i % 2) * 128 + 128, :], writes=[rx])
                    jk, rj = junk.next()
                    sv, rs = st4.next()
                    self.rstd(x[:], D, rx, (jk[:], rj), sv, rs)
                    xb, rxb = xn.next()
                    self.tsc("dve", xb[:], x[:], sv[:, 3:4], None, ALU.mult, None, [rx, rs], [rxb])
                    p, rp = pT.next()
                    for kc in range(8):
                        self.tr(p[:, kc * 128:(kc + 1) * 128], xb[:, kc * 128:(kc + 1) * 128], ident[:],
                                [rxb], [rp], sig=(kc == 7))
                    self.cp("act", memT[:, :, i * 128:(i + 1) * 128],
                            p[:].rearrange("p (k t) -> p k t", k=8), [rp], [rmT])
                for h in range(4):
                    pf, rpf = pF.next()
                    for kc in range(8):
                        self.mm(pf[0:64, :], Wm[:, kc, h * 64:(h + 1) * 64], memT[:, kc, :], kc == 0, kc == 7,
                                [rW, rmT], [rpf], sig=(kc == 7))
                    self.cp("dve", KT[0:64, h, :], pf[0:64, :], [rpf], [rKT])
                for i in range(4):
                    pf, rpf = pF.next()
                    for kc in range(8):
                        self.mm(pf[:, 0:256], memT[:, kc, i * 128:(i + 1) * 128], Wm[:, kc, 256:512],
                                kc == 0, kc == 7, [rW, rmT], [rpf], sig=(kc == 7))
                    self.cp("dve", V[:, i, :], pf[:, 0:256], [rpf], [rV])
                s.barrier()
            mid = NT // 2

            def keytiles(j):
                sl = 0 if j < mid else 1
                return [2 * sl, 2 * sl + 1]
            self.attn_core(ctx, "mem", 9, lambda h: h, KT, rKT, V, rV, keytiles, None, None, 4, 2)
            s.barrier()

    def s5_phase(self, l):
        s, T, NT = self.s, self.T, self.NT
        cols, W = self.cols, self.W
        identf = self.identf
        TWO_PI = 2.0 * np.pi
        MAGIC = 12582912.0
        with ExitStack() as ctx:
            WA = self.sb(ctx, [128, 32, 128], BF16, "WA")
            WB = self.sb(ctx, [128, 32, 128], BF16, "WB")
            CW = self.sb(ctx, [128, 32, 128], BF16, "CW")
            CV = self.sb(ctx, [128, 32, 128], BF16, "CV")
            COS16 = self.sb(ctx, [128, 32, 128], BF16, "COS16")
            SIN16 = self.sb(ctx, [128, 32, 128], BF16, "SIN16")
            RT = self.sb(ctx, [128, 32, 128], F32, "RT")
            ROT = self.sb(ctx, [128, 32, 128], F32, "ROT")
            rc = self.sb(ctx, [128, 32], F32, "rc")
            rP = Res()
            with ExitStack() as c1:
                row = self.W["s5row%d" % l]
                R_ = [self.sb(c1, [128, 1024], F32, "row") for _ in range(12)]
                rr = Res()
                lre, lim, lst, bre, bim = R_[0:5]
                t1, t2, t3, t4, t5, t6, t7 = R_[5:12]
                A_ = lambda out, in_, func, **kw: self.act(out, in_, func, [rr], [rr], **kw)

                def frac_sin(dst, y, tmp):
                    self.tsc("dve", tmp, y, MAGIC, None, ALU.add, None, [rr], [rr])
                    self.tsc("dve", tmp, tmp, MAGIC, None, ALU.subtract, None, [rr], [rr])
                    self.ttn("dve", tmp, y, tmp, ALU.subtract, [rr], [rr])
                    A_(dst, tmp, AF.Sin, scale=TWO_PI)
                for dd in range(2):
                    for i in range(5):
                        s.dma(R_[i][:], row[i, :, dd * 1024:(dd + 1) * 1024], writes=[rr])
                    A_(t1[:], lst[:], AF.Exp)
                    self.ttn("dve", t2[:], lre[:], t1[:], ALU.mult, [rr], [rr])
                    A_(t2[:], t2[:], AF.Exp)
                    self.stt(t3[:], lim[:], 1.0 / TWO_PI, t1[:], ALU.mult, ALU.mult, [rr], [rr])
                    frac_sin(t4[:], t3[:], t5[:])
                    self.tsc("dve", t3[:], t3[:], 0.25, None, ALU.add, None, [rr], [rr])
                    frac_sin(t6[:], t3[:], t5[:])
                    self.ttn("dve", t4[:], t4[:], t2[:], ALU.mult, [rr], [rr])
                    self.ttn("dve", t6[:], t6[:], t2[:], ALU.mult, [rr], [rr])
                    self.tsc("dve", t6[:], t6[:], -1.0, None, ALU.add, None, [rr], [rr])
                    self.ttn("dve", t1[:], lre[:], lre[:], ALU.mult, [rr], [rr])
                    self.ttn("dve", t2[:], lim[:], lim[:], ALU.mult, [rr], [rr])
                    self.ttn("dve", t1[:], t1[:], t2[:], ALU.add, [rr], [rr])
                    s.op("dve", lambda e: e.reciprocal(out=t1[:], in_=t1[:]), [rr], [rr])
                    self.ttn("dve", t2[:], t6[:], lre[:], ALU.mult, [rr], [rr])
                    self.ttn("dve", t5[:], t4[:], lim[:], ALU.mult, [rr], [rr])
                    self.ttn("dve", t2[:], t2[:], t5[:], ALU.add, [rr], [rr])
                    self.ttn("dve", t2[:], t2[:], t1[:], ALU.mult, [rr], [rr])
                    self.ttn("dve", t3[:], t4[:], lre[:], ALU.mult, [rr], [rr])
                    self.ttn("dve", t5[:], t6[:], lim[:], ALU.mult, [rr], [rr])
                    self.ttn("dve", t3[:], t3[:], t5[:], ALU.subtract, [rr], [rr])
                    self.ttn("dve", t3[:], t3[:], t1[:], ALU.mult, [rr], [rr])
                    self.ttn("dve", t4[:], t2[:], bre[:], ALU.mult, [rr], [rr])
                    self.ttn("dve", t5[:], t3[:], bim[:], ALU.mult, [rr], [rr])
                    self.ttn("dve", t4[:], t4[:], t5[:], ALU.subtract, [rr], [rr])
                    self.ttn("dve", t6[:], t2[:], bim[:], ALU.mult, [rr], [rr])
                    self.ttn("dve", t5[:], t3[:], bre[:], ALU.mult, [rr], [rr])
                    self.ttn("dve", t6[:], t6[:], t5[:], ALU.add, [rr], [rr])
                    v3 = lambda t: t[:].rearrange("p (a n) -> p a n", n=64)
                    dsl = slice(dd * 16, (dd + 1) * 16)
                    self.cp("dve", WA[:, dsl, 0:64], v3(t4), [rr], [rP])
                    self.cp("dve", WA[:, dsl, 64:128], v3(t6), [rr], [rP])
                    self.cp("dve", WB[:, dsl, 0:64], v3(t6), [rr], [rP])
                    self.tsc("dve", WB[:, dsl, 64:128], v3(t4), -1.0, None, ALU.mult, None, [rr], [rP])
                s.barrier()
            with ExitStack() as c1:
                cst = self.W["s5const"]
                JT = self.sb(c1, [128, 32, 128], F32, "JT")
                COS = self.sb(c1, [128, 32, 128], F32, "COS")
                SIN = self.sb(c1, [128, 32, 128], F32, "SIN")
                Y = self.sb(c1, [128, 32, 128], F32, "Yt")
                TM = self.sb(c1, [128, 32, 128], F32, "TM")
                P2 = self.sb(c1, [128, 128], F32, "P2")
                sm = self.sb(c1, [128, 8, 32], F32, "sm")
                rr = Res()
                s.dma(JT[:].rearrange("p a j -> p (a j)"), cst[0], writes=[rr])
                s.dma(P2[:], self.W["p2"], writes=[rr])
                lrec = cols[:, CP.sl("lre%d" % l)]
                limc = cols[:, CP.sl("lim%d" % l)]
                lstc = cols[:, CP.sl("lst%d" % l)]
                A_ = lambda out, in_, func, **kw: self.act(out, in_, func, [rr], [rr, rP], **kw)

                def frac_sin2(dst, y, tmp):
                    self.tsc("dve", tmp, y, MAGIC, None, ALU.add, None, [rr], [rr])
                    self.tsc("dve", tmp, tmp, MAGIC, None, ALU.subtract, None, [rr], [rr])
                    self.ttn("dve", tmp, y, tmp, ALU.subtract, [rr], [rr])
                    A_(dst, tmp, AF.Sin, scale=TWO_PI)
                dtc, yc, tq, cL, sL = sm[:, 0, :], sm[:, 1, :], sm[:, 2, :], sm[:, 3, :], sm[:, 4, :]
                A_(dtc, lstc, AF.Exp)
                self.ttn("dve", tq, lrec, dtc, ALU.mult, [rr], [rr])
                A_(rc[:], tq, AF.Exp)
                self.stt(yc, limc, 1.0 / TWO_PI, dtc, ALU.mult, ALU.mult, [rr], [rr])
                ycb = yc.unsqueeze(2).to_broadcast([128, 32, 128])
                self.ttn("dve", Y[:], JT[:], ycb, ALU.mult, [rr], [rr])
                f2 = lambda t: t[:].rearrange("p a j -> p (a j)")
                frac_sin2(f2(SIN), f2(Y), f2(TM))
                self.tsc("dve", f2(Y), f2(Y), 0.25, None, ALU.add, None, [rr], [rr])
                frac_sin2(f2(COS), f2(Y), f2(TM))
                self.cp("act", COS16[:], COS[:], [rr], [rr, rP])
                self.cp("act", SIN16[:], SIN[:], [rr], [rr, rP])
                self.tsc("dve", Y[:], JT[:], 0.5, None, ALU.is_gt, None, [rr], [rr])
                self.ttn("dve", RT[:], Y[:], rc[:].unsqueeze(2).to_broadcast([128, 32, 128]), ALU.mult,
                         [rr], [rr, rP])
                self.tsc("dve", tq, yc, 128.0, None, ALU.mult, None, [rr], [rr])
                frac_sin2(sL, tq, sm[:, 5, :])
                self.tsc("dve", tq, tq, 0.25, None, ALU.add, None, [rr], [rr])
                frac_sin2(cL, tq, sm[:, 5, :])
                self.tsc("dve", sL, sL, cols[:, CP.sl("sgn")], None, ALU.mult, None, [rr], [rr])
                self.ttn("dve", ROT[:], identf[:].unsqueeze(1).to_broadcast([128, 32, 128]),
                         cL.unsqueeze(2).to_broadcast([128, 32, 128]), ALU.mult, [rr], [rr, rP])
                self.ttn("dve", TM[:], P2[:].unsqueeze(1).to_broadcast([128, 32, 128]),
                         sL.unsqueeze(2).to_broadcast([128, 32, 128]), ALU.mult, [rr], [rr])
                self.ttn("dve", ROT[:], ROT[:], TM[:], ALU.add, [rr], [rr, rP])
                c_d = self.W["s5c%d" % l]
                s.dma(Y[0:64].rearrange("p a j -> p (a j)"), c_d[0], writes=[rr])
                s.dma(Y[64:128].rearrange("p a j -> p (a j)"), c_d[1], writes=[rr])
                s.dma(TM[0:64].rearrange("p a j -> p (a j)"), c_d[1], writes=[rr])
                s.dma(TM[64:128].rearrange("p a j -> p (a j)"), c_d[0], writes=[rr])
                self.cp("dve", CW[0:64], Y[0:64], [rr], [rP])
                self.tsc("dve", CW[64:128], Y[64:128], -1.0, None, ALU.mult, None, [rr], [rP])
                self.tsc("dve", CV[:], TM[:], -1.0, None, ALU.mult, None, [rr], [rP])
                s.barrier()
            flag = cols[:, CP.sl("flag")]
            with ExitStack() as c1:
                aT = self.sbring(c1, 3, [128, 2, 128], BF16, "aT")
                t1r = self.sbring(c1, 2, [128, 4, 128], BF16, "s5t1")
                t2r = self.sbring(c1, 2, [128, 4, 128], BF16, "s5t2")
                Xr = self.sbring(c1, 2, [128, 4, 128], F32, "s5X")
                Zr = self.sbring(c1, 3, [128, 4, 128], F32, "s5Z")
                Zbr = self.sbring(c1, 3, [128, 4, 128], BF16, "s5Zb")
                A16 = self.sbring(c1, 2, [128, 4, 128], BF16, "s5A16")
                B16 = self.sbring(c1, 2, [128, 4, 128], BF16, "s5B16")
                Wr = self.sbring(c1, 6, [128, 4, 128], BF16, "s5W")
                Vr = self.sbring(c1, 6, [128, 4, 128], BF16, "s5V")
                INIT = self.sb(c1, [128, 16], F32, "s5init")
                rI = [Res() for _ in range(4)]
                ysb = self.sbring(c1, 3, [128, 128], F32, "s5y")
                yld = self.sbring(c1, 3, [128, 128], F32, "s5yl")
                pA = self.psring(c1, 2, [128, 4, 128], F32, "pA")
                pB = self.psring(c1, 2, [128, 4, 128], F32, "pB")
                pI = self.psring(c1, 2, [128, 4], F32, "pI")
                pY = self.psring(c1, 2, [128, 128], F32, "pY")
                mid = NT // 2
                steps = []
                orders = {0: list(range(NT)), 1: list(range(NT - 1, -1, -1))}
                for d in range(2):
                    for ci, c in enumerate(orders[d]):
                        for half in range(2):
                            for quad in range(2):
                                steps.append((d, ci, c, half, quad))
                nst = len(steps)
                aTs = {}

                def load_a(d, ci):
                    if ci >= NT or (d, ci) in aTs:
                        return
                    c = orders[d][ci]
                    a, ra = aT.next()
                    s.dma(a[:], self.PT_d[0:2, :, c * 128:(c + 1) * 128].rearrange("c p t -> p c t"), writes=[ra])
                    aTs[(d, ci)] = (a, ra)
                AB = {}
                ST = {}

                def emit_ab(k):
                    d, ci, c, half, quad = steps[k]
                    load_a(d, ci)
                    if half == 0 and quad == 0:
                        load_a(d, ci + 1)
                    a, ra = aTs[(d, ci)]
                    dg0 = d * 16 + half * 8 + quad * 4
                    pa, rpa = pA.next()
                    pb, rpb = pB.next()
                    for i in range(4):
                        self.mm(pa[:, i, :], WA[:, dg0 + i, :], a[:, half, :], True, True, [rP, ra], [rpa], sig=(i == 3))
                    for i in range(4):
                        self.mm(pb[:, i, :], WB[:, dg0 + i, :], a[:, half, :], True, True, [rP, ra], [rpb], sig=(i == 3))
                    a16, ra16 = A16.next()
                    b16, rb16 = B16.next()
                    self.cp("act", a16[:], pa[:], [rpa], [ra16])
                    self.cp("act", b16[:], pb[:], [rpb], [rb16])
                    AB[k] = (a16, ra16, b16, rb16)

                def emit_scan(k):
                    d, ci, c, half, quad = steps[k]
                    jf = 0 if d == 0 else 127
                    qi = half * 2 + quad
                    g0 = half * 8 + quad * 4
                    dg0 = d * 16 + g0
                    a16, ra16, b16, rb16 = AB.pop(k)
                    if ci == 0 and half == 0 and quad == 0:
                        s.op("dve", lambda e: e.memset(INIT[:], 0.0), [], rI)
                    x1, rx1 = t1r.next()
                    x2, rx2 = t2r.next()
                    X, rX = Xr.next()
                    self.ttn("dve", x1[:], a16[:], COS16[:, dg0:dg0 + 4, :], ALU.mult, [ra16, rP], [rx1])
                    self.ttn("dve", x2[:], b16[:], SIN16[:, dg0:dg0 + 4, :], ALU.mult, [rb16, rP], [rx2])
                    self.ttn("dve", X[:], x1[:], x2[:], ALU.add, [rx1, rx2], [rX])
                    self.ttn("dve", X[:, :, jf], X[:, :, jf], INIT[:, g0:g0 + 4], ALU.add, [rX, rI[qi]], [rX])
                    Z, rZ = Zr.next()
                    z2 = Z[:].rearrange("p g j -> p (g j)")
                    x2d = X[:].rearrange("p g j -> p (g j)")
                    r2d = RT[:, dg0:dg0 + 4, :].rearrange("p g j -> p (g j)")
                    if d == 1:
                        z2, x2d, r2d = z2[:, ::-1], x2d[:, ::-1], r2d[:, ::-1]
                    s.op("dve", lambda e, z2=z2, x2d=x2d, r2d=r2d: e.tensor_tensor_scan(
                        out=z2, data0=r2d, data1=x2d, initial=0.0, op0=ALU.mult, op1=ALU.add), [rX, rP], [rZ])
                    ST[k] = dict(Z=Z, rZ=rZ)

                def emit_rot(k):
                    d, ci, c, half, quad = steps[k]
                    jl = 127 if d == 0 else 0
                    dg0 = d * 16 + half * 8 + quad * 4
                    Z, rZ = ST[k]["Z"], ST[k]["rZ"]
                    pi_, rpi = pI.next()
                    for i in range(4):
                        self.mm(pi_[:, i:i + 1], ROT[:, dg0 + i, :], Z[:, i, jl:jl + 1], True, True, [rP, rZ], [rpi],
                                sig=(i == 3))
                    ST[k]["pi"] = (pi_, rpi)
                    Wt, rWt = Wr.next()
                    Vt, rVt = Vr.next()
                    Zb, rZb = Zbr.next()
                    self.cp("act", Zb[:], Z[:], [rZ], [rZb])
                    ST[k]["wv"] = (Wt, rWt, Vt, rVt, dg0)
                    ST[k]["zb"] = (Zb, rZb)

                def emit_wv(k):
                    Wt, rWt, Vt, rVt, dg0 = ST[k]["wv"]
                    Zb, rZb = ST[k]["zb"]
                    self.ttn("dve", Wt[:], Zb[:], COS16[:, dg0:dg0 + 4, :], ALU.mult, [rZb, rP], [rWt])
                    self.ttn("dve", Vt[:], Zb[:], SIN16[:, dg0:dg0 + 4, :], ALU.mult, [rZb, rP], [rVt])

                def emit_evac(k):
                    d, ci, c, half, quad = steps[k]
                    qi = half * 2 + quad
                    g0 = half * 8 + quad * 4
                    dg0 = d * 16 + g0
                    pi_, rpi = ST[k]["pi"]
                    if ci == NT - 1:
                        return
                    self.ttn("dve", INIT[:, g0:g0 + 4], pi_[:, 0:4], rc[:, dg0:dg0 + 4], ALU.mult, [rpi, rP], [rI[qi]])
                    nxt = orders[d][ci + 1] if ci + 1 < NT else None
                    if nxt is not None and ((c < mid) != (nxt < mid)):
                        self.tsc("dve", INIT[:, g0:g0 + 4], INIT[:, g0:g0 + 4], flag, None, ALU.mult, None,
                                 [rI[qi]], [rI[qi]])

                def emit_c(k):
                    d, ci, c, half, quad = steps[k]
                    py, rpy = pY.next()
                    n = 0
                    for kk in (k - 1, k):
                        Wt, rWt, Vt, rVt, dg0 = ST[kk]["wv"]
                        for i in range(4):
                            self.mm(py[:], CW[:, dg0 + i, :], Wt[:, i, :], n == 0, False, [rP, rWt], [rpy], sig=False)
                            n += 1
                            self.mm(py[:], CV[:, dg0 + i, :], Vt[:, i, :], False, n == 15, [rP, rVt], [rpy],
                                    sig=(n == 15))
                            n += 1
                    ys, rys = ysb.next()
                    ydst = self.Y_d[half, :, c * 128:(c + 1) * 128]
                    if d == 0:
                        self.cp("act", ys[:], py[:], [rpy], [rys])
                    else:
                        yl, ryl = yld.next()
                        s.dma(yl[:], ydst, reads=[self.rYd], writes=[ryl])
                        self.ttn("dve", ys[:], py[:], yl[:], ALU.add, [rpy, ryl], [rys])
                    s.dma(ydst, ys[:], reads=[rys], writes=[self.rYd], q="pool")
                    del ST[k - 1]
                    del ST[k]
                emit_ab(0)
                for k in range(nst):
                    emit_scan(k)
                    if k >= 1:
                        emit_wv(k - 1)
                        emit_evac(k - 1)
                    if k + 1 < nst:
                        emit_ab(k + 1)
                    if k >= 3 and steps[k - 3][4] == 1:
                        emit_c(k - 3)
                    emit_rot(k)
                emit_wv(nst - 1)
                emit_evac(nst - 1)
                if steps[nst - 3][4] == 1:
                    emit_c(nst - 3)
                if steps[nst - 2][4] == 1:
                    emit_c(nst - 2)
                emit_c(nst - 1)
                s.barrier()
            with ExitStack() as c1:
                Wg = self.sb(c1, [128, 2, 256], BF16, "Wglu")
                rW = Res()
                wgl = W["wglu%d" % l]
                st = self.sbring(c1, 2, [128, 256], F32, "stg")
                self.load_cast(c1, lambda i: Wg[:, i, :], lambda i: wgl[i * 128:(i + 1) * 128, :], 2, None, None,
                               rW, st)
                ssd = cols[:, CP.sl("ssd%d" % l)]
                yt = self.sbring(c1, 2, [128, 2, 512], F32, "ey")
                at = self.sbring(c1, 2, [128, 2, 512], BF16, "ea")
                tt_ = self.sbring(c1, 2, [128, 2, 512], F32, "et")
                gT = self.sbring(c1, 2, [128, 2, 512], BF16, "eg")
                sgm = self.sbring(c1, 2, [128, 512], BF16, "esg")
                bo = self.sbring(c1, 2, [128, 2, 512], BF16, "ebo")
                pZ = self.psring(c1, 2, [128, 512], F32, "pZ")
                K1 = 2.0 * (2.0 / np.pi) ** 0.5
                for st_i in range(T // 512):
                    t0 = st_i * 512
                    y, ry = yt.next()
                    a, ra = at.next()
                    s.dma(y[:], self.Y_d[:, :, t0:t0 + 512].rearrange("c p t -> p c t"), reads=[self.rYd], writes=[ry])
                    s.dma(a[:], self.PT_d[0:2, :, t0:t0 + 512].rearrange("c p t -> p c t"), writes=[ra])
                    for hf in range(2):
                        self.stt(y[:, hf, :], a[:, hf, :], ssd[:, hf:hf + 1], y[:, hf, :], ALU.mult, ALU.add,
                                 [ry, ra], [ry])
                    t, rt = tt_.next()
                    self.ttn("dve", t[:], y[:], y[:], ALU.mult, [ry], [rt])
                    self.tsc("dve", t[:], t[:], 0.044715, 1.0, ALU.mult, ALU.add, [rt], [rt])
                    self.ttn("dve", t[:], t[:], y[:], ALU.mult, [rt, ry], [rt])
                    self.act(t[:], t[:], AF.Sigmoid, [rt], [rt], scale=K1)
                    g, rg = gT.next()
                    self.ttn("dve", g[:], t[:], y[:], ALU.mult, [rt, ry], [rg])
                    b_, rb = bo.next()
                    for oc in range(2):
                        pz, rpz = pZ.next()
                        for kc in range(2):
                            self.mm(pz[:], Wg[:, kc, oc * 128:(oc + 1) * 128], g[:, kc, :], kc == 0, kc == 1,
                                    [rW, rg], [rpz], sig=(kc == 1))
                        sg, rsg = sgm.next()
                        self.act(sg[:], pz[:], AF.Sigmoid, [rpz], [rsg])
                        self.ttn("dve", b_[:, oc, :], g[:, oc, :], sg[:], ALU.mult, [rg, rsg], [rb])
                    s.dma(self.BR_d[0, :, :, t0:t0 + 512].rearrange("c p t -> p c t"), b_[:], reads=[rb], q="pool")
                s.barrier()

    def merge_phase(self, h_in, h_out, l):
        s, T = self.s, self.T
        cols, W = self.cols, self.W
        with ExitStack() as ctx:
            Wg = self.sb(ctx, [128, 8, 5120], BF16, "Wgate")
            Pb = self.sb(ctx, [128, 10, D], BF16, "Pbr")
            Wo = self.sb(ctx, [128, 8, D], BF16, "Wout")
            rW = Res()
            with ExitStack() as c2:
                st = self.sbring(c2, 3, [128, 2560], F32, "stg")
                gm = cols[:, CP.sl("gm%d" % l)]
                win = W["w_in%d" % l]
                self.load_cast(c2, lambda i: Wg[:, i // 2, (i % 2) * 2560:(i % 2) * 2560 + 2560],
                               lambda i: win[(i // 2) * 128:(i // 2) * 128 + 128,
                                             O_GATE + (i % 2) * 2560:O_GATE + (i % 2) * 2560 + 2560],
                               16, None, lambda i: gm[:, i // 2:i // 2 + 1], rW, st)
                wbr = W["wbr%d" % l]
                self.load_cast(c2, lambda i: Pb[:, i, :], lambda i: wbr[i // 2, (i % 2) * 128:(i % 2) * 128 + 128, :],
                               10, None, None, rW, st)
                wo = W["wout%d" % l]
                self.load_cast(c2, lambda i: Wo[:, i, :], lambda i: wo[i * 128:(i + 1) * 128, :], 8, None, None,
                               rW, st)
                s.barrier()
            uT = self.sbring(ctx, 2, [128, 8, 512], BF16, "muT")
            br = self.sbring(ctx, 2, [128, 10, 512], BF16, "mbr")
            mT = self.sb(ctx, [128, 8, 512], BF16, "mT")
            rmT = Res()
            sgr = self.sbring(ctx, 2, [128, 512], F32, "msg")
            tmr = self.sbring(ctx, 2, [128, 512], F32, "mtm")
            acr = self.sbring(ctx, 2, [128, 512], F32, "mac")
            xr = self.sbring(ctx, 2, [128, D], F32, "mxr")
            pg = self.psring(ctx, 2, [128, 512], F32, "pg")
            pp = self.psring(ctx, 2, [128, 512], F32, "pp")
            pd = self.psring(ctx, 2, [128, 512], F32, "pd")
            ins_ = {}

            def load_in(si):
                if si >= T // 512:
                    return
                t0 = si * 512
                u, ru = uT.next()
                s.dma(u[:], self.uT_d[:, :, t0:t0 + 512].rearrange("k p t -> p k t"), writes=[ru])
                b_, rb = br.next()
                for bi in range(5):
                    s.dma(b_[:, bi * 2:bi * 2 + 2, :], self.BR_d[bi, :, :, t0:t0 + 512].rearrange("c p t -> p c t"),
                          writes=[rb])
                ins_[si] = (u, ru, b_, rb)
            load_in(0)
            for st_i in range(T // 512):
                t0 = st_i * 512
                u, ru, b_, rb = ins_.pop(st_i)
                for oc in range(8):
                    if oc == 2:
                        load_in(st_i + 1)
                    ac, rac = acr.next()
                    for bi in range(5):
                        g, rg = pg.next()
                        for kc in range(8):
                            self.mm(g[:], Wg[:, kc, bi * 1024 + oc * 128:bi * 1024 + oc * 128 + 128], u[:, kc, :],
                                    kc == 0, kc == 7, [rW, ru], [rg], sig=(kc == 7))
                        p_, rp = pp.next()
                        for hf in range(2):
                            self.mm(p_[:], Pb[:, bi * 2 + hf, oc * 128:(oc + 1) * 128], b_[:, bi * 2 + hf, :],
                                    hf == 0, hf == 1, [rW, rb], [rp], sig=(hf == 1))
                        sg, rsg = sgr.next()
                        self.act(sg[:], g[:], AF.Sigmoid, [rg], [rsg])
                        if bi == 0:
                            self.ttn("dve", ac[:], sg[:], p_[:], ALU.mult, [rsg, rp], [rac])
                        else:
                            tm, rtm = tmr.next()
                            self.ttn("dve", tm[:], sg[:], p_[:], ALU.mult, [rsg, rp], [rtm])
                            if bi < 4:
                                self.ttn("pool", ac[:], ac[:], tm[:], ALU.add, [rac, rtm], [rac])
                            else:
                                self.ttn("pool", mT[:, oc, :], ac[:], tm[:], ALU.add, [rac, rtm], [rmT])
                for tt in range(4):
                    tk = t0 + tt * 128
                    xo, rxo = xr.next()
                    s.dma(xo[:], h_in[tk:tk + 128, :], writes=[rxo])
                    for half in range(2):
                        d, rd = pd.next()
                        for oc in range(8):
                            self.mm(d[:], mT[:, oc, tt * 128:(tt + 1) * 128], Wo[:, oc, half * 512:(half + 1) * 512],
                                    oc == 0, oc == 7, [rmT, rW], [rd], sig=(oc == 7))
                        self.ttn("dve", xo[:, half * 512:(half + 1) * 512], d[:], xo[:, half * 512:(half + 1) * 512],
                                 ALU.add, [rd, rxo], [rxo])
                    s.dma(h_out[tk:tk + 128, :], xo[:], reads=[rxo], q="pool")
            s.barrier()


def build_program(T, dbg=False, stages=None, na_slots=10):
    b = Builder(T, dbg)
    nc, s, es = b.nc, b.s, b.es
    NT = T // 128
    x = b.din("x", [T, D])
    y = b.dout("y", [T, D])
    b.memd = b.din("mem", [2, N_MEM, D])
    b.tokd = b.din("tok", [T, TOKW])
    colsd = b.din("cols", [128, CP.n])
    fgain = b.din("fgain", [128, D])
    identd = b.din("ident", [128, 128])
    W = {}
    W["p2"] = b.din("p2", [128, 128])
    W["s5const"] = b.din("s5const", [2, 128, 4096])
    nsl = len(b.na_specials()) + 1
    for l in range(DEPTH):
        for f in (1, 2):
            W["wg%d%d" % (l, f)] = b.din("wg%d%d" % (l, f), [D, DFF])
            W["wu%d%d" % (l, f)] = b.din("wu%d%d" % (l, f), [D, DFF])
            W["wd%d%d" % (l, f)] = b.din("wd%d%d" % (l, f), [DFF, D])
        W["w_in%d" % l] = b.din("w_in%d" % l, [D, 7264])
        W["wq%d" % l] = b.din("wq%d" % l, [192, 384])
        W["wkv%d" % l] = b.din("wkv%d" % l, [128, 512])
        W["wmem%d" % l] = b.din("wmem%d" % l, [D, 512])
        W["wglu%d" % l] = b.din("wglu%d" % l, [256, 256])
        W["wbr%d" % l] = b.din("wbr%d" % l, [5, 256, D])
        W["wout%d" % l] = b.din("wout%d" % l, [D, D])
        W["s5row%d" % l] = b.din("s5row%d" % l, [5, 128, 2048])
        W["s5c%d" % l] = b.din("s5c%d" % l, [2, 64, 4096])
        W["swa_tab%d" % l] = b.din("swa_tab%d" % l, [5, 128, 4, 384])
        W["na_tab%d" % l] = b.din("na_tab%d" % l, [nsl, 128, 4, 768])
    b.W = W
    hA = b.dscr("hA", [T, D])
    hB = b.dscr("hB", [T, D])
    b.uT_d = b.dscr("uT_d", [8, 128, T], BF16)
    b.PT_d = b.dscr("PT_d", [NFM, 128, T], BF16)
    b.VT_d = b.dscr("VT_d", [T, 384], BF16)
    b.QM_d = b.dscr("QM_d", [4, 99, T], BF16)
    b.KM_d = b.dscr("KM_d", [4, 99, T], BF16)
    b.VM_d = b.dscr("VM_d", [T, 260], BF16)
    b.BR_d = b.dscr("BR_d", [5, 2, 128, T], BF16)
    b.Y_d = b.dscr("Y_d", [2, 128, T])
    b.NRM_d = b.dscr("NRM_d", [8, 512])
    b.rNRM = [Res() for _ in range(8)]
    b.rYd = Res()
    b.cols = b.sb(es, [128, CP.n], F32, "cols")
    fgain_sb = b.sb(es, [128, D], F32, "fgain")
    b.identf = b.sb(es, [128, 128], F32, "identf")
    b.ident = b.sb(es, [128, 128], BF16, "ident")
    b.kmax2 = b.sb(es, [128, 4], F32, "kmax2")
    b.onesf = b.sb(es, [128, 64], F32, "onesf")
    s.op("dve", lambda e: e.memset(b.onesf[:], 1.0), [], [Res()])
    b.rkm = Res()
    rc = Res()
    s.dma(b.cols[:], colsd[:, :], writes=[rc])
    s.dma(fgain_sb[:], fgain[:, :], writes=[rc])
    s.dma(b.identf[:], identd[:, :], writes=[rc])
    s.op("dve", lambda e: e.tensor_copy(out=b.ident[:], in_=b.identf[:]), reads=[rc], writes=[rc])
    s.barrier()
    cur = x
    bufs = [hA, hB]
    bi = 0
    if stages is None:
        stages = ("ffn1", "proj", "s5", "swa", "na", "mla", "mem", "merge", "ffn2")
    for l in range(DEPTH):
        if "ffn1" in stages:
            dst = bufs[bi]
            bi ^= 1
            b.ffn_phase(cur, dst, W["wg%d1" % l], W["wu%d1" % l], W["wd%d1" % l], b.cols[:, CP.sl("g1%d" % l)])
            cur = dst
        if "proj" in stages:
            b.proj_phase(cur, l)
        if "s5" in stages:
            b.s5_phase(l)
        if "swa" in stages:
            b.swa_phase(l)
        if "na" in stages:
            b.na_phase(l)
        if "mla" in stages:
            b.mla_phase(l)
        if "mem" in stages:
            b.mem_phase(l)
        if "merge" in stages:
            dst = bufs[bi]
            bi ^= 1
            b.merge_phase(cur, dst, l)
            cur = dst
        if "ffn2" in stages:
            dst = bufs[bi]
            bi ^= 1
            b.ffn_phase(cur, dst, W["wg%d2" % l], W["wu%d2" % l], W["wd%d2" % l], b.cols[:, CP.sl("g2%d" % l)])
            cur = dst
        if dbg and l == 0 and stages is not None and "stop1" in stages:
            break
    b.final_norm(cur, y, fgain_sb)
    s.emit()
    return nc


def _t5_bucket_np(rel):
    nb = 16
    max_exact = 8
    ret = (rel > 0).astype(np.int32) * nb
    n = np.abs(rel)
    nf = np.maximum(n, 1).astype(np.float32)
    large = max_exact + (np.log(nf / np.float32(max_exact)) / np.float32(np.log(128 / max_exact))
                         * np.float32(nb - max_exact)).astype(np.int32)
    large = np.minimum(large, nb - 1)
    return ret + np.where(n < max_exact, n, large)


def _na_keytiles(NT, j):
    mid = NT // 2
    base = min(max(j - 2, 0), NT - 6)
    if j == mid - 1:
        base = min(max(mid - 4, 0), NT - 6)
    return list(range(base, base + 6))


def _na_specials(NT):
    mid = NT // 2
    return sorted(set(x for x in (0, 1, mid - 2, mid - 1, mid, mid + 1, NT - 3, NT - 2, NT - 1) if 0 <= x < NT))


def host_tables(T, nseg):
    S = T // nseg
    pos = (np.arange(T) % S).astype(np.float32)
    inv = (np.float32(10000.0) ** (-np.arange(16, dtype=np.float32) / np.float32(16))).astype(np.float32)
    ang = (pos[:, None] * inv[None, :]).astype(np.float32)
    tok = np.zeros((T, TOKW), np.float32)
    tok[:, 0:16] = np.cos(ang)
    tok[:, 16:32] = np.sin(ang)
    seg = np.arange(T) // S
    if nseg == 2:
        tok[:, 32] = np.where(seg == 0, 0.0, NEGV)
        tok[:, 33] = np.where(seg == 1, 0.0, NEGV)
    tok[:, 34] = (np.arange(T) < T // 2).astype(np.float32)
    tok[:, 35] = (np.arange(T) >= T // 2).astype(np.float32)
    return tok


def swa_tables(T, nseg, t5_bias):
    NT = T // 128
    S = T // nseg
    mid = NT // 2
    specials = [0, mid - 1, mid, NT - 1]
    rep = [j for j in range(NT) if j not in specials]
    js = [rep[0] if rep else 0] + specials
    out = np.full((5, 128, 4, 384), NEGV, np.float32)
    q = np.arange(128)
    for si, j in enumerate(js):
        qpos = j * 128 + q
        for sidx in range(3):
            it = j - 1 + sidx
            if it < 0 or it >= NT:
                continue
            kpos = it * 128 + np.arange(128)
            rel = kpos[None, :] - qpos[:, None]
            valid = (np.abs(rel) <= SWA_WIN) & ((kpos[None, :] // S) == (qpos[:, None] // S))
            bias = t5_bias[_t5_bucket_np(rel)]
            blk = np.where(valid[:, :, None], bias, NEGV).astype(np.float32)
            out[si, :, :, sidx * 128:(sidx + 1) * 128] = blk.transpose(0, 2, 1)
    return out


def na_tables(T, nseg, rpb):
    NT = T // 128
    S = T // nseg
    rows = S // GRID_W
    kh = min(NA_KH, rows)
    sp = _na_specials(NT)
    rep = [j for j in range(NT) if j not in sp]
    js = [rep[0] if rep else None] + sp
    out = np.full((len(js), 128, 4, 768), NEGV, np.float32)

    def tab(j):
        t = np.full((128, 4, 768), NEGV, np.float32)
        qtok = j * 128 + np.arange(128)
        qseg, qr, qc = qtok // S, (qtok % S) // GRID_W, qtok % GRID_W
        rs = np.clip(qr - kh // 2, 0, rows - kh)
        cs = np.clip(qc - NA_KW // 2, 0, GRID_W - NA_KW)
        cnt = np.zeros(128, np.int64)
        for i, kt in enumerate(_na_keytiles(NT, j)):
            ktok = kt * 128 + np.arange(128)
            kseg, kr, kc = ktok // S, (ktok % S) // GRID_W, ktok % GRID_W
            valid = ((kseg[None, :] == qseg[:, None]) & (kr[None, :] >= rs[:, None]) & (kr[None, :] < rs[:, None] + kh)
                     & (kc[None, :] >= cs[:, None]) & (kc[None, :] < cs[:, None] + NA_KW))
            dr = np.clip(kr[None, :] - qr[:, None] + (NA_KH - 1), 0, 2 * NA_KH - 2)
            dc = np.clip(kc[None, :] - qc[:, None] + (NA_KW - 1), 0, 2 * NA_KW - 2)
            bias = rpb[:, dr, dc]
            t[:, :, i * 128:(i + 1) * 128] = np.where(valid[None], bias, NEGV).transpose(1, 0, 2)
            cnt += valid.sum(1)
        assert (cnt == kh * NA_KW).all(), ("NA coverage", j, cnt.min(), cnt.max())
        return t
    for si, j in enumerate(js):
        if j is not None:
            out[si] = tab(j)
    if rep:
        for j in rep[1:]:
            if j in (rep[len(rep) // 2], rep[-1]):
                assert np.array_equal(tab(j), out[0]), ("NA interior mismatch", j)
    return out


def s5_layouts(w, l):
    lam_re, lam_im, log_step = w["ssm_lam_re"][l], w["ssm_lam_im"][l], w["ssm_log_step"][l]
    b_re, b_im, c_re, c_im = w["ssm_b_re"][l], w["ssm_b_im"][l], w["ssm_c_re"][l], w["ssm_c_im"][l]
    row = np.zeros((5, 128, 2048), np.float32)
    row[0] = np.broadcast_to(lam_re.reshape(1, 2048), (128, 2048))
    row[1] = np.broadcast_to(lam_im.reshape(1, 2048), (128, 2048))
    row[2] = np.broadcast_to(np.repeat(log_step.reshape(32), 64).reshape(1, 2048), (128, 2048))
    bp = np.zeros((2, 128, 2, 16, 64), np.float32)
    cc = np.zeros((2, 64, 2, 16, 128), np.float32)
    for g in range(16):
        k0 = (g % 8) * 16
        bp[0, k0:k0 + 16, :, g, :] = b_re[:, g].transpose(2, 0, 1)
        bp[1, k0:k0 + 16, :, g, :] = b_im[:, g].transpose(2, 0, 1)
        cc[0, :, :, g, k0:k0 + 16] = c_re[:, g].transpose(2, 0, 1)
        cc[1, :, :, g, k0:k0 + 16] = c_im[:, g].transpose(2, 0, 1)
    row[3] = bp[0].reshape(128, 2048)
    row[4] = bp[1].reshape(128, 2048)
    s5c = cc.reshape(2, 64, 4096)
    n_of_p = np.arange(128) % 64
    lre_c = lam_re.reshape(32, 64)[:, n_of_p].T.copy()
    lim_c = lam_im.reshape(32, 64)[:, n_of_p].T.copy()
    lst_c = np.broadcast_to(log_step.reshape(1, 32), (128, 32)).copy()
    return row, s5c, lre_c, lim_c, lst_c


def s5_consts():
    c = np.zeros((2, 128, 32, 128), np.float32)
    j = np.arange(128, dtype=np.float32)
    c[0, :, 0:16, :] = j
    c[0, :, 16:32, :] = 127.0 - j
    c[1] = 1.0
    c[1, :, 0:16, 0] = 0.0
    c[1, :, 16:32, 127] = 0.0
    p2 = np.zeros((128, 128), np.float32)
    p2[np.arange(128), (np.arange(128) + 64) % 128] = 1.0
    return c.reshape(2, 128, 4096), p2


def colgain(g):
    return np.ascontiguousarray(g.reshape(-1, 128).T)


def make_shared_inputs(w):
    sh = {}
    for l in range(DEPTH):
        ffw = {1: (w["ffn1_w_gate"], w["ffn1_w_up"], w["ffn1_w_down"]),
               2: (w["ffn2_w_gate"], w["ffn2_w_up"], w["ffn2_w_down"])}
        for f in (1, 2):
            sh["wg%d%d" % (l, f)] = np.ascontiguousarray(ffw[f][0][l])
            sh["wu%d%d" % (l, f)] = np.ascontiguousarray(ffw[f][1][l])
            sh["wd%d%d" % (l, f)] = np.ascontiguousarray(ffw[f][2][l])
        sh["w_in%d" % l] = np.ascontiguousarray(w["w_in"][l])
        sh["wq%d" % l] = np.ascontiguousarray(w["mla_w_q_up"][l])
        sh["wkv%d" % l] = np.ascontiguousarray(w["mla_w_kv_up"][l])
        sh["wmem%d" % l] = np.ascontiguousarray(w["mem_w_kv"][l])
        sh["wglu%d" % l] = np.ascontiguousarray(w["ssm_w_glu"][l])
        sh["wbr%d" % l] = np.ascontiguousarray(w["w_branch"][l])
        sh["wout%d" % l] = np.ascontiguousarray(w["w_out"][l])
    c, p2 = s5_consts()
    sh["s5const"] = c
    sh["p2"] = p2
    sh["ident"] = np.eye(128, dtype=np.float32)
    sh["fgain"] = np.ascontiguousarray(np.broadcast_to(w["final_norm"].reshape(1, D), (128, D)))
    return sh


def make_cols(w, nseg, s5cols):
    cols = np.zeros((128, CP.n), np.float32)
    for l in range(DEPTH):
        cols[:, CP.sl("g1%d" % l)] = colgain(w["ffn1_norm"][l])
        cols[:, CP.sl("g2%d" % l)] = colgain(w["ffn2_norm"][l])
        cols[:, CP.sl("gm%d" % l)] = colgain(w["mix_norm"][l])
        cols[:, CP.sl("gmem%d" % l)] = colgain(w["mem_norm"][l])
        qn = np.zeros((128, 2), np.float32)
        qn[:, 0] = w["mla_q_norm"][l][0:128]
        qn[0:64, 1] = w["mla_q_norm"][l][128:192]
        cols[:, CP.sl("qn%d" % l)] = qn
        cols[:, CP.sl("kvn%d" % l)] = w["mla_kv_norm"][l].reshape(128, 1)
        cols[:, CP.sl("ssd%d" % l)] = w["ssm_d"][l].reshape(2, 128).T
        cols[:, CP.sl("sink%d" % l)] = np.broadcast_to(w["swa_sink"][l].reshape(1, 4), (128, 4))
        lre_c, lim_c, lst_c = s5cols[l]
        cols[:, CP.sl("lre%d" % l)] = lre_c
        cols[:, CP.sl("lim%d" % l)] = lim_c
        cols[:, CP.sl("lst%d" % l)] = lst_c
    cols[:, CP.sl("flag")] = 1.0 if nseg == 1 else 0.0
    cols[0:64, CP.sl("sgn")] = 1.0
    cols[64:128, CP.sl("sgn")] = -1.0
    return cols


def make_type_inputs(T, nseg, w, sh):
    d = dict(sh)
    d["tok"] = host_tables(T, nseg)
    s5cols = []
    for l in range(DEPTH):
        row, s5c, lre_c, lim_c, lst_c = s5_layouts(w, l)
        d["s5row%d" % l] = row
        d["s5c%d" % l] = s5c
        s5cols.append((lre_c, lim_c, lst_c))
        d["swa_tab%d" % l] = swa_tables(T, nseg, w["t5_bias"])
        d["na_tab%d" % l] = na_tables(T, nseg, w["na_rpb"][l])
    d["cols"] = make_cols(w, nseg, s5cols)
    return d


_PROG_CACHE = {}


def kernel(**inputs):
    w = {k: np.asarray(v, dtype=np.float32) for k, v in inputs.items()}
    xp, xs = w["x_prompt"], w["x_sample"]
    mp, ms = w["mem_prompt"], w["mem_sample"]
    B, S, _ = xp.shape
    T = S
    sh = make_shared_inputs(w)
    tp = make_type_inputs(T, 1, w, sh)
    ts = make_type_inputs(T, 2, w, sh)
    ACTIVE = [0, 1, 4, 5]
    big = [k for k in tp if k.startswith(("wg", "wu", "wd", "w_in", "wq", "wkv", "wmem", "wglu", "wbr", "wout"))]
    idle = dict(tp)
    for k in big:
        idle[k] = np.zeros_like(tp[k])
    idle["x"] = np.zeros((T, D), np.float32)
    idle["mem"] = np.zeros((2, N_MEM, D), np.float32)
    in_maps = []
    for c in range(NCORES):
        if c not in ACTIVE:
            in_maps.append(idle)
            continue
        cc = ACTIVE.index(c)
        if cc < 2:
            m = dict(tp)
            m["x"] = np.ascontiguousarray(xp[cc])
            m["mem"] = np.ascontiguousarray(np.stack([mp[cc], mp[cc]]))
        else:
            i0 = (cc - 2) * 2
            m = dict(ts)
            m["x"] = np.ascontiguousarray(xs[i0:i0 + 2].reshape(T, D))
            m["mem"] = np.ascontiguousarray(ms[i0:i0 + 2])
        in_maps.append(m)
    if T not in _PROG_CACHE:
        _PROG_CACHE[T] = build_program(T)
    nc = _PROG_CACHE[T]
    res = run_bass_kernel_spmd(nc, in_maps, core_ids=list(range(NCORES)))
    outs = [np.asarray(r["y"], dtype=np.float32) for r in res.results]
    y_prompt = np.stack([outs[ACTIVE[0]], outs[ACTIVE[1]]]).reshape(xp.shape)
    y_sample = np.concatenate([outs[ACTIVE[2]].reshape(2, -1, D), outs[ACTIVE[3]].reshape(2, -1, D)],
                              0).reshape(xs.shape)
    return (y_prompt, y_sample)
```

```python
import numpy as np
from contextlib import ExitStack
import concourse.bass as bass
import concourse.mybir as mybir
from concourse.bass_utils import run_bass_kernel_spmd

F32 = mybir.dt.float32
BF16 = mybir.dt.bfloat16
AF = mybir.ActivationFunctionType
ALU = mybir.AluOpType
AX = mybir.AxisListType

D = 1024
DFF = 2816
NFC = DFF // 128
DEPTH = 2
EPS = 1e-6
NCORES = 8


class Res:
    __slots__ = ("name", "w", "r")

    def __init__(self, name=""):
        self.name = name
        self.w = None
        self.r = {}


class Sched:
    ENG = ("pe", "act", "dve", "pool", "sp")

    def __init__(self, nc, es, ndma=12):
        self.nc = nc
        self.es = es
        self.epoch = 0
        self.q = {e: [] for e in self.ENG}
        self.semh = {}
        self.cnt = {}
        self.known = {e: {} for e in self.ENG}
        for e in self.ENG:
            self.semh[e] = es.enter_context(nc.semaphore("s_" + e))
            self.cnt[e] = 0
        self.pending = {e: False for e in self.ENG}
        self.dslots = {}
        self.duse = {}
        self.dnext = {}
        for qn in ("sp", "pool"):
            ks = []
            for i in range(ndma if qn == "sp" else 8):
                k = "d_%s_%d" % (qn, i)
                self.semh[k] = es.enter_context(nc.semaphore(k))
                self.duse[k] = 0
                ks.append(k)
            self.dslots[qn] = ks
            self.dnext[qn] = 0

    def _deps(self, eng, reads, writes):
        deps = {}

        def add(k, v):
            if deps.get(k, 0) < v:
                deps[k] = v
        for r in reads:
            if r.w is not None:
                add(*r.w)
        for w in writes:
            if w.w is not None:
                add(*w.w)
            for k, v in w.r.items():
                add(k, v)
        out = []
        for k, v in deps.items():
            if eng == "pe" and k.split("#")[0] == "pe":
                continue
            if self.known[eng].get(k, 0) >= v:
                continue
            self.known[eng][k] = v
            out.append((k, v))
        return out

    def _reg(self, ev, reads, writes):
        k, v = ev
        for r in reads:
            if r.r.get(k, 0) < v:
                r.r[k] = v
        for w in writes:
            w.w = ev
            w.r = {}

    def op(self, eng, fn, reads=(), writes=(), sig=True):
        waits = self._deps(eng, reads, writes)
        ev = (self.ek(eng), self.cnt[eng] + 1)
        if sig:
            self.cnt[eng] += 1
            self.pending[eng] = False
            inc = (self.ek(eng), 1)
        else:
            self.pending[eng] = True
            inc = None
        self._reg(ev, reads, writes)
        self.q[eng].append((waits, fn, inc))

    def dma(self, out, in_, reads=(), writes=(), q="sp", **kw):
        ks = self.dslots[q]
        k = ks[self.dnext[q]]
        self.dnext[q] = (self.dnext[q] + 1) % len(ks)
        prev = 16 * self.duse[k]
        self.duse[k] += 1
        val = 16 * self.duse[k]
        waits = self._deps(q, reads, writes)
        if prev > 0 and self.known[q].get(k, 0) < prev:
            waits.append((k, prev))
            self.known[q][k] = prev
        self._reg((k, val), reads, writes)
        self.q[q].append((waits, (lambda e: e.dma_start(out=out, in_=in_, **kw)), (k, 16)))

    def coll(self, kind, op, groups, ins, outs, reads=(), writes=()):
        q = "pool"
        ks = self.dslots[q]
        k = ks[self.dnext[q]]
        self.dnext[q] = (self.dnext[q] + 1) % len(ks)
        prev = 16 * self.duse[k]
        self.duse[k] += 1
        val = 16 * self.duse[k]
        waits = self._deps(q, reads, writes)
        if prev > 0 and self.known[q].get(k, 0) < prev:
            waits.append((k, prev))
            self.known[q][k] = prev
        self._reg((k, val), reads, writes)
        self.q[q].append((waits, (lambda e: e.collective_compute(
            kind, op, replica_groups=groups, ins=ins, outs=outs)), (k, 16)))

    def barrier(self):
        for e in self.ENG:
            if self.pending[e]:
                self.op(e, lambda g: g.nop(), sig=True)
        evs = [(self.ek(e), self.cnt[e]) for e in self.ENG if self.cnt[e] > 0]
        evs += [(k, 16 * u) for k, u in self.duse.items() if u > 0]
        for e in self.ENG:
            waits = []
            for k, v in evs:
                if k == self.ek(e):
                    continue
                if self.known[e].get(k, 0) >= v:
                    continue
                self.known[e][k] = v
                waits.append((k, v))
            if waits:
                self.q[e].append((waits, None, None))
        if max(self.cnt.values()) > 12000:
            self.epoch += 1
            for e in self.ENG:
                self.semh[e + "#%d" % self.epoch] = self.es.enter_context(
                    self.nc.semaphore("s_%s_%d" % (e, self.epoch)))
                self.cnt[e] = 0

    def ek(self, e):
        return e if self.epoch == 0 else e + "#%d" % self.epoch

    def emit(self):
        self.barrier()
        nc = self.nc
        me = self

        def replay(name, eng):
            for waits, fn, inc in me.q[name]:
                for k, v in waits:
                    eng.wait_ge(me.semh[k], v)
                if fn is None:
                    continue
                ins = fn(eng)
                if inc is not None:
                    ins.then_inc(me.semh[inc[0]], inc[1])

        with nc.Block() as block:
            @block.tensor
            def _(e):
                replay("pe", e)

            @block.scalar
            def _(e):
                replay("act", e)

            @block.vector
            def _(e):
                replay("dve", e)

            @block.gpsimd
            def _(e):
                replay("pool", e)

            @block.sync
            def _(e):
                replay("sp", e)


class Ring:
    def __init__(self, tiles):
        self.t = tiles
        self.r = [Res() for _ in tiles]
        self.i = 0

    def next(self):
        i = self.i
        self.i = (i + 1) % len(self.t)
        return self.t[i], self.r[i]


N_MEM = 256
HD = 64
GRID_W = 64
SSM_G, SSM_N, SSM_P = 16, 64, 16
NA_KH, NA_KW = 8, 16
SWA_WIN = 128
NEGV = -30000.0
MLA_DQ = 96
MLA_SCALE = MLA_DQ ** -0.5
ATT_SCALE = HD ** -0.5
O_AIN, O_SWQ, O_SWK, O_SWV, O_NAQ, O_NAK, O_NAV, O_CQ, O_CKV, O_KR, O_MQ, O_GATE = (
    0, 256, 512, 640, 768, 1024, 1280, 1536, 1728, 1856, 1888, 2144)
NPROJ = 2144
FM_CHUNKS = [(0, 128), (128, 128),
             (256, 128), (384, 128),
             (512, 128),
             (768, 128), (896, 128),
             (1024, 128), (1152, 128),
             (1888, 128), (2016, 128)]
NFM = len(FM_CHUNKS)
LAT_CHUNKS = [(1536, 128), (1664, 64), (1728, 128)]
TOKW = 36


class ColPack:
    def __init__(self):
        self.off = {}
        self.n = 0

    def add(self, name, w):
        self.off[name] = (self.n, w)
        self.n += w

    def sl(self, name):
        o, w = self.off[name]
        return slice(o, o + w)


def make_colpack():
    cp = ColPack()
    for l in range(DEPTH):
        for nm, w in (("g1", 8), ("g2", 8), ("gm", 8), ("gmem", 8), ("qn", 2), ("kvn", 1),
                      ("ssd", 2), ("sink", 4), ("lre", 32), ("lim", 32), ("lst", 32)):
            cp.add("%s%d" % (nm, l), w)
    cp.add("flag", 1)
    cp.add("sgn", 1)
    return cp


CP = make_colpack()


class Builder:
    def __init__(self, T, dbg=False):
        self.T = T
        self.NT = T // 128
        self.dbg = dbg
        self.nc = bass.Bass("TRN2", target_bir_lowering=False)
        self.es = ExitStack()
        self.s = Sched(self.nc, self.es)
        self.uid = 0

    def name(self, p):
        self.uid += 1
        return "%s_%d" % (p, self.uid)

    def din(self, name, shape, dt=F32):
        return self.nc.dram_tensor(name, list(shape), dt, kind="ExternalInput").ap()

    def dout(self, name, shape, dt=F32):
        return self.nc.dram_tensor(name, list(shape), dt, kind="ExternalOutput").ap()

    def dscr(self, name, shape, dt=F32):
        kind = "ExternalOutput" if self.dbg else "Internal"
        return self.nc.dram_tensor(name, list(shape), dt, kind=kind).ap()

    def sb(self, ctx, shape, dt, p="t"):
        return ctx.enter_context(self.nc.sbuf_tensor(self.name(p), list(shape), dt))

    def ps(self, ctx, shape, dt=F32, p="ps"):
        return ctx.enter_context(self.nc.psum_tensor(self.name(p), list(shape), dt))

    def sbring(self, ctx, n, shape, dt, p="r"):
        return Ring([self.sb(ctx, shape, dt, p) for _ in range(n)])

    def psring(self, ctx, n, shape, dt=F32, p="pr"):
        return Ring([self.ps(ctx, shape, dt, p) for _ in range(n)])

    def mm(self, out, lhsT, rhs, start, stop, reads, writes, sig=True):
        self.s.op("pe", lambda e: e.matmul(out, lhsT=lhsT, rhs=rhs, start=start, stop=stop),
                  reads, writes, sig)

    def tr(self, out, in_, ident, reads, writes, sig=True):
        self.s.op("pe", lambda e: e.transpose(out=out, in_=in_, identity=ident), reads, writes, sig)

    def act(self, out, in_, func, reads, writes, **kw):
        self.s.op("act", lambda e: e.activation(out=out, in_=in_, func=func, **kw), reads, writes)

    def tsc(self, eng, out, in0, s1, s2, op0, op1, reads, writes):
        if op1 is None:
            self.s.op(eng, lambda e: e.tensor_scalar(out=out, in0=in0, scalar1=s1, scalar2=None, op0=op0),
                      reads, writes)
        else:
            self.s.op(eng, lambda e: e.tensor_scalar(out=out, in0=in0, scalar1=s1, scalar2=s2,
                                                     op0=op0, op1=op1), reads, writes)

    def ttn(self, eng, out, in0, in1, op, reads, writes):
        self.s.op(eng, lambda e: e.tensor_tensor(out=out, in0=in0, in1=in1, op=op), reads, writes)

    def stt(self, out, in0, scalar, in1, op0, op1, reads, writes):
        self.s.op("dve", lambda e: e.scalar_tensor_tensor(out=out, in0=in0, scalar=scalar, in1=in1,
                                                          op0=op0, op1=op1), reads, writes)

    def cp(self, eng, out, in_, reads, writes):
        if eng == "act":
            self.s.op("act", lambda e: e.copy(out=out, in_=in_), reads, writes)
        else:
            self.s.op(eng, lambda e: e.tensor_copy(out=out, in_=in_), reads, writes)

    def rstd(self, src, n, rsrc, junk, sv, rsv):
        jk, rj = junk
        self.act(jk, src, AF.Square, [rsrc], [rj, rsv], accum_out=sv[:, 0:1])
        self.tsc("dve", sv[:, 1:2], sv[:, 0:1], 1.0 / n, EPS, ALU.mult, ALU.add, [rsv], [rsv])
        self.act(sv[:, 2:3], sv[:, 1:2], AF.Sqrt, [rsv], [rsv])
        self.s.op("dve", lambda e: e.reciprocal(out=sv[:, 3:4], in_=sv[:, 2:3]), [rsv], [rsv])

    def load_cast(self, ctx2, dst_fn, src_fn, n, shape, gain_fn=None, rW=None, st=None):
        s = self.s
        if st is None:
            st = self.sbring(ctx2, 2, shape, F32, "stg")
        for i in range(n):
            t, r = st.next()
            src = src_fn(i)
            tv = t[0:src.shape[0], 0:src.shape[1]]
            s.dma(tv, src, writes=[r])
            eng = ("dve", "act")[i % 2]
            dst = dst_fn(i)
            g = gain_fn(i) if gain_fn is not None else None
            if g is None:
                self.cp(eng, dst, tv, [r], [rW])
            elif eng == "act":
                self.act(dst, tv, AF.Copy, [r], [rW], scale=g)
            else:
                self.tsc(eng, dst, tv, g, None, ALU.mult, None, [r], [rW])

    def ffn_phase(self, h_in, h_out, wg, wu, wd, gain_col):
        nc, s, T = self.nc, self.s, self.T
        ident = self.ident
        with ExitStack() as ctx:
            WG = self.sb(ctx, [128, 8, DFF], BF16, "WG")
            WU = self.sb(ctx, [128, 8, DFF], BF16, "WU")
            WD = self.sb(ctx, [128, NFC, D], BF16, "WD")
            rW = Res()
            with ExitStack() as c2:
                st = self.sbring(c2, 4, [128, DFF], F32, "stg")
                self.load_cast(c2, lambda i: WG[:, i, :], lambda i: wg[i * 128:(i + 1) * 128, :], 8,
                               None, lambda i: gain_col[:, i:i + 1], rW, st)
                self.load_cast(c2, lambda i: WU[:, i, :], lambda i: wu[i * 128:(i + 1) * 128, :], 8,
                               None, lambda i: gain_col[:, i:i + 1], rW, st)
                self.load_cast(c2, lambda i: WD[:, i, :], lambda i: wd[i * 128:(i + 1) * 128, :], NFC,
                               None, None, rW, st)
                s.barrier()
            xt = self.sbring(ctx, 2, [128, D], F32, "xt")
            xr = self.sbring(ctx, 2, [128, D], F32, "xr")
            xn = self.sbring(ctx, 2, [128, D], BF16, "xn")
            st4 = self.sbring(ctx, 3, [128, 4], F32, "st4")
            xTr = self.sbring(ctx, 2, [128, 8, 512], BF16, "xT")
            hT = self.sb(ctx, [128, NFC, 512], BF16, "hT")
            rhT = Res()
            sg = self.sbring(ctx, 2, [128, 512], BF16, "sg")
            pT = self.psring(ctx, 1, [128, D], BF16, "pT")
            pG = self.psring(ctx, 2, [128, 512], F32, "pG")
            pU = self.psring(ctx, 2, [128, 512], F32, "pU")
            pD = self.psring(ctx, 2, [128, 512], F32, "pD")
            NS = T // 512
            xTs = {}
            pend = {}

            def norm_chain(si, tt):
                if si >= NS:
                    return
                if tt == 0:
                    xTs[si] = xTr.next()
                t0 = si * 512 + tt * 128
                x, rx = xt.next()
                s.dma(x[:], h_in[t0:t0 + 128, :], writes=[rx])
                xb, rxb = xn.next()
                sv, rs = st4.next()
                self.rstd(x[:], D, rx, (xb[:], rxb), sv, rs)
                self.tsc("dve", xb[:], x[:], sv[:, 3:4], None, ALU.mult, None, [rx, rs], [rxb])
                pend[(si, tt)] = (xb, rxb)

            def transposes(si, tt):
                if si >= NS:
                    return
                xb, rxb = pend.pop((si, tt))
                xT, rxT = xTs[si]
                p, rp = pT.next()
                for kc in range(8):
                    self.tr(p[:, kc * 128:(kc + 1) * 128], xb[:, kc * 128:(kc + 1) * 128], ident[:],
                            [rxb], [rp], sig=(kc == 7))
                self.cp("act", xT[:, :, tt * 128:(tt + 1) * 128],
                        p[:].rearrange("p (k t) -> p k t", k=8), [rp], [rxT])
            norm_chain(0, 0)
            for tt in range(4):
                norm_chain(0, tt + 1) if tt + 1 < 4 else None
                transposes(0, tt)
            for st_i in range(NS):
                xT, rxT = xTs[st_i]
                for fc in range(NFC):
                    if fc in (1, 5, 9, 13):
                        norm_chain(st_i + 1, (fc - 1) // 4)
                    if fc in (4, 8, 12, 16):
                        transposes(st_i + 1, (fc - 4) // 4)
                    g, rg = pG.next()
                    u, ru = pU.next()
                    for (W, pt, rpt) in ((WG, g, rg), (WU, u, ru)):
                        for kc in range(8):
                            self.mm(pt[:], W[:, kc, fc * 128:(fc + 1) * 128], xT[:, kc, :],
                                    kc == 0, kc == 7, [rW, rxT], [rpt], sig=(kc == 7))
                    sgt, rsg = sg.next()
                    self.act(sgt[:], g[:], AF.Silu, [rg], [rsg])
                    self.ttn("dve", hT[:, fc, :], sgt[:], u[:], ALU.mult, [rsg, ru], [rhT])
                xos = []
                for tt in range(4):
                    t0 = st_i * 512 + tt * 128
                    xo, rxo = xr.next()
                    s.dma(xo[:], h_in[t0:t0 + 128, :], writes=[rxo])
                    for half in range(2):
                        d, rd = pD.next()
                        for fc in range(NFC):
                            self.mm(d[:], hT[:, fc, tt * 128:(tt + 1) * 128],
                                    WD[:, fc, half * 512:(half + 1) * 512],
                                    fc == 0, fc == NFC - 1, [rhT, rW], [rd], sig=(fc == NFC - 1))
                        self.stt(xo[:, half * 512:(half + 1) * 512], d[:], 0.5,
                                 xo[:, half * 512:(half + 1) * 512], ALU.mult, ALU.add, [rd, rxo], [rxo])
                    s.dma(h_out[t0:t0 + 128, :], xo[:], reads=[rxo], q="pool")
            s.barrier()

    def final_norm(self, h_in, y_out, gain_bc):
        s, T = self.s, self.T
        with ExitStack() as ctx:
            xt = self.sbring(ctx, 3, [128, D], F32, "fx")
            junk = self.sbring(ctx, 1, [128, D], BF16, "fj")
            st4 = self.sbring(ctx, 3, [128, 4], F32, "fs")
            for ti in range(T // 128):
                t0 = ti * 128
                x, rx = xt.next()
                s.dma(x[:], h_in[t0:t0 + 128, :], writes=[rx])
                jk, rj = junk.next()
                sv, rs = st4.next()
                self.rstd(x[:], D, rx, (jk[:], rj), sv, rs)
                self.stt(x[:], x[:], sv[:, 3:4], gain_bc[:], ALU.mult, ALU.mult, [rx, rs], [rx])
                s.dma(y_out[t0:t0 + 128, :], x[:], reads=[rx], q="pool")
            s.barrier()

    def proj_phase(self, h_in, l):
        s, T = self.s, self.T
        ident, cols, W = self.ident, self.cols, self.W
        with ExitStack() as ctx:
            Win = self.sb(ctx, [128, 8, NPROJ], BF16, "Win")
            Wq = self.sb(ctx, [128, 2, 384], BF16, "Wq")
            Wkv = self.sb(ctx, [128, 512], BF16, "Wkv")
            rW = Res()
            with ExitStack() as c2:
                st = self.sbring(c2, 3, [128, NPROJ], F32, "stg")
                gm = cols[:, CP.sl("gm%d" % l)]
                win = W["w_in%d" % l]
                self.load_cast(c2, lambda i: Win[:, i, :], lambda i: win[i * 128:(i + 1) * 128, 0:NPROJ],
                               8, None, lambda i: gm[:, i:i + 1], rW, st)
                qn = cols[:, CP.sl("qn%d" % l)]
                wq = W["wq%d" % l]
                self.load_cast(c2, lambda i: Wq[0:(128 if i == 0 else 64), i, :],
                               lambda i: wq[i * 128:min(192, (i + 1) * 128), :], 2, None,
                               lambda i: qn[0:(128 if i == 0 else 64), i:i + 1], rW, st)
                for c0 in (O_SWQ, O_NAQ, O_MQ):
                    self.tsc("dve", Win[:, :, c0:c0 + 256], Win[:, :, c0:c0 + 256], ATT_SCALE, None, ALU.mult,
                             None, [rW], [rW])
                kvn = cols[:, CP.sl("kvn%d" % l)]
                wkv = W["wkv%d" % l]
                self.load_cast(c2, lambda i: Wkv[:, :], lambda i: wkv[:, :], 1, None,
                               lambda i: kvn[:, 0:1], rW, st)
                s.barrier()
            xt = self.sbring(ctx, 2, [128, D], F32, "xt")
            xn = self.sbring(ctx, 2, [128, D], BF16, "xn")
            junk = self.sbring(ctx, 1, [128, 512], BF16, "junk")
            st4 = self.sbring(ctx, 3, [128, 4], F32, "st4")
            xTr = self.sbring(ctx, 2, [128, 8, 512], BF16, "xT")
            fm = self.sbring(ctx, 3, [128, 512], BF16, "fm")
            latTr = [self.sbring(ctx, 2, [128, 512], BF16, "latT") for _ in range(3)]
            vt = self.sbring(ctx, 2, [128, 384], BF16, "vt")
            tok = self.sbring(ctx, 3, [128, TOKW], F32, "tok")
            lat = self.sbring(ctx, 2, [128, 352], F32, "lat")
            sq = self.sbring(ctx, 2, [128, 4], F32, "sq")
            skv = self.sbring(ctx, 2, [128, 4], F32, "skv")
            q_s = self.sbring(ctx, 2, [128, 4, 96], F32, "q_s")
            kv_s = self.sbring(ctx, 2, [128, 4, 128], F32, "kv_s")
            qf = self.sbring(ctx, 2, [128, 4, 99], BF16, "qf")
            kf = self.sbring(ctx, 2, [128, 4, 99], BF16, "kf")
            vm = self.sbring(ctx, 2, [128, 4, 65], BF16, "vm")
            kr = self.sbring(ctx, 2, [128, 32], F32, "kr")
            tmpq = self.sbring(ctx, 4, [128, 4, 16], F32, "rtq")
            tmpk = self.sbring(ctx, 4, [128, 16], F32, "rtk")
            nrmq = self.sbring(ctx, 2, [128, 12], F32, "nrmq")
            nrmk = self.sbring(ctx, 2, [128, 12], F32, "nrmk")
            qTr = self.sbring(ctx, 2, [128, 4, 512], BF16, "qT_sb")
            kTr = self.sbring(ctx, 2, [128, 4, 512], BF16, "kT_sb")
            pT = self.psring(ctx, 1, [128, D], BF16, "pT")
            pF = self.psring(ctx, 2, [128, 512], F32, "pF")
            psV = self.psring(ctx, 1, [128, 384], F32, "psV")
            psL = self.psring(ctx, 1, [128, 352], F32, "psL")
            psU = self.psring(ctx, 2, [128, 512], F32, "psU")
            pQT = self.psring(ctx, 1, [128, 4, 128], BF16, "pQT")
            kmax2, rkm = self.kmax2, self.rkm
            s.op("dve", lambda e: e.memset(kmax2[:], 0.0), [], [rkm])
            for t_, r_ in zip(kf.t, kf.r):
                s.op("pool", lambda e, t_=t_: e.memset(t_[:], 1.0), [], [r_])
            for t_, r_ in zip(vm.t, vm.r):
                s.op("pool", lambda e, t_=t_: e.memset(t_[:], 1.0), [], [r_])
            tokd = self.tokd
            NS = T // 512
            xTs = {}

            def run(gens):
                gens = [g for g in gens if g is not None]
                while gens:
                    for g in list(gens):
                        try:
                            next(g)
                        except StopIteration:
                            gens.remove(g)

            def prep(si):
                if si >= NS:
                    return
                xTs[si] = xTr.next()
                xT, rxT = xTs[si]
                for tt in range(4):
                    tk = si * 512 + tt * 128
                    x, rx = xt.next()
                    s.dma(x[:], h_in[tk:tk + 128, :], writes=[rx])
                    xb, rxb = xn.next()
                    sv, rs = st4.next()
                    self.rstd(x[:], D, rx, (xb[:], rxb), sv, rs)
                    yield
                    self.tsc("dve", xb[:], x[:], sv[:, 3:4], None, ALU.mult, None, [rx, rs], [rxb])
                    yield
                    p, rp = pT.next()
                    for kc in range(8):
                        self.tr(p[:, kc * 128:(kc + 1) * 128], xb[:, kc * 128:(kc + 1) * 128], ident[:],
                                [rxb], [rp], sig=(kc == 7))
                    self.cp("act", xT[:, :, tt * 128:(tt + 1) * 128],
                            p[:].rearrange("p (k t) -> p k t", k=8), [rp], [rxT])
                    yield

            LT = {}

            def part2(si):
                xT, rxT = xTs[si]
                t0 = si * 512
                s.dma(self.uT_d[:, :, t0:t0 + 512].rearrange("k p t -> p k t"), xT[:], reads=[rxT], q="pool")
                lts = [r.next() for r in latTr]
                LT[si] = lts
                for ci, (c0, w) in enumerate(FM_CHUNKS + LAT_CHUNKS):
                    pf, rpf = pF.next()
                    for kc in range(8):
                        self.mm(pf[0:w, :], Win[:, kc, c0:c0 + w], xT[:, kc, :], kc == 0, kc == 7,
                                [rW, rxT], [rpf], sig=(kc == 7))
                    if ci < NFM:
                        f, rf = fm.next()
                        self.cp("act" if ci % 2 == 0 else "dve", f[0:w, :], pf[0:w, :], [rpf], [rf])
                        s.dma(self.PT_d[ci, 0:w, t0:t0 + 512], f[0:w, :], reads=[rf], q="pool")
                    else:
                        lt, rlt = lts[ci - NFM]
                        self.cp("act" if ci % 2 == 0 else "dve", lt[0:w, :], pf[0:w, :], [rpf], [rlt])
                    yield

            TS = {}

            def stA(si, tt):
                xT, rxT = xTs[si]
                lts = LT[si]
                tk = si * 512 + tt * 128
                tsl = slice(tt * 128, (tt + 1) * 128)
                pv, rpv = psV.next()
                for (c0, w, o0) in ((O_SWV, 128, 0), (O_NAV, 256, 128)):
                    for kc in range(8):
                        self.mm(pv[:, o0:o0 + w], xT[:, kc, tsl], Win[:, kc, c0:c0 + w],
                                kc == 0, kc == 7, [rW, rxT], [rpv], sig=(kc == 7))
                v, rv = vt.next()
                self.cp("act", v[:], pv[:], [rpv], [rv])
                s.dma(self.VT_d[tk:tk + 128, :], v[:], reads=[rv], q="pool")
                yield
                pl, rpl = psL.next()
                for kc in range(8):
                    self.mm(pl[:], xT[:, kc, tsl], Win[:, kc, O_CQ:O_CQ + 352], kc == 0, kc == 7,
                            [rW, rxT], [rpl], sig=(kc == 7))
                la, rla = lat.next()
                self.cp("act", la[:], pl[:], [rpl], [rla])
                tb, rtb = tok.next()
                s.dma(tb[:], tokd[tk:tk + 128, :], writes=[rtb])
                yield
                jk, rj = junk.next()
                a, ra = sq.next()
                self.rstd(la[:, 0:192], 192, rla, (jk[:, 0:192], rj), a, ra)
                yield
                b_, rb = skv.next()
                self.rstd(la[:, 192:320], 128, rla, (jk[:, 0:128], rj), b_, rb)
                yield
                pq, rpq = psU.next()
                self.mm(pq[:, 0:384], lts[0][0][:, tsl], Wq[:, 0, :], True, False, [rW, lts[0][1]], [rpq], sig=False)
                self.mm(pq[:, 0:384], lts[1][0][0:64, tsl], Wq[0:64, 1, :], False, True, [rW, lts[1][1]], [rpq])
                pk, rpk = psU.next()
                self.mm(pk[:], lts[2][0][:, tsl], Wkv[:, :], True, True, [rW, lts[2][1]], [rpk])
                TS[(si, tt)] = dict(tb=tb, rtb=rtb, la=la, rla=rla, a=a, ra=ra, b=b_, rb=rb, pq=pq, rpq=rpq,
                                    pk=pk, rpk=rpk)
                yield

            def stB(si, tt):
                d = TS[(si, tt)]
                tb, rtb, a, ra, pq, rpq = d["tb"], d["rtb"], d["a"], d["ra"], d["pq"], d["rpq"]
                tsl = slice(tt * 128, (tt + 1) * 128)
                if tt == 0:
                    d["qT"] = qTr.next()
                else:
                    d["qT"] = TS[(si, 0)]["qT"]
                qT_sb, rqT = d["qT"]
                jk, rj = junk.next()
                qs, rqs = q_s.next()
                self.tsc("dve", qs[:].rearrange("p h d -> p (h d)"), pq[:, 0:384], a[:, 3:4], None,
                         ALU.mult, None, [rpq, ra], [rqs])
                yield
                nm, rnm = nrmq.next()
                for h in range(4):
                    self.act(jk[:, 0:96], qs[:, h, :], AF.Square, [rqs], [rj, rnm], accum_out=nm[:, h:h + 1])
                    yield
                self.act(nm[:, 4:8], nm[:, 0:4], AF.Sqrt, [rnm], [rnm])
                q, rq = qf.next()
                self.tsc("dve", q[:, :, 0], nm[:, 4:8], -1.0, None, ALU.mult, None, [rnm], [rq])
                yield
                self.cp("pool", q[:, :, 1:3], tb[:, 32:34].unsqueeze(1).to_broadcast([128, 4, 2]), [rtb], [rq])
                self.cp("act", q[:, :, 3:67], qs[:, :, 0:64], [rqs], [rq])
                yield
                cosb = tb[:, 0:16].unsqueeze(1).to_broadcast([128, 4, 16])
                sinb = tb[:, 16:32].unsqueeze(1).to_broadcast([128, 4, 16])
                ta, rta = tmpq.next()
                tbb, rtbb = tmpq.next()
                self.ttn("dve", ta[:], qs[:, :, 64:80], cosb, ALU.mult, [rqs, rtb], [rta])
                yield
                self.ttn("dve", tbb[:], qs[:, :, 80:96], sinb, ALU.mult, [rqs, rtb], [rtbb])
                yield
                self.ttn("dve", q[:, :, 67:83], ta[:], tbb[:], ALU.subtract, [rta, rtbb], [rq])
                yield
                tc, rtc = tmpq.next()
                td, rtd = tmpq.next()
                self.ttn("dve", tc[:], qs[:, :, 64:80], sinb, ALU.mult, [rqs, rtb], [rtc])
                yield
                self.ttn("dve", td[:], qs[:, :, 80:96], cosb, ALU.mult, [rqs, rtb], [rtd])
                yield
                self.ttn("dve", q[:, :, 83:99], tc[:], td[:], ALU.add, [rtc, rtd], [rq])
                yield
                pt2, rpt2 = pQT.next()
                for h in range(4):
                    self.tr(pt2[0:99, h, :], q[:, h, :], ident[:], [rq], [rpt2], sig=(h == 3))
                self.cp("act", qT_sb[0:99, :, tsl], pt2[0:99, :, :], [rpt2], [rqT])
                if tt == 3:
                    t0 = si * 512
                    s.dma(self.QM_d[:, :, t0:t0 + 512].rearrange("h p t -> p h t"), qT_sb[0:99, :, :],
                          reads=[rqT], q="pool")
                yield

            def stC(si, tt):
                d = TS[(si, tt)]
                tb, rtb, la, rla, b_, rb, pk, rpk = (d["tb"], d["rtb"], d["la"], d["rla"], d["b"], d["rb"],
                                                      d["pk"], d["rpk"])
                tk = si * 512 + tt * 128
                tsl = slice(tt * 128, (tt + 1) * 128)
                if tt == 0:
                    d["kT"] = kTr.next()
                else:
                    d["kT"] = TS[(si, 0)]["kT"]
                kT_sb, rkT = d["kT"]
                jk, rj = junk.next()
                ks, rks = kv_s.next()
                self.tsc("dve", ks[:].rearrange("p h d -> p (h d)"), pk[:], b_[:, 3:4], None,
                         ALU.mult, None, [rpk, rb], [rks])
                yield
                krt, rkr = kr.next()
                c1 = tb[:, 0:16]
                s1 = tb[:, 16:32]
                ta, rta = tmpk.next()
                tbb, rtbb = tmpk.next()
                self.ttn("dve", ta[:], la[:, 320:336], c1, ALU.mult, [rla, rtb], [rta])
                yield
                self.ttn("dve", tbb[:], la[:, 336:352], s1, ALU.mult, [rla, rtb], [rtbb])
                yield
                self.ttn("dve", krt[:, 0:16], ta[:], tbb[:], ALU.subtract, [rta, rtbb], [rkr])
                yield
                tc, rtc = tmpk.next()
                td, rtd = tmpk.next()
                self.ttn("dve", tc[:], la[:, 320:336], s1, ALU.mult, [rla, rtb], [rtc])
                yield
                self.ttn("dve", td[:], la[:, 336:352], c1, ALU.mult, [rla, rtb], [rtd])
                yield
                self.ttn("dve", krt[:, 16:32], tc[:], td[:], ALU.add, [rtc, rtd], [rkr])
                yield
                k, rk = kf.next()
                self.cp("pool", k[:, :, 1:3], tb[:, 34:36].unsqueeze(1).to_broadcast([128, 4, 2]), [rtb], [rk])
                self.cp("act", k[:, :, 3:67], ks[:, :, 0:64], [rks], [rk])
                yield
                self.cp("pool", k[:, :, 67:99], krt[:].unsqueeze(1).to_broadcast([128, 4, 32]), [rkr], [rk])
                nm2, rnm2 = nrmk.next()
                for h in range(4):
                    self.act(jk[:, 0:64], ks[:, h, 0:64], AF.Square, [rks], [rj, rnm2], accum_out=nm2[:, h:h + 1])
                    yield
                self.act(jk[:, 0:32], krt[:], AF.Square, [rkr], [rj, rnm2], accum_out=nm2[:, 4:5])
                self.tsc("dve", nm2[:, 0:4], nm2[:, 0:4], nm2[:, 4:5], None, ALU.add, None, [rnm2], [rnm2])
                self.ttn("dve", kmax2[:], kmax2[:], nm2[:, 0:4], ALU.max, [rnm2, rkm], [rkm])
                yield
                vmt, rvm = vm.next()
                self.cp("pool", vmt[:, :, 0:64], ks[:, :, 64:128], [rks], [rvm])
                s.dma(self.VM_d[tk:tk + 128, :].rearrange("t (h c) -> t h c", h=4), vmt[:], reads=[rvm], q="pool")
                yield
                pt3, rpt3 = pQT.next()
                for h in range(4):
                    self.tr(pt3[0:99, h, :], k[:, h, :], ident[:], [rk], [rpt3], sig=(h == 3))
                self.cp("dve", kT_sb[0:99, :, tsl], pt3[0:99, :, :], [rpt3], [rkT])
                if tt == 3:
                    t0 = si * 512
                    s.dma(self.KM_d[:, :, t0:t0 + 512].rearrange("h p t -> p h t"), kT_sb[0:99, :, :],
                          reads=[rkT], q="pool")
                yield

            run([prep(0)])
            tiles = [(si, tt) for si in range(NS) for tt in range(4)]
            for si in range(NS):
                run([part2(si), prep(si + 1)])
                for tt in range(4):
                    if tt == 0:
                        run([stA(si, 0)])
                    nxt = stA(si, tt + 1) if tt + 1 < 4 else None
                    run([stB(si, tt), stC(si, tt), nxt])
                for tt in range(4):
                    del TS[(si, tt)]
            s.barrier()

    def mla_phase(self, l):
        s, T, NT = self.s, self.T, self.NT
        ident, identf = self.ident, self.identf
        with ExitStack() as ctx:
            KMs = self.sb(ctx, [128, 4, T], BF16, "KMs")
            VMs = self.sb(ctx, [128, NT, 260], BF16, "VMs")
            rK, rV = Res(), Res()
            for h in range(4):
                for c in range(0, T, 2048):
                    w = min(2048, T - c)
                    s.dma(KMs[0:99, h, c:c + w], self.KM_d[h, :, c:c + w], writes=[rK])
            for c in range(0, NT, 16):
                w = min(16, NT - c)
                s.dma(VMs[:, c:c + w, :],
                      self.VM_d[c * 128:(c + w) * 128, :].rearrange("(k p) c -> p k c", p=128), writes=[rV])
            ctxk = ExitStack()
            pK = self.ps(ctxk, [128, 4, 128], F32, "pK")
            rpK = Res()
            km = self.sb(ctx, [128, 8], F32, "km")
            rkm2 = Res()
            for h in range(4):
                self.tr(pK[0:1, h, :], self.kmax2[:, h:h + 1], identf[:], [self.rkm], [rpK], sig=(h == 3))
            s.op("dve", lambda e: e.tensor_reduce(out=km[0:1, 0:4], in_=pK[0:1, :, :], axis=AX.X, op=ALU.max),
                 [rpK], [rkm2])
            self.act(km[0:1, 4:8], km[0:1, 0:4], AF.Sqrt, [rkm2], [rkm2])
            s.barrier()
            ctxk.close()
            QW = 1024
            QT = self.sbring(ctx, 2, [128, 4, QW], BF16, "QT")
            Pt = self.sbring(ctx, 3, [128, QW], BF16, "Pt")
            rrr = self.sbring(ctx, 2, [128, 512], F32, "mrr")
            osb = self.sbring(ctx, 2, [65, 512], F32, "mosb")
            onb = self.sbring(ctx, 3, [64, 512], BF16, "monb")
            pS = self.psring(ctx, 3, [128, QW], F32, "pS")
            pO = self.psring(ctx, 1, [128, QW], F32, "pO")
            bcs = self.sbring(ctx, 2, [64, 512], F32, "mbc")
            nslot = [0]
            onesf = self.onesf
            NQG = T // QW
            qts = {}

            def load_q(qg):
                if qg >= NQG or qg in qts:
                    return
                qt, rq = QT.next()
                q0 = qg * QW
                s.dma(qt[0:99, :, :], self.QM_d[:, :, q0:q0 + QW].rearrange("h p t -> p h t"), writes=[rq])
                for h in range(4):
                    self.tsc("dve", qt[0:1, h, :], qt[0:1, h, :], km[0:1, 4 + h:5 + h], None, ALU.mult, None,
                             [rq, rkm2], [rq])
                qts[qg] = (qt, rq)
            items = [(qg, h, kt) for qg in range(NQG) for h in range(4) for kt in range(NT)]
            n = len(items)
            LA = 2
            sbuf_ = {}
            pbuf = {}
            obuf = {}

            def emit_S(i):
                qg, h, kt = items[i]
                load_q(qg)
                qt, rq = qts[qg]
                sp_, rs_ = pS.next()
                for hf in range(2):
                    self.mm(sp_[:, hf * 512:(hf + 1) * 512], KMs[0:99, h, kt * 128:(kt + 1) * 128],
                            qt[0:99, h, hf * 512:(hf + 1) * 512], True, True, [rK, rq], [rs_], sig=(hf == 1))
                sbuf_[i] = (sp_, rs_)

            def emit_exp(i):
                sp_, rs_ = sbuf_.pop(i)
                p, rp = Pt.next()
                self.act(p[:], sp_[:], AF.Exp, [rs_], [rp], scale=MLA_SCALE)
                pbuf[i] = (p, rp)

            def emit_PV(i):
                qg, h, kt = items[i]
                p, rp = pbuf.pop(i)
                if kt == 0:
                    obuf[(qg, h)] = pO.next()
                o, ro = obuf[(qg, h)]
                for hf in range(2):
                    self.mm(o[0:65, hf * 512:(hf + 1) * 512], VMs[:, kt, h * 65:(h + 1) * 65],
                            p[:, hf * 512:(hf + 1) * 512], kt == 0, kt == NT - 1, [rp, rV], [ro], sig=(hf == 1))
                if kt == NT - 1:
                    hs = []
                    for hf in range(2):
                        oh = o[:, hf * 512:(hf + 1) * 512]
                        ob, rob = osb.next()
                        self.cp("dve", ob[0:65, :], oh[0:65, :], [ro], [rob])
                        hs.append((ob, rob))
                    for hf in range(2):
                        ob, rob = hs[hf]
                        rr, rrr_ = rrr.next()
                        s.op("dve", lambda e, rr=rr, ob=ob: e.reciprocal(out=rr[64:65, :], in_=ob[64:65, :]),
                             [rob], [rrr_])
                        hs[hf] = (rr, rrr_, ob, rob)
                    bb = []
                    for hf in range(2):
                        rr, rrr_, ob, rob = hs[hf]
                        sl = nslot[0] % 8
                        nslot[0] += 1
                        rsl = self.rNRM[sl]
                        s.dma(self.NRM_d[sl:sl + 1, :], rr[64:65, :], reads=[rrr_], writes=[rsl], q="pool")
                        bc, rbc = bcs.next()
                        s.dma(bc[:], self.NRM_d[sl:sl + 1, :].partition_broadcast(64), reads=[rsl], writes=[rbc])
                        bb.append((bc, rbc))
                    for hf in range(2):
                        rr, rrr_, ob, rob = hs[hf]
                        bc, rbc = bb[hf]
                        on, ron = onb.next()
                        self.ttn("dve", on[:], ob[0:64, :], bc[:], ALU.mult, [rob, rbc], [ron])
                        q0 = qg * QW + hf * 512
                        s.dma(self.BR_d[3, h // 2, (h % 2) * 64:(h % 2) * 64 + 64, q0:q0 + 512], on[:], reads=[ron],
                              q="pool")
                    del obuf[(qg, h)]
                    if h == 1:
                        load_q(qg + 1)
            for i in range(min(LA, n)):
                emit_S(i)
            for i in range(n):
                emit_exp(i)
                if i + LA < n:
                    emit_S(i + LA)
                emit_PV(i)
            s.barrier()

    def attn_core(self, ctx, name, qchunk0, kv_of, KT, rKT, V, rV, keytiles, table, sink, br_idx, maxk):
        s, T, NT = self.s, self.T, self.NT
        ident = self.ident
        QT = self.sbring(ctx, 2, [64, 4, 512], BF16, "aQT")
        Sb = self.sbring(ctx, 2, [128, maxk * 128], F32, "aSb")
        Pb = self.sbring(ctx, 2, [128, maxk * 128], BF16, "aPb")
        PTs = self.sbring(ctx, 2, [128, maxk, 128], BF16, "aPT")
        st = self.sbring(ctx, 6, [128, 8], F32, "ast")
        Oall = self.sbring(ctx, 2, [128, 256], BF16, "aOall")
        brT = self.sbring(ctx, 2, [128, 2, 512], BF16, "abrT")
        nb = 2 if maxk > 4 else 1
        pS = self.psring(ctx, 2, [128, 512 * nb], F32, "apS")
        pP = self.psring(ctx, 2, [128, maxk, 128], BF16, "apP")
        pO_t = self.ps(ctx, [128, 64], F32, "apO")
        pO = Ring([pO_t[:, :]])
        pT2 = self.psring(ctx, 1, [128, 2, 128], BF16, "apT2")
        NQG = T // 512
        qts = {}

        def load_q(qg):
            if qg >= NQG or qg in qts:
                return
            qt, rq = QT.next()
            q0 = qg * 512
            for h in range(4):
                s.dma(qt[0:64, h, :], self.PT_d[qchunk0 + h // 2, (h % 2) * 64:(h % 2) * 64 + 64, q0:q0 + 512],
                      writes=[rq])
            qts[qg] = (qt, rq)
        items = [(qg, jj, h) for qg in range(NQG) for jj in range(4) for h in range(4)]
        n = len(items)
        it = {}
        tabs = {}
        oas = {}
        bts = {}

        def fS(i):
            qg, jj, h = items[i]
            j = qg * 4 + jj
            load_q(qg)
            if jj == 1 and h == 0:
                load_q(qg + 1)
            qt, rq = qts[qg]
            kts = keytiles(j)
            nk = len(kts)
            if h == 0:
                tabs[j] = table(j) if table is not None else None
            g = kv_of(h)
            sp_, rs_ = pS.next()
            for ii, kt in enumerate(kts):
                self.mm(sp_[:, ii * 128:(ii + 1) * 128], qt[0:64, h, jj * 128:(jj + 1) * 128],
                        KT[0:64, g, kt * 128:(kt + 1) * 128], True, True, [rq, rKT], [rs_], sig=(ii == nk - 1))
            it[i] = dict(sp=sp_, rs=rs_, kts=kts, g=g, tab=tabs[j])

        def fD1(i):
            qg, jj, h = items[i]
            d = it[i]
            sv, rsv = st.next()
            Wd = len(d["kts"]) * 128
            sp_, rs_ = d["sp"], d["rs"]
            if d["tab"] is not None:
                tb_ap, rtab = d["tab"]
                sb_, rsb = Sb.next()
                self.ttn("dve", sb_[:, 0:Wd], sp_[:, 0:Wd], tb_ap[:, h, 0:Wd], ALU.add, [rs_, rtab], [rsb])
                s.op("dve", lambda e, sv=sv, sb_=sb_, Wd=Wd: e.tensor_reduce(
                    out=sv[:, 0:1], in_=sb_[:, 0:Wd], axis=AX.X, op=ALU.max), [rsb], [rsv])
                if sink is not None:
                    self.ttn("dve", sv[:, 0:1], sv[:, 0:1], sink[:, h:h + 1], ALU.max, [rsv], [rsv])
                src, rsrc = sb_, rsb
            else:
                s.op("dve", lambda e, sv=sv, sp_=sp_, Wd=Wd: e.tensor_reduce(
                    out=sv[:, 0:1], in_=sp_[:, 0:Wd], axis=AX.X, op=ALU.max), [rs_], [rsv])
                src, rsrc = sp_, rs_
            self.act(sv[:, 1:2], sv[:, 0:1], AF.Copy, [rsv], [rsv], scale=-1.0)
            d.update(sv=sv, rsv=rsv, src=src, rsrc=rsrc, Wd=Wd)

        def fE(i):
            qg, jj, h = items[i]
            d = it[i]
            pb, rpb = Pb.next()
            sv, rsv, Wd = d["sv"], d["rsv"], d["Wd"]
            self.act(pb[:, 0:Wd], d["src"][:, 0:Wd], AF.Exp, [d["rsrc"], rsv], [rpb, rsv], bias=sv[:, 1:2], scale=1.0,
                     accum_out=sv[:, 2:3])
            if sink is not None:
                self.act(sv[:, 3:4], sink[:, h:h + 1], AF.Exp, [rsv], [rsv], bias=sv[:, 1:2], scale=1.0)
            d.update(pb=pb, rpb=rpb)

        def fT(i):
            d = it[i]
            nk = len(d["kts"])
            pp, rpp = pP.next()
            for ii in range(nk):
                self.tr(pp[:, ii, :], d["pb"][:, ii * 128:(ii + 1) * 128], ident[:], [d["rpb"]], [rpp],
                        sig=(ii == nk - 1))
            d.update(pp=pp, rpp=rpp)

        def fC(i):
            d = it[i]
            nk = len(d["kts"])
            pts, rpts = PTs.next()
            self.cp("act", pts[:, 0:nk, :], d["pp"][:, 0:nk, :], [d["rpp"]], [rpts])
            d.update(pts=pts, rpts=rpts)

        def fPV(i):
            d = it[i]
            kts, g = d["kts"], d["g"]
            nk = len(kts)
            o, ro = pO.next()
            for ii, kt in enumerate(kts):
                self.mm(o, d["pts"][:, ii, :], V[:, kt, g * 64:(g + 1) * 64], ii == 0, ii == nk - 1,
                        [d["rpts"], rV], [ro], sig=(ii == nk - 1))
            d.update(o=o, ro=ro)

        def fD2(i):
            qg, jj, h = items[i]
            d = it.pop(i)
            sv, rsv = d["sv"], d["rsv"]
            if h == 0:
                oas[(qg, jj)] = Oall.next()
            oa, roa = oas[(qg, jj)]
            if jj == 0 and h == 0:
                bts[qg] = brT.next()
            bt, rbt = bts[qg]
            if sink is not None:
                self.ttn("dve", sv[:, 2:3], sv[:, 2:3], sv[:, 3:4], ALU.add, [rsv], [rsv])
            s.op("dve", lambda e, sv=sv: e.reciprocal(out=sv[:, 4:5], in_=sv[:, 2:3]), [rsv], [rsv])
            self.act(oa[:, h * 64:(h + 1) * 64], d["o"], AF.Copy, [d["ro"], rsv], [roa], scale=sv[:, 4:5])
            if h == 3:
                pt, rpt = pT2.next()
                for c in range(2):
                    self.tr(pt[:, c, :], oa[:, c * 128:(c + 1) * 128], ident[:], [roa], [rpt], sig=(c == 1))
                self.cp("act", bt[:, :, jj * 128:(jj + 1) * 128], pt[:, :, :], [rpt], [rbt])
                if jj == 3:
                    q0 = qg * 512
                    s.dma(self.BR_d[br_idx, :, :, q0:q0 + 512].rearrange("c p t -> p c t"), bt[:], reads=[rbt],
                          q="pool")
        for k in range(-2, n + 1):
            if 0 <= k + 2 < n:
                fS(k + 2)
            if 0 <= k + 1 < n:
                fD1(k + 1)
                fE(k + 1)
            if 0 <= k < n:
                fT(k)
                fC(k)
            if 0 <= k - 1 < n:
                fPV(k - 1)
                fD2(k - 1)

    def swa_phase(self, l):
        s, T, NT = self.s, self.T, self.NT
        with ExitStack() as ctx:
            KT = self.sb(ctx, [64, 2, T], BF16, "swKT")
            V = self.sb(ctx, [128, NT, 128], BF16, "swV")
            tabs = self.sb(ctx, [128, 5, 4, 384], F32, "swtab")
            rKT, rV, rT = Res(), Res(), Res()
            for g in range(2):
                s.dma(KT[0:64, g, :], self.PT_d[4, g * 64:(g + 1) * 64, :], writes=[rKT])
            s.dma(V[:], self.VT_d[:, 0:128].rearrange("(k p) c -> p k c", p=128), writes=[rV])
            for i in range(5):
                s.dma(tabs[:, i, :, :], self.W["swa_tab%d" % l][i], writes=[rT])
            mid = NT // 2
            slot = {0: 1, mid - 1: 2, mid: 3, NT - 1: 4}

            def keytiles(j):
                return [max(j - 1, 0), j, min(j + 1, NT - 1)]

            def table(j):
                return (tabs[:, slot.get(j, 0), :, :], rT)
            sink = self.cols[:, CP.sl("sink%d" % l)]
            self.attn_core(ctx, "swa", 2, lambda h: h // 2, KT, rKT, V, rV, keytiles, table, sink, 1, 3)
            s.barrier()

    def na_keytiles(self, j):
        NT = self.NT
        mid = NT // 2
        base = min(max(j - 2, 0), NT - 6)
        if j == mid - 1:
            base = min(max(mid - 4, 0), NT - 6)
        return list(range(base, base + 6))

    def na_specials(self):
        NT = self.NT
        mid = NT // 2
        sp = sorted(set(x for x in (0, 1, mid - 2, mid - 1, mid, mid + 1, NT - 3, NT - 2, NT - 1)
                        if 0 <= x < NT))
        return sp

    def na_phase(self, l):
        s, T, NT = self.s, self.T, self.NT
        with ExitStack() as ctx:
            KT = self.sb(ctx, [64, 4, T], BF16, "naKT")
            V = self.sb(ctx, [128, NT, 256], BF16, "naV")
            tab0 = self.sb(ctx, [128, 4, 768], F32, "natab0")
            tabr = self.sbring(ctx, 2, [128, 4, 768], F32, "natabr")
            rKT, rV, rT0 = Res(), Res(), Res()
            for h in range(4):
                s.dma(KT[0:64, h, :], self.PT_d[7 + h // 2, (h % 2) * 64:(h % 2) * 64 + 64, :], writes=[rKT])
            s.dma(V[:], self.VT_d[:, 128:384].rearrange("(k p) c -> p k c", p=128), writes=[rV])
            natab = self.W["na_tab%d" % l]
            s.dma(tab0[:], natab[0], writes=[rT0])
            sp = self.na_specials()
            slot = {j: i + 1 for i, j in enumerate(sp)}

            def table(j):
                if j not in slot:
                    return (tab0, rT0)
                t, r = tabr.next()
                s.dma(t[:], natab[slot[j]], writes=[r])
                return (t, r)
            self.attn_core(ctx, "na", 5, lambda h: h, KT, rKT, V, rV, self.na_keytiles, table, None, 2, 6)
            s.barrier()

    def mem_phase(self, l):
        s, T, NT = self.s, self.T, self.NT
        ident, cols, W = self.ident, self.cols, self.W
        with ExitStack() as ctx:
            KT = self.sb(ctx, [64, 4, 512], BF16, "mKT")
            V = self.sb(ctx, [128, 4, 256], BF16, "mV")
            rKT, rV = Res(), Res()
            with ExitStack() as c1:
                Wm = self.sb(c1, [128, 8, 512], BF16, "Wm")
                rW = Res()
                gmem = cols[:, CP.sl("gmem%d" % l)]
                wm = W["wmem%d" % l]
                st = self.sbring(c1, 2, [128, 512], F32, "stg")
                self.load_cast(c1, lambda i: Wm[:, i, :], lambda i: wm[i * 128:(i + 1) * 128, :], 8, None,
                               lambda i: gmem[:, i:i + 1], rW, st)
                xt = self.sbring(c1, 2, [128, D], F32, "xt")
                xn = self.sbring(c1, 1, [128, D], BF16, "xn")
                junk = self.sbring(c1, 1, [128, D], BF16, "junk")
                st4 = self.sbring(c1, 2, [128, 4], F32, "st4")
                memT = self.sb(c1, [128, 8, 512], BF16, "memT")
                rmT = Res()
                pT = self.psring(c1, 1, [128, D], BF16, "pT")
                pF = self.psring(c1, 2, [128, 512], F32, "pF")
                for i in range(4):
                    x, rx = xt.next()
                    s.dma(x[:], self.memd[i // 2, (i % 2) * 128:(i % 2) * 128 + 128, :], writes=[rx])
                    jk, rj = junk.next()
                    sv, rs = st4.next()
                    self.rstd(x[:], D, rx, (jk[:], rj), sv, rs)
                    xb, rxb = xn.next()
                    self.tsc("dve", xb[:], x[:], sv[:, 3:4], None, ALU.mult, None, [rx, rs], [rxb])
                    p, rp = pT.next()
                    for kc in range(8):
                        self.tr(p[:, kc * 128:(kc + 1) * 128], xb[:, kc * 128:(kc + 1) * 128], ident[:],
                                [rxb], [rp], sig=(kc == 7))
                    self.cp("act", memT[:, :, i * 128:(i + 1) * 128],
                            p[:].rearrange("p (k t) -> p k t", k=8), [rp], [rmT])
                for h in range(4):
                    pf, rpf = pF.next()
                    for kc in range(8):
                        self.mm(pf[0:64, :], Wm[:, kc, h * 64:(h + 1) * 64], memT[:, kc, :], kc == 0, kc == 7,
                                [rW, rmT], [rpf], sig=(kc == 7))
                    self.cp("dve", KT[0:64, h, :], pf[0:64, :], [rpf], [rKT])
                for i in range(4):
                    pf, rpf = pF.next()
                    for kc in range(8):
                        self.mm(pf[:, 0:256], memT[:, kc, i * 128:(i + 1) * 128], Wm[:, kc, 256:512],
                                kc == 0, kc == 7, [rW, rmT], [rpf], sig=(kc == 7))
                    self.cp("dve", V[:, i, :], pf[:, 0:256], [rpf], [rV])
                s.barrier()
            mid = NT // 2

            def keytiles(j):
                sl = 0 if j < mid else 1
                return [2 * sl, 2 * sl + 1]
            self.attn_core(ctx, "mem", 9, lambda h: h, KT, rKT, V, rV, keytiles, None, None, 4, 2)
            s.barrier()

    def s5_phase(self, l):
        s, T, NT = self.s, self.T, self.NT
        cols, W = self.cols, self.W
        identf = self.identf
        TWO_PI = 2.0 * np.pi
        MAGIC = 12582912.0
        with ExitStack() as ctx:
            WA = self.sb(ctx, [128, 32, 128], BF16, "WA")
            WB = self.sb(ctx, [128, 32, 128], BF16, "WB")
            CW = self.sb(ctx, [128, 32, 128], BF16, "CW")
            CV = self.sb(ctx, [128, 32, 128], BF16, "CV")
            COS16 = self.sb(ctx, [128, 32, 128], BF16, "COS16")
            SIN16 = self.sb(ctx, [128, 32, 128], BF16, "SIN16")
            RT = self.sb(ctx, [128, 32, 128], F32, "RT")
            ROT = self.sb(ctx, [128, 32, 128], F32, "ROT")
            rc = self.sb(ctx, [128, 32], F32, "rc")
            rP = Res()
            with ExitStack() as c1:
                row = self.W["s5row%d" % l]
                R_ = [self.sb(c1, [128, 1024], F32, "row") for _ in range(12)]
                rr = Res()
                lre, lim, lst, bre, bim = R_[0:5]
                t1, t2, t3, t4, t5, t6, t7 = R_[5:12]
                A_ = lambda out, in_, func, **kw: self.act(out, in_, func, [rr], [rr], **kw)

                def frac_sin(dst, y, tmp):
                    self.tsc("dve", tmp, y, MAGIC, None, ALU.add, None, [rr], [rr])
                    self.tsc("dve", tmp, tmp, MAGIC, None, ALU.subtract, None, [rr], [rr])
                    self.ttn("dve", tmp, y, tmp, ALU.subtract, [rr], [rr])
                    A_(dst, tmp, AF.Sin, scale=TWO_PI)
                for dd in range(2):
                    for i in range(5):
                        s.dma(R_[i][:], row[i, :, dd * 1024:(dd + 1) * 1024], writes=[rr])
                    A_(t1[:], lst[:], AF.Exp)
                    self.ttn("dve", t2[:], lre[:], t1[:], ALU.mult, [rr], [rr])
                    A_(t2[:], t2[:], AF.Exp)
                    self.stt(t3[:], lim[:], 1.0 / TWO_PI, t1[:], ALU.mult, ALU.mult, [rr], [rr])
                    frac_sin(t4[:], t3[:], t5[:])
                    self.tsc("dve", t3[:], t3[:], 0.25, None, ALU.add, None, [rr], [rr])
                    frac_sin(t6[:], t3[:], t5[:])
                    self.ttn("dve", t4[:], t4[:], t2[:], ALU.mult, [rr], [rr])
                    self.ttn("dve", t6[:], t6[:], t2[:], ALU.mult, [rr], [rr])
                    self.tsc("dve", t6[:], t6[:], -1.0, None, ALU.add, None, [rr], [rr])
                    self.ttn("dve", t1[:], lre[:], lre[:], ALU.mult, [rr], [rr])
                    self.ttn("dve", t2[:], lim[:], lim[:], ALU.mult, [rr], [rr])
                    self.ttn("dve", t1[:], t1[:], t2[:], ALU.add, [rr], [rr])
                    s.op("dve", lambda e: e.reciprocal(out=t1[:], in_=t1[:]), [rr], [rr])
                    self.ttn("dve", t2[:], t6[:], lre[:], ALU.mult, [rr], [rr])
                    self.ttn("dve", t5[:], t4[:], lim[:], ALU.mult, [rr], [rr])
                    self.ttn("dve", t2[:], t2[:], t5[:], ALU.add, [rr], [rr])
                    self.ttn("dve", t2[:], t2[:], t1[:], ALU.mult, [rr], [rr])
                    self.ttn("dve", t3[:], t4[:], lre[:], ALU.mult, [rr], [rr])
                    self.ttn("dve", t5[:], t6[:], lim[:], ALU.mult, [rr], [rr])
                    self.ttn("dve", t3[:], t3[:], t5[:], ALU.subtract, [rr], [rr])
                    self.ttn("dve", t3[:], t3[:], t1[:], ALU.mult, [rr], [rr])
                    self.ttn("dve", t4[:], t2[:], bre[:], ALU.mult, [rr], [rr])
                    self.ttn("dve", t5[:], t3[:], bim[:], ALU.mult, [rr], [rr])
                    self.ttn("dve", t4[:], t4[:], t5[:], ALU.subtract, [rr], [rr])
                    self.ttn("dve", t6[:], t2[:], bim[:], ALU.mult, [rr], [rr])
                    self.ttn("dve", t5[:], t3[:], bre[:], ALU.mult, [rr], [rr])
                    self.ttn("dve", t6[:], t6[:], t5[:], ALU.add, [rr], [rr])
                    v3 = lambda t: t[:].rearrange("p (a n) -> p a n", n=64)
                    dsl = slice(dd * 16, (dd + 1) * 16)
                    self.cp("dve", WA[:, dsl, 0:64], v3(t4), [rr], [rP])
                    self.cp("dve", WA[:, dsl, 64:128], v3(t6), [rr], [rP])
                    self.cp("dve", WB[:, dsl, 0:64], v3(t6), [rr], [rP])
                    self.tsc("dve", WB[:, dsl, 64:128], v3(t4), -1.0, None, ALU.mult, None, [rr], [rP])
                s.barrier()
            with ExitStack() as c1:
                cst = self.W["s5const"]
                JT = self.sb(c1, [128, 32, 128], F32, "JT")
                COS = self.sb(c1, [128, 32, 128], F32, "COS")
                SIN = self.sb(c1, [128, 32, 128], F32, "SIN")
                Y = self.sb(c1, [128, 32, 128], F32, "Yt")
                TM = self.sb(c1, [128, 32, 128], F32, "TM")
                P2 = self.sb(c1, [128, 128], F32, "P2")
                sm = self.sb(c1, [128, 8, 32], F32, "sm")
                rr = Res()
                s.dma(JT[:].rearrange("p a j -> p (a j)"), cst[0], writes=[rr])
                s.dma(P2[:], self.W["p2"], writes=[rr])
                lrec = cols[:, CP.sl("lre%d" % l)]
                limc = cols[:, CP.sl("lim%d" % l)]
                lstc = cols[:, CP.sl("lst%d" % l)]
                A_ = lambda out, in_, func, **kw: self.act(out, in_, func, [rr], [rr, rP], **kw)

                def frac_sin2(dst, y, tmp):
                    self.tsc("dve", tmp, y, MAGIC, None, ALU.add, None, [rr], [rr])
                    self.tsc("dve", tmp, tmp, MAGIC, None, ALU.subtract, None, [rr], [rr])
                    self.ttn("dve", tmp, y, tmp, ALU.subtract, [rr], [rr])
                    A_(dst, tmp, AF.Sin, scale=TWO_PI)
                dtc, yc, tq, cL, sL = sm[:, 0, :], sm[:, 1, :], sm[:, 2, :], sm[:, 3, :], sm[:, 4, :]
                A_(dtc, lstc, AF.Exp)
                self.ttn("dve", tq, lrec, dtc, ALU.mult, [rr], [rr])
                A_(rc[:], tq, AF.Exp)
                self.stt(yc, limc, 1.0 / TWO_PI, dtc, ALU.mult, ALU.mult, [rr], [rr])
                ycb = yc.unsqueeze(2).to_broadcast([128, 32, 128])
                self.ttn("dve", Y[:], JT[:], ycb, ALU.mult, [rr], [rr])
                f2 = lambda t: t[:].rearrange("p a j -> p (a j)")
                frac_sin2(f2(SIN), f2(Y), f2(TM))
                self.tsc("dve", f2(Y), f2(Y), 0.25, None, ALU.add, None, [rr], [rr])
                frac_sin2(f2(COS), f2(Y), f2(TM))
                self.cp("act", COS16[:], COS[:], [rr], [rr, rP])
                self.cp("act", SIN16[:], SIN[:], [rr], [rr, rP])
                self.tsc("dve", Y[:], JT[:], 0.5, None, ALU.is_gt, None, [rr], [rr])
                self.ttn("dve", RT[:], Y[:], rc[:].unsqueeze(2).to_broadcast([128, 32, 128]), ALU.mult,
                         [rr], [rr, rP])
                self.tsc("dve", tq, yc, 128.0, None, ALU.mult, None, [rr], [rr])
                frac_sin2(sL, tq, sm[:, 5, :])
                self.tsc("dve", tq, tq, 0.25, None, ALU.add, None, [rr], [rr])
                frac_sin2(cL, tq, sm[:, 5, :])
                self.tsc("dve", sL, sL, cols[:, CP.sl("sgn")], None, ALU.mult, None, [rr], [rr])
                self.ttn("dve", ROT[:], identf[:].unsqueeze(1).to_broadcast([128, 32, 128]),
                         cL.unsqueeze(2).to_broadcast([128, 32, 128]), ALU.mult, [rr], [rr, rP])
                self.ttn("dve", TM[:], P2[:].unsqueeze(1).to_broadcast([128, 32, 128]),
                         sL.unsqueeze(2).to_broadcast([128, 32, 128]), ALU.mult, [rr], [rr])
                self.ttn("dve", ROT[:], ROT[:], TM[:], ALU.add, [rr], [rr, rP])
                c_d = self.W["s5c%d" % l]
                s.dma(Y[0:64].rearrange("p a j -> p (a j)"), c_d[0], writes=[rr])
                s.dma(Y[64:128].rearrange("p a j -> p (a j)"), c_d[1], writes=[rr])
                s.dma(TM[0:64].rearrange("p a j -> p (a j)"), c_d[1], writes=[rr])
                s.dma(TM[64:128].rearrange("p a j -> p (a j)"), c_d[0], writes=[rr])
                self.cp("dve", CW[0:64], Y[0:64], [rr], [rP])
                self.tsc("dve", CW[64:128], Y[64:128], -1.0, None, ALU.mult, None, [rr], [rP])
                self.tsc("dve", CV[:], TM[:], -1.0, None, ALU.mult, None, [rr], [rP])
                s.barrier()
            flag = cols[:, CP.sl("flag")]
            with ExitStack() as c1:
                aT = self.sbring(c1, 3, [128, 2, 128], BF16, "aT")
                t1r = self.sbring(c1, 2, [128, 4, 128], BF16, "s5t1")
                t2r = self.sbring(c1, 2, [128, 4, 128], BF16, "s5t2")
                Xr = self.sbring(c1, 2, [128, 4, 128], F32, "s5X")
                Zr = self.sbring(c1, 3, [128, 4, 128], F32, "s5Z")
                Zbr = self.sbring(c1, 3, [128, 4, 128], BF16, "s5Zb")
                A16 = self.sbring(c1, 2, [128, 4, 128], BF16, "s5A16")
                B16 = self.sbring(c1, 2, [128, 4, 128], BF16, "s5B16")
                Wr = self.sbring(c1, 6, [128, 4, 128], BF16, "s5W")
                Vr = self.sbring(c1, 6, [128, 4, 128], BF16, "s5V")
                INIT = self.sb(c1, [128, 16], F32, "s5init")
                rI = [Res() for _ in range(4)]
                ysb = self.sbring(c1, 3, [128, 128], F32, "s5y")
                yld = self.sbring(c1, 3, [128, 128], F32, "s5yl")
                pA = self.psring(c1, 2, [128, 4, 128], F32, "pA")
                pB = self.psring(c1, 2, [128, 4, 128], F32, "pB")
                pI = self.psring(c1, 2, [128, 4], F32, "pI")
                pY = self.psring(c1, 2, [128, 128], F32, "pY")
                mid = NT // 2
                steps = []
                orders = {0: list(range(NT)), 1: list(range(NT - 1, -1, -1))}
                for d in range(2):
                    for ci, c in enumerate(orders[d]):
                        for half in range(2):
                            for quad in range(2):
                                steps.append((d, ci, c, half, quad))
                nst = len(steps)
                aTs = {}

                def load_a(d, ci):
                    if ci >= NT or (d, ci) in aTs:
                        return
                    c = orders[d][ci]
                    a, ra = aT.next()
                    s.dma(a[:], self.PT_d[0:2, :, c * 128:(c + 1) * 128].rearrange("c p t -> p c t"), writes=[ra])
                    aTs[(d, ci)] = (a, ra)
                AB = {}
                ST = {}

                def emit_ab(k):
                    d, ci, c, half, quad = steps[k]
                    load_a(d, ci)
                    if half == 0 and quad == 0:
                        load_a(d, ci + 1)
                    a, ra = aTs[(d, ci)]
                    dg0 = d * 16 + half * 8 + quad * 4
                    pa, rpa = pA.next()
                    pb, rpb = pB.next()
                    for i in range(4):
                        self.mm(pa[:, i, :], WA[:, dg0 + i, :], a[:, half, :], True, True, [rP, ra], [rpa], sig=(i == 3))
                    for i in range(4):
                        self.mm(pb[:, i, :], WB[:, dg0 + i, :], a[:, half, :], True, True, [rP, ra], [rpb], sig=(i == 3))
                    a16, ra16 = A16.next()
                    b16, rb16 = B16.next()
                    self.cp("act", a16[:], pa[:], [rpa], [ra16])
                    self.cp("act", b16[:], pb[:], [rpb], [rb16])
                    AB[k] = (a16, ra16, b16, rb16)

                def emit_scan(k):
                    d, ci, c, half, quad = steps[k]
                    jf = 0 if d == 0 else 127
                    qi = half * 2 + quad
                    g0 = half * 8 + quad * 4
                    dg0 = d * 16 + g0
                    a16, ra16, b16, rb16 = AB.pop(k)
                    if ci == 0 and half == 0 and quad == 0:
                        s.op("dve", lambda e: e.memset(INIT[:], 0.0), [], rI)
                    x1, rx1 = t1r.next()
                    x2, rx2 = t2r.next()
                    X, rX = Xr.next()
                    self.ttn("dve", x1[:], a16[:], COS16[:, dg0:dg0 + 4, :], ALU.mult, [ra16, rP], [rx1])
                    self.ttn("dve", x2[:], b16[:], SIN16[:, dg0:dg0 + 4, :], ALU.mult, [rb16, rP], [rx2])
                    self.ttn("dve", X[:], x1[:], x2[:], ALU.add, [rx1, rx2], [rX])
                    self.ttn("dve", X[:, :, jf], X[:, :, jf], INIT[:, g0:g0 + 4], ALU.add, [rX, rI[qi]], [rX])
                    Z, rZ = Zr.next()
                    z2 = Z[:].rearrange("p g j -> p (g j)")
                    x2d = X[:].rearrange("p g j -> p (g j)")
                    r2d = RT[:, dg0:dg0 + 4, :].rearrange("p g j -> p (g j)")
                    if d == 1:
                        z2, x2d, r2d = z2[:, ::-1], x2d[:, ::-1], r2d[:, ::-1]
                    s.op("dve", lambda e, z2=z2, x2d=x2d, r2d=r2d: e.tensor_tensor_scan(
                        out=z2, data0=r2d, data1=x2d, initial=0.0, op0=ALU.mult, op1=ALU.add), [rX, rP], [rZ])
                    ST[k] = dict(Z=Z, rZ=rZ)

                def emit_rot(k):
                    d, ci, c, half, quad = steps[k]
                    jl = 127 if d == 0 else 0
                    dg0 = d * 16 + half * 8 + quad * 4
                    Z, rZ = ST[k]["Z"], ST[k]["rZ"]
                    pi_, rpi = pI.next()
                    for i in range(4):
                        self.mm(pi_[:, i:i + 1], ROT[:, dg0 + i, :], Z[:, i, jl:jl + 1], True, True, [rP, rZ], [rpi],
                                sig=(i == 3))
                    ST[k]["pi"] = (pi_, rpi)
                    Wt, rWt = Wr.next()
                    Vt, rVt = Vr.next()
                    Zb, rZb = Zbr.next()
                    self.cp("act", Zb[:], Z[:], [rZ], [rZb])
                    ST[k]["wv"] = (Wt, rWt, Vt, rVt, dg0)
                    ST[k]["zb"] = (Zb, rZb)

                def emit_wv(k):
                    Wt, rWt, Vt, rVt, dg0 = ST[k]["wv"]
                    Zb, rZb = ST[k]["zb"]
                    self.ttn("dve", Wt[:], Zb[:], COS16[:, dg0:dg0 + 4, :], ALU.mult, [rZb, rP], [rWt])
                    self.ttn("dve", Vt[:], Zb[:], SIN16[:, dg0:dg0 + 4, :], ALU.mult, [rZb, rP], [rVt])

                def emit_evac(k):
                    d, ci, c, half, quad = steps[k]
                    qi = half * 2 + quad
                    g0 = half * 8 + quad * 4
                    dg0 = d * 16 + g0
                    pi_, rpi = ST[k]["pi"]
                    if ci == NT - 1:
                        return
                    self.ttn("dve", INIT[:, g0:g0 + 4], pi_[:, 0:4], rc[:, dg0:dg0 + 4], ALU.mult, [rpi, rP], [rI[qi]])
                    nxt = orders[d][ci + 1] if ci + 1 < NT else None
                    if nxt is not None and ((c < mid) != (nxt < mid)):
                        self.tsc("dve", INIT[:, g0:g0 + 4], INIT[:, g0:g0 + 4], flag, None, ALU.mult, None,
                                 [rI[qi]], [rI[qi]])

                def emit_c(k):
                    d, ci, c, half, quad = steps[k]
                    py, rpy = pY.next()
                    n = 0
                    for kk in (k - 1, k):
                        Wt, rWt, Vt, rVt, dg0 = ST[kk]["wv"]
                        for i in range(4):
                            self.mm(py[:], CW[:, dg0 + i, :], Wt[:, i, :], n == 0, False, [rP, rWt], [rpy], sig=False)
                            n += 1
                            self.mm(py[:], CV[:, dg0 + i, :], Vt[:, i, :], False, n == 15, [rP, rVt], [rpy],
                                    sig=(n == 15))
                            n += 1
                    ys, rys = ysb.next()
                    ydst = self.Y_d[half, :, c * 128:(c + 1) * 128]
                    if d == 0:
                        self.cp("act", ys[:], py[:], [rpy], [rys])
                    else:
                        yl, ryl = yld.next()
                        s.dma(yl[:], ydst, reads=[self.rYd], writes=[ryl])
                        self.ttn("dve", ys[:], py[:], yl[:], ALU.add, [rpy, ryl], [rys])
                    s.dma(ydst, ys[:], reads=[rys], writes=[self.rYd], q="pool")
                    del ST[k - 1]
                    del ST[k]
                emit_ab(0)
                for k in range(nst):
                    emit_scan(k)
                    if k >= 1:
                        emit_wv(k - 1)
                        emit_evac(k - 1)
                    if k + 1 < nst:
                        emit_ab(k + 1)
                    if k >= 3 and steps[k - 3][4] == 1:
                        emit_c(k - 3)
                    emit_rot(k)
                emit_wv(nst - 1)
                emit_evac(nst - 1)
                if steps[nst - 3][4] == 1:
                    emit_c(nst - 3)
                if steps[nst - 2][4] == 1:
                    emit_c(nst - 2)
                emit_c(nst - 1)
                s.barrier()
            with ExitStack() as c1:
                Wg = self.sb(c1, [128, 2, 256], BF16, "Wglu")
                rW = Res()
                wgl = W["wglu%d" % l]
                st = self.sbring(c1, 2, [128, 256], F32, "stg")
                self.load_cast(c1, lambda i: Wg[:, i, :], lambda i: wgl[i * 128:(i + 1) * 128, :], 2, None, None,
                               rW, st)
                ssd = cols[:, CP.sl("ssd%d" % l)]
                yt = self.sbring(c1, 2, [128, 2, 512], F32, "ey")
                at = self.sbring(c1, 2, [128, 2, 512], BF16, "ea")
                tt_ = self.sbring(c1, 2, [128, 2, 512], F32, "et")
                gT = self.sbring(c1, 2, [128, 2, 512], BF16, "eg")
                sgm = self.sbring(c1, 2, [128, 512], BF16, "esg")
                bo = self.sbring(c1, 2, [128, 2, 512], BF16, "ebo")
                pZ = self.psring(c1, 2, [128, 512], F32, "pZ")
                K1 = 2.0 * (2.0 / np.pi) ** 0.5
                for st_i in range(T // 512):
                    t0 = st_i * 512
                    y, ry = yt.next()
                    a, ra = at.next()
                    s.dma(y[:], self.Y_d[:, :, t0:t0 + 512].rearrange("c p t -> p c t"), reads=[self.rYd], writes=[ry])
                    s.dma(a[:], self.PT_d[0:2, :, t0:t0 + 512].rearrange("c p t -> p c t"), writes=[ra])
                    for hf in range(2):
                        self.stt(y[:, hf, :], a[:, hf, :], ssd[:, hf:hf + 1], y[:, hf, :], ALU.mult, ALU.add,
                                 [ry, ra], [ry])
                    t, rt = tt_.next()
                    self.ttn("dve", t[:], y[:], y[:], ALU.mult, [ry], [rt])
                    self.tsc("dve", t[:], t[:], 0.044715, 1.0, ALU.mult, ALU.add, [rt], [rt])
                    self.ttn("dve", t[:], t[:], y[:], ALU.mult, [rt, ry], [rt])
                    self.act(t[:], t[:], AF.Sigmoid, [rt], [rt], scale=K1)
                    g, rg = gT.next()
                    self.ttn("dve", g[:], t[:], y[:], ALU.mult, [rt, ry], [rg])
                    b_, rb = bo.next()
                    for oc in range(2):
                        pz, rpz = pZ.next()
                        for kc in range(2):
                            self.mm(pz[:], Wg[:, kc, oc * 128:(oc + 1) * 128], g[:, kc, :], kc == 0, kc == 1,
                                    [rW, rg], [rpz], sig=(kc == 1))
                        sg, rsg = sgm.next()
                        self.act(sg[:], pz[:], AF.Sigmoid, [rpz], [rsg])
                        self.ttn("dve", b_[:, oc, :], g[:, oc, :], sg[:], ALU.mult, [rg, rsg], [rb])
                    s.dma(self.BR_d[0, :, :, t0:t0 + 512].rearrange("c p t -> p c t"), b_[:], reads=[rb], q="pool")
                s.barrier()

    def merge_phase(self, h_in, h_out, l):
        s, T = self.s, self.T
        cols, W = self.cols, self.W
        with ExitStack() as ctx:
            Wg = self.sb(ctx, [128, 8, 5120], BF16, "Wgate")
            Pb = self.sb(ctx, [128, 10, D], BF16, "Pbr")
            Wo = self.sb(ctx, [128, 8, D], BF16, "Wout")
            rW = Res()
            with ExitStack() as c2:
                st = self.sbring(c2, 3, [128, 2560], F32, "stg")
                gm = cols[:, CP.sl("gm%d" % l)]
                win = W["w_in%d" % l]
                self.load_cast(c2, lambda i: Wg[:, i // 2, (i % 2) * 2560:(i % 2) * 2560 + 2560],
                               lambda i: win[(i // 2) * 128:(i // 2) * 128 + 128,
                                             O_GATE + (i % 2) * 2560:O_GATE + (i % 2) * 2560 + 2560],
                               16, None, lambda i: gm[:, i // 2:i // 2 + 1], rW, st)
                wbr = W["wbr%d" % l]
                self.load_cast(c2, lambda i: Pb[:, i, :], lambda i: wbr[i // 2, (i % 2) * 128:(i % 2) * 128 + 128, :],
                               10, None, None, rW, st)
                wo = W["wout%d" % l]
                self.load_cast(c2, lambda i: Wo[:, i, :], lambda i: wo[i * 128:(i + 1) * 128, :], 8, None, None,
                               rW, st)
                s.barrier()
            uT = self.sbring(ctx, 2, [128, 8, 512], BF16, "muT")
            br = self.sbring(ctx, 2, [128, 10, 512], BF16, "mbr")
            mT = self.sb(ctx, [128, 8, 512], BF16, "mT")
            rmT = Res()
            sgr = self.sbring(ctx, 2, [128, 512], F32, "msg")
            tmr = self.sbring(ctx, 2, [128, 512], F32, "mtm")
            acr = self.sbring(ctx, 2, [128, 512], F32, "mac")
            xr = self.sbring(ctx, 2, [128, D], F32, "mxr")
            pg = self.psring(ctx, 2, [128, 512], F32, "pg")
            pp = self.psring(ctx, 2, [128, 512], F32, "pp")
            pd = self.psring(ctx, 2, [128, 512], F32, "pd")
            ins_ = {}

            def load_in(si):
                if si >= T // 512:
                    return
                t0 = si * 512
                u, ru = uT.next()
                s.dma(u[:], self.uT_d[:, :, t0:t0 + 512].rearrange("k p t -> p k t"), writes=[ru])
                b_, rb = br.next()
                for bi in range(5):
                    s.dma(b_[:, bi * 2:bi * 2 + 2, :], self.BR_d[bi, :, :, t0:t0 + 512].rearrange("c p t -> p c t"),
                          writes=[rb])
                ins_[si] = (u, ru, b_, rb)
            load_in(0)
            for st_i in range(T // 512):
                t0 = st_i * 512
                u, ru, b_, rb = ins_.pop(st_i)
                for oc in range(8):
                    if oc == 2:
                        load_in(st_i + 1)
                    ac, rac = acr.next()
                    for bi in range(5):
                        g, rg = pg.next()
                        for kc in range(8):
                            self.mm(g[:], Wg[:, kc, bi * 1024 + oc * 128:bi * 1024 + oc * 128 + 128], u[:, kc, :],
                                    kc == 0, kc == 7, [rW, ru], [rg], sig=(kc == 7))
                        p_, rp = pp.next()
                        for hf in range(2):
                            self.mm(p_[:], Pb[:, bi * 2 + hf, oc * 128:(oc + 1) * 128], b_[:, bi * 2 + hf, :],
                                    hf == 0, hf == 1, [rW, rb], [rp], sig=(hf == 1))
                        sg, rsg = sgr.next()
                        self.act(sg[:], g[:], AF.Sigmoid, [rg], [rsg])
                        if bi == 0:
                            self.ttn("dve", ac[:], sg[:], p_[:], ALU.mult, [rsg, rp], [rac])
                        else:
                            tm, rtm = tmr.next()
                            self.ttn("dve", tm[:], sg[:], p_[:], ALU.mult, [rsg, rp], [rtm])
                            if bi < 4:
                                self.ttn("pool", ac[:], ac[:], tm[:], ALU.add, [rac, rtm], [rac])
                            else:
                                self.ttn("pool", mT[:, oc, :], ac[:], tm[:], ALU.add, [rac, rtm], [rmT])
                for tt in range(4):
                    tk = t0 + tt * 128
                    xo, rxo = xr.next()
                    s.dma(xo[:], h_in[tk:tk + 128, :], writes=[rxo])
                    for half in range(2):
                        d, rd = pd.next()
                        for oc in range(8):
                            self.mm(d[:], mT[:, oc, tt * 128:(tt + 1) * 128], Wo[:, oc, half * 512:(half + 1) * 512],
                                    oc == 0, oc == 7, [rmT, rW], [rd], sig=(oc == 7))
                        self.ttn("dve", xo[:, half * 512:(half + 1) * 512], d[:], xo[:, half * 512:(half + 1) * 512],
                                 ALU.add, [rd, rxo], [rxo])
                    s.dma(h_out[tk:tk + 128, :], xo[:], reads=[rxo], q="pool")
            s.barrier()


def build_program(T, dbg=False, stages=None, na_slots=10):
    b = Builder(T, dbg)
    nc, s, es = b.nc, b.s, b.es
    NT = T // 128
    x = b.din("x", [T, D])
    y = b.dout("y", [T, D])
    b.memd = b.din("mem", [2, N_MEM, D])
    b.tokd = b.din("tok", [T, TOKW])
    colsd = b.din("cols", [128, CP.n])
    fgain = b.din("fgain", [128, D])
    identd = b.din("ident", [128, 128])
    W = {}
    W["p2"] = b.din("p2", [128, 128])
    W["s5const"] = b.din("s5const", [2, 128, 4096])
    nsl = len(b.na_specials()) + 1
    for l in range(DEPTH):
        for f in (1, 2):
            W["wg%d%d" % (l, f)] = b.din("wg%d%d" % (l, f), [D, DFF])
            W["wu%d%d" % (l, f)] = b.din("wu%d%d" % (l, f), [D, DFF])
            W["wd%d%d" % (l, f)] = b.din("wd%d%d" % (l, f), [DFF, D])
        W["w_in%d" % l] = b.din("w_in%d" % l, [D, 7264])
        W["wq%d" % l] = b.din("wq%d" % l, [192, 384])
        W["wkv%d" % l] = b.din("wkv%d" % l, [128, 512])
        W["wmem%d" % l] = b.din("wmem%d" % l, [D, 512])
        W["wglu%d" % l] = b.din("wglu%d" % l, [256, 256])
        W["wbr%d" % l] = b.din("wbr%d" % l, [5, 256, D])
        W["wout%d" % l] = b.din("wout%d" % l, [D, D])
        W["s5row%d" % l] = b.din("s5row%d" % l, [5, 128, 2048])
        W["s5c%d" % l] = b.din("s5c%d" % l, [2, 64, 4096])
        W["swa_tab%d" % l] = b.din("swa_tab%d" % l, [5, 128, 4, 384])
        W["na_tab%d" % l] = b.din("na_tab%d" % l, [nsl, 128, 4, 768])
    b.W = W
    hA = b.dscr("hA", [T, D])
    hB = b.dscr("hB", [T, D])
    b.uT_d = b.dscr("uT_d", [8, 128, T], BF16)
    b.PT_d = b.dscr("PT_d", [NFM, 128, T], BF16)
    b.VT_d = b.dscr("VT_d", [T, 384], BF16)
    b.QM_d = b.dscr("QM_d", [4, 99, T], BF16)
    b.KM_d = b.dscr("KM_d", [4, 99, T], BF16)
    b.VM_d = b.dscr("VM_d", [T, 260], BF16)
    b.BR_d = b.dscr("BR_d", [5, 2, 128, T], BF16)
    b.Y_d = b.dscr("Y_d", [2, 128, T])
    b.NRM_d = b.dscr("NRM_d", [8, 512])
    b.rNRM = [Res() for _ in range(8)]
    b.rYd = Res()
    b.cols = b.sb(es, [128, CP.n], F32, "cols")
    fgain_sb = b.sb(es, [128, D], F32, "fgain")
    b.identf = b.sb(es, [128, 128], F32, "identf")
    b.ident = b.sb(es, [128, 128], BF16, "ident")
    b.kmax2 = b.sb(es, [128, 4], F32, "kmax2")
    b.onesf = b.sb(es, [128, 64], F32, "onesf")
    s.op("dve", lambda e: e.memset(b.onesf[:], 1.0), [], [Res()])
    b.rkm = Res()
    rc = Res()
    s.dma(b.cols[:], colsd[:, :], writes=[rc])
    s.dma(fgain_sb[:], fgain[:, :], writes=[rc])
    s.dma(b.identf[:], identd[:, :], writes=[rc])
    s.op("dve", lambda e: e.tensor_copy(out=b.ident[:], in_=b.identf[:]), reads=[rc], writes=[rc])
    s.barrier()
    cur = x
    bufs = [hA, hB]
    bi = 0
    if stages is None:
        stages = ("ffn1", "proj", "s5", "swa", "na", "mla", "mem", "merge", "ffn2")
    for l in range(DEPTH):
        if "ffn1" in stages:
            dst = bufs[bi]
            bi ^= 1
            b.ffn_phase(cur, dst, W["wg%d1" % l], W["wu%d1" % l], W["wd%d1" % l], b.cols[:, CP.sl("g1%d" % l)])
            cur = dst
        if "proj" in stages:
            b.proj_phase(cur, l)
        if "s5" in stages:
            b.s5_phase(l)
        if "swa" in stages:
            b.swa_phase(l)
        if "na" in stages:
            b.na_phase(l)
        if "mla" in stages:
            b.mla_phase(l)
        if "mem" in stages:
            b.mem_phase(l)
        if "merge" in stages:
            dst = bufs[bi]
            bi ^= 1
            b.merge_phase(cur, dst, l)
            cur = dst
        if "ffn2" in stages:
            dst = bufs[bi]
            bi ^= 1
            b.ffn_phase(cur, dst, W["wg%d2" % l], W["wu%d2" % l], W["wd%d2" % l], b.cols[:, CP.sl("g2%d" % l)])
            cur = dst
        if dbg and l == 0 and stages is not None and "stop1" in stages:
            break
    b.final_norm(cur, y, fgain_sb)
    s.emit()
    return nc


def _t5_bucket_np(rel):
    nb = 16
    max_exact = 8
    ret = (rel > 0).astype(np.int32) * nb
    n = np.abs(rel)
    nf = np.maximum(n, 1).astype(np.float32)
    large = max_exact + (np.log(nf / np.float32(max_exact)) / np.float32(np.log(128 / max_exact))
                         * np.float32(nb - max_exact)).astype(np.int32)
    large = np.minimum(large, nb - 1)
    return ret + np.where(n < max_exact, n, large)


def _na_keytiles(NT, j):
    mid = NT // 2
    base = min(max(j - 2, 0), NT - 6)
    if j == mid - 1:
        base = min(max(mid - 4, 0), NT - 6)
    return list(range(base, base + 6))


def _na_specials(NT):
    mid = NT // 2
    return sorted(set(x for x in (0, 1, mid - 2, mid - 1, mid, mid + 1, NT - 3, NT - 2, NT - 1) if 0 <= x < NT))


def host_tables(T, nseg):
    S = T // nseg
    pos = (np.arange(T) % S).astype(np.float32)
    inv = (np.float32(10000.0) ** (-np.arange(16, dtype=np.float32) / np.float32(16))).astype(np.float32)
    ang = (pos[:, None] * inv[None, :]).astype(np.float32)
    tok = np.zeros((T, TOKW), np.float32)
    tok[:, 0:16] = np.cos(ang)
    tok[:, 16:32] = np.sin(ang)
    seg = np.arange(T) // S
    if nseg == 2:
        tok[:, 32] = np.where(seg == 0, 0.0, NEGV)
        tok[:, 33] = np.where(seg == 1, 0.0, NEGV)
    tok[:, 34] = (np.arange(T) < T // 2).astype(np.float32)
    tok[:, 35] = (np.arange(T) >= T // 2).astype(np.float32)
    return tok


def swa_tables(T, nseg, t5_bias):
    NT = T // 128
    S = T // nseg
    mid = NT // 2
    specials = [0, mid - 1, mid, NT - 1]
    rep = [j for j in range(NT) if j not in specials]
    js = [rep[0] if rep else 0] + specials
    out = np.full((5, 128, 4, 384), NEGV, np.float32)
    q = np.arange(128)
    for si, j in enumerate(js):
        qpos = j * 128 + q
        for sidx in range(3):
            it = j - 1 + sidx
            if it < 0 or it >= NT:
                continue
            kpos = it * 128 + np.arange(128)
            rel = kpos[None, :] - qpos[:, None]
            valid = (np.abs(rel) <= SWA_WIN) & ((kpos[None, :] // S) == (qpos[:, None] // S))
            bias = t5_bias[_t5_bucket_np(rel)]
            blk = np.where(valid[:, :, None], bias, NEGV).astype(np.float32)
            out[si, :, :, sidx * 128:(sidx + 1) * 128] = blk.transpose(0, 2, 1)
    return out


def na_tables(T, nseg, rpb):
    NT = T // 128
    S = T // nseg
    rows = S // GRID_W
    kh = min(NA_KH, rows)
    sp = _na_specials(NT)
    rep = [j for j in range(NT) if j not in sp]
    js = [rep[0] if rep else None] + sp
    out = np.full((len(js), 128, 4, 768), NEGV, np.float32)

    def tab(j):
        t = np.full((128, 4, 768), NEGV, np.float32)
        qtok = j * 128 + np.arange(128)
        qseg, qr, qc = qtok // S, (qtok % S) // GRID_W, qtok % GRID_W
        rs = np.clip(qr - kh // 2, 0, rows - kh)
        cs = np.clip(qc - NA_KW // 2, 0, GRID_W - NA_KW)
        cnt = np.zeros(128, np.int64)
        for i, kt in enumerate(_na_keytiles(NT, j)):
            ktok = kt * 128 + np.arange(128)
            kseg, kr, kc = ktok // S, (ktok % S) // GRID_W, ktok % GRID_W
            valid = ((kseg[None, :] == qseg[:, None]) & (kr[None, :] >= rs[:, None]) & (kr[None, :] < rs[:, None] + kh)
                     & (kc[None, :] >= cs[:, None]) & (kc[None, :] < cs[:, None] + NA_KW))
            dr = np.clip(kr[None, :] - qr[:, None] + (NA_KH - 1), 0, 2 * NA_KH - 2)
            dc = np.clip(kc[None, :] - qc[:, None] + (NA_KW - 1), 0, 2 * NA_KW - 2)
            bias = rpb[:, dr, dc]
            t[:, :, i * 128:(i + 1) * 128] = np.where(valid[None], bias, NEGV).transpose(1, 0, 2)
            cnt += valid.sum(1)
        assert (cnt == kh * NA_KW).all(), ("NA coverage", j, cnt.min(), cnt.max())
        return t
    for si, j in enumerate(js):
        if j is not None:
            out[si] = tab(j)
    if rep:
        for j in rep[1:]:
            if j in (rep[len(rep) // 2], rep[-1]):
                assert np.array_equal(tab(j), out[0]), ("NA interior mismatch", j)
    return out


def s5_layouts(w, l):
    lam_re, lam_im, log_step = w["ssm_lam_re"][l], w["ssm_lam_im"][l], w["ssm_log_step"][l]
    b_re, b_im, c_re, c_im = w["ssm_b_re"][l], w["ssm_b_im"][l], w["ssm_c_re"][l], w["ssm_c_im"][l]
    row = np.zeros((5, 128, 2048), np.float32)
    row[0] = np.broadcast_to(lam_re.reshape(1, 2048), (128, 2048))
    row[1] = np.broadcast_to(lam_im.reshape(1, 2048), (128, 2048))
    row[2] = np.broadcast_to(np.repeat(log_step.reshape(32), 64).reshape(1, 2048), (128, 2048))
    bp = np.zeros((2, 128, 2, 16, 64), np.float32)
    cc = np.zeros((2, 64, 2, 16, 128), np.float32)
    for g in range(16):
        k0 = (g % 8) * 16
        bp[0, k0:k0 + 16, :, g, :] = b_re[:, g].transpose(2, 0, 1)
        bp[1, k0:k0 + 16, :, g, :] = b_im[:, g].transpose(2, 0, 1)
        cc[0, :, :, g, k0:k0 + 16] = c_re[:, g].transpose(2, 0, 1)
        cc[1, :, :, g, k0:k0 + 16] = c_im[:, g].transpose(2, 0, 1)
    row[3] = bp[0].reshape(128, 2048)
    row[4] = bp[1].reshape(128, 2048)
    s5c = cc.reshape(2, 64, 4096)
    n_of_p = np.arange(128) % 64
    lre_c = lam_re.reshape(32, 64)[:, n_of_p].T.copy()
    lim_c = lam_im.reshape(32, 64)[:, n_of_p].T.copy()
    lst_c = np.broadcast_to(log_step.reshape(1, 32), (128, 32)).copy()
    return row, s5c, lre_c, lim_c, lst_c


def s5_consts():
    c = np.zeros((2, 128, 32, 128), np.float32)
    j = np.arange(128, dtype=np.float32)
    c[0, :, 0:16, :] = j
    c[0, :, 16:32, :] = 127.0 - j
    c[1] = 1.0
    c[1, :, 0:16, 0] = 0.0
    c[1, :, 16:32, 127] = 0.0
    p2 = np.zeros((128, 128), np.float32)
    p2[np.arange(128), (np.arange(128) + 64) % 128] = 1.0
    return c.reshape(2, 128, 4096), p2


def colgain(g):
    return np.ascontiguousarray(g.reshape(-1, 128).T)


def make_shared_inputs(w):
    sh = {}
    for l in range(DEPTH):
        ffw = {1: (w["ffn1_w_gate"], w["ffn1_w_up"], w["ffn1_w_down"]),
               2: (w["ffn2_w_gate"], w["ffn2_w_up"], w["ffn2_w_down"])}
        for f in (1, 2):
            sh["wg%d%d" % (l, f)] = np.ascontiguousarray(ffw[f][0][l])
            sh["wu%d%d" % (l, f)] = np.ascontiguousarray(ffw[f][1][l])
            sh["wd%d%d" % (l, f)] = np.ascontiguousarray(ffw[f][2][l])
        sh["w_in%d" % l] = np.ascontiguousarray(w["w_in"][l])
        sh["wq%d" % l] = np.ascontiguousarray(w["mla_w_q_up"][l])
        sh["wkv%d" % l] = np.ascontiguousarray(w["mla_w_kv_up"][l])
        sh["wmem%d" % l] = np.ascontiguousarray(w["mem_w_kv"][l])
        sh["wglu%d" % l] = np.ascontiguousarray(w["ssm_w_glu"][l])
        sh["wbr%d" % l] = np.ascontiguousarray(w["w_branch"][l])
        sh["wout%d" % l] = np.ascontiguousarray(w["w_out"][l])
    c, p2 = s5_consts()
    sh["s5const"] = c
    sh["p2"] = p2
    sh["ident"] = np.eye(128, dtype=np.float32)
    sh["fgain"] = np.ascontiguousarray(np.broadcast_to(w["final_norm"].reshape(1, D), (128, D)))
    return sh


def make_cols(w, nseg, s5cols):
    cols = np.zeros((128, CP.n), np.float32)
    for l in range(DEPTH):
        cols[:, CP.sl("g1%d" % l)] = colgain(w["ffn1_norm"][l])
        cols[:, CP.sl("g2%d" % l)] = colgain(w["ffn2_norm"][l])
        cols[:, CP.sl("gm%d" % l)] = colgain(w["mix_norm"][l])
        cols[:, CP.sl("gmem%d" % l)] = colgain(w["mem_norm"][l])
        qn = np.zeros((128, 2), np.float32)
        qn[:, 0] = w["mla_q_norm"][l][0:128]
        qn[0:64, 1] = w["mla_q_norm"][l][128:192]
        cols[:, CP.sl("qn%d" % l)] = qn
        cols[:, CP.sl("kvn%d" % l)] = w["mla_kv_norm"][l].reshape(128, 1)
        cols[:, CP.sl("ssd%d" % l)] = w["ssm_d"][l].reshape(2, 128).T
        cols[:, CP.sl("sink%d" % l)] = np.broadcast_to(w["swa_sink"][l].reshape(1, 4), (128, 4))
        lre_c, lim_c, lst_c = s5cols[l]
        cols[:, CP.sl("lre%d" % l)] = lre_c
        cols[:, CP.sl("lim%d" % l)] = lim_c
        cols[:, CP.sl("lst%d" % l)] = lst_c
    cols[:, CP.sl("flag")] = 1.0 if nseg == 1 else 0.0
    cols[0:64, CP.sl("sgn")] = 1.0
    cols[64:128, CP.sl("sgn")] = -1.0
    return cols


def make_type_inputs(T, nseg, w, sh):
    d = dict(sh)
    d["tok"] = host_tables(T, nseg)
    s5cols = []
    for l in range(DEPTH):
        row, s5c, lre_c, lim_c, lst_c = s5_layouts(w, l)
        d["s5row%d" % l] = row
        d["s5c%d" % l] = s5c
        s5cols.append((lre_c, lim_c, lst_c))
        d["swa_tab%d" % l] = swa_tables(T, nseg, w["t5_bias"])
        d["na_tab%d" % l] = na_tables(T, nseg, w["na_rpb"][l])
    d["cols"] = make_cols(w, nseg, s5cols)
    return d


_PROG_CACHE = {}


def kernel(**inputs):
    w = {k: np.asarray(v, dtype=np.float32) for k, v in inputs.items()}
    xp, xs = w["x_prompt"], w["x_sample"]
    mp, ms = w["mem_prompt"], w["mem_sample"]
    B, S, _ = xp.shape
    T = S
    sh = make_shared_inputs(w)
    tp = make_type_inputs(T, 1, w, sh)
    ts = make_type_inputs(T, 2, w, sh)
    ACTIVE = [0, 1, 4, 5]
    big = [k for k in tp if k.startswith(("wg", "wu", "wd", "w_in", "wq", "wkv", "wmem", "wglu", "wbr", "wout"))]
    idle = dict(tp)
    for k in big:
        idle[k] = np.zeros_like(tp[k])
    idle["x"] = np.zeros((T, D), np.float32)
    idle["mem"] = np.zeros((2, N_MEM, D), np.float32)
    in_maps = []
    for c in range(NCORES):
        if c not in ACTIVE:
            in_maps.append(idle)
            continue
        cc = ACTIVE.index(c)
        if cc < 2:
            m = dict(tp)
            m["x"] = np.ascontiguousarray(xp[cc])
            m["mem"] = np.ascontiguousarray(np.stack([mp[cc], mp[cc]]))
        else:
            i0 = (cc - 2) * 2
            m = dict(ts)
            m["x"] = np.ascontiguousarray(xs[i0:i0 + 2].reshape(T, D))
            m["mem"] = np.ascontiguousarray(ms[i0:i0 + 2])
        in_maps.append(m)
    if T not in _PROG_CACHE:
        _PROG_CACHE[T] = build_program(T)
    nc = _PROG_CACHE[T]
    res = run_bass_kernel_spmd(nc, in_maps, core_ids=list(range(NCORES)))
    outs = [np.asarray(r["y"], dtype=np.float32) for r in res.results]
    y_prompt = np.stack([outs[ACTIVE[0]], outs[ACTIVE[1]]]).reshape(xp.shape)
    y_sample = np.concatenate([outs[ACTIVE[2]].reshape(2, -1, D), outs[ACTIVE[3]].reshape(2, -1, D)],
                              0).reshape(xs.shape)
    return (y_prompt, y_sample)
```
